# Optimizing a Trainium2 kernel written in Bass

```python
import jax, jax.numpy as jnp
from jax import lax
import numpy as np

D_MODEL = 1024
BATCH = 8
SEQ = 2048
DEPTH = 4

GRID_W = 64
CTX_LEN = 256
HEAD_DIM = 64
NA_HEADS = 4
NA_WIN_ROWS = 8
NA_WIN_COLS = 16
NA_SCALE = HEAD_DIM ** -0.5
POOL_GROUPS = 4
POOL_CH = 64
POOL_WINDOWS = (2, 4, 8, 16)
FFT_GROUPS = 4
FFT_CH = 64
MLA_HEADS = 4
MLA_Q_RANK = 256
MLA_KV_RANK = 128
MLA_NOPE = 64
MLA_ROPE = 32
MLA_V = 64
MLA_SCALE = (MLA_NOPE + MLA_ROPE) ** -0.5
ROPE_BASE = 10000.0
NA_W = NA_HEADS * HEAD_DIM
POOL_W = POOL_GROUPS * POOL_CH
FFT_W = FFT_GROUPS * FFT_CH
MLA_W = MLA_HEADS * MLA_V
BRANCH_W = NA_W
N_BRANCH = 4
IN_SPLITS = tuple(int(s) for s in np.cumsum([NA_W, NA_W, NA_W, POOL_W, FFT_W, MLA_Q_RANK, MLA_KV_RANK, MLA_ROPE]))
IN_COLS = IN_SPLITS[-1] + N_BRANCH * D_MODEL
N_EXPERTS = 16
EXPERT_FF = 2048
EC_CAPACITY = 2
QUERY_BLOCK = 128
DN_ALPHA = (2 * DEPTH) ** 0.25
DN_BETA = (8 * DEPTH) ** -0.25
LN_EPS = 1e-5
NEG = -1e30

kernel_name = "hybrid_na_pool_fourier_mla_ecmoe_dit"


def layer_norm(x, g, b):
    xf = x.astype(jnp.float32)
    mu = xf.mean(-1, keepdims=True)
    var = jnp.square(xf - mu).mean(-1, keepdims=True)
    y = (xf - mu) * lax.rsqrt(var + LN_EPS)
    return (y * g.astype(jnp.float32) + b.astype(jnp.float32)).astype(x.dtype)


def rms_norm(x, g):
    xf = x.astype(jnp.float32)
    y = xf * lax.rsqrt(jnp.mean(xf * xf, -1, keepdims=True) + LN_EPS)
    return (y * g.astype(jnp.float32)).astype(x.dtype)


def axial_rope(L):
    n_freq = MLA_ROPE // 4
    inv = ROPE_BASE ** (-jnp.arange(n_freq, dtype=jnp.float32) / n_freq)
    t = jnp.arange(L)
    row = (t // GRID_W).astype(jnp.float32)
    col = (t % GRID_W).astype(jnp.float32)
    ang = jnp.concatenate([row[:, None] * inv, col[:, None] * inv], axis=-1)
    return jnp.cos(ang), jnp.sin(ang)


def apply_rope(x, cos, sin):
    xf = x.astype(jnp.float32)
    x1, x2 = xf[..., 0::2], xf[..., 1::2]
    c, s = cos[None, :, None, :], sin[None, :, None, :]
    out = jnp.stack([x1 * c - x2 * s, x1 * s + x2 * c], axis=-1).reshape(x.shape)
    return out.astype(x.dtype)


def attend(q, k, v, scale):
    s = jnp.einsum('bqhd,bkhd->bhqk', q, k).astype(jnp.float32) * scale
    p = jax.nn.softmax(s, axis=-1).astype(v.dtype)
    return jnp.einsum('bhqk,bkhd->bqhd', p, v)


def attend_blocked(q, k, v, scale):
    B, L, H, dk = q.shape
    nb = L // QUERY_BLOCK
    qb = q.reshape(B, nb, QUERY_BLOCK, H, dk).transpose(1, 0, 2, 3, 4)
    out = lax.map(lambda qi: attend(qi, k, v, scale), qb)
    return out.transpose(1, 0, 2, 3, 4).reshape(B, L, H, v.shape[-1])


def neighborhood_attention(q, k, v, kc, vc, rpb):
    B, L, H, d = q.shape
    R = L // GRID_W
    kr = min(NA_WIN_ROWS, R)
    rows = jnp.arange(R)
    r0 = jnp.clip(rows - kr // 2, 0, R - kr)
    row_idx = r0[:, None] + jnp.arange(kr)[None, :]
    cols = jnp.arange(GRID_W)
    c0 = jnp.clip(cols - NA_WIN_COLS // 2, 0, GRID_W - NA_WIN_COLS)
    col_in = (cols[None, :] >= c0[:, None]) & (cols[None, :] < c0[:, None] + NA_WIN_COLS)
    row_off = row_idx - rows[:, None] + (NA_WIN_ROWS - 1)
    col_off = jnp.clip(cols[None, :] - cols[:, None], -(NA_WIN_COLS - 1), NA_WIN_COLS - 1) + (NA_WIN_COLS - 1)
    bias = rpb[:, row_off[:, None, :, None], col_off[None, :, None, :]].astype(jnp.float32)
    bias = jnp.where(col_in[None, None, :, None, :], bias, NEG)
    qg = q.reshape(B, R, GRID_W, H, d)
    kg = k.reshape(B, R, GRID_W, H, d)[:, row_idx]
    vg = v.reshape(B, R, GRID_W, H, d)[:, row_idx]
    s_win = jnp.einsum('brqhd,brkchd->bhrqkc', qg, kg).astype(jnp.float32) * NA_SCALE + bias[None]
    s_ctx = jnp.einsum('brqhd,bkhd->bhrqk', qg, kc).astype(jnp.float32) * NA_SCALE
    nwin = kr * GRID_W
    s = jnp.concatenate([s_win.reshape(B, H, R, GRID_W, nwin), s_ctx], axis=-1)
    p = jax.nn.softmax(s, axis=-1).astype(v.dtype)
    p_win = p[..., :nwin].reshape(B, H, R, GRID_W, kr, GRID_W)
    o = jnp.einsum('bhrqkc,brkchd->brqhd', p_win, vg) + jnp.einsum('bhrqk,bkhd->brqhd', p[..., nwin:], vc)
    return o.reshape(B, L, H, d)


def multiscale_pool(u, w_pool, pool_scale):
    B, L, _ = u.shape
    ug = u.reshape(B, L, POOL_GROUPS, POOL_CH).astype(jnp.float32)
    cs = jnp.concatenate([jnp.zeros_like(ug[:, :1]), jnp.cumsum(ug, axis=1)], axis=1)
    t = jnp.arange(L)[:, None]
    half = jnp.array(POOL_WINDOWS, dtype=jnp.int32)[None, :] // 2
    lo = jnp.clip(t - half, 0, L)
    hi = jnp.clip(t + half, 0, L)
    gi = jnp.arange(POOL_GROUPS)[None, :]
    mean = (cs[:, hi, gi] - cs[:, lo, gi]) / (hi - lo).astype(jnp.float32)[None, :, :, None]
    y = (mean - ug).astype(u.dtype)
    y = jnp.einsum('blgc,gcd->blgd', y, w_pool).reshape(B, L, POOL_W)
    return y * pool_scale


def fourier_mix(u):
    B, L, _ = u.shape
    ug = u.reshape(B, L, FFT_GROUPS, FFT_CH).astype(jnp.float32)
    y = jnp.fft.fft2(ug, axes=(1, 3), norm='ortho').real
    return y.reshape(B, L, FFT_W).astype(u.dtype)


def mla_qkv(cq, ckv, krope, q_norm, w_uq, kv_norm, w_ukv, rope):
    B, L, _ = cq.shape
    q = (rms_norm(cq, q_norm) @ w_uq).reshape(B, L, MLA_HEADS, MLA_NOPE + MLA_ROPE)
    kv = (rms_norm(ckv, kv_norm) @ w_ukv).reshape(B, L, MLA_HEADS, MLA_NOPE + MLA_V)
    q_nope, q_rope = q[..., :MLA_NOPE], q[..., MLA_NOPE:]
    k_nope, v = kv[..., :MLA_NOPE], kv[..., MLA_NOPE:]
    k_rope = krope[:, :, None, :]
    if rope is not None:
        q_rope = apply_rope(q_rope, *rope)
        k_rope = apply_rope(k_rope, *rope)
    q = jnp.concatenate([q_nope, q_rope], axis=-1)
    k = jnp.concatenate([k_nope, jnp.broadcast_to(k_rope, (B, L, MLA_HEADS, MLA_ROPE))], axis=-1)
    return q, k, v


def merge_branches(branches, gates, w_branch, w_out):
    B, L, _ = gates.shape
    y = jnp.stack(branches, axis=2)
    proj = jnp.einsum('blnw,nwd->blnd', y, w_branch)
    g = jax.nn.sigmoid(gates.astype(jnp.float32)).astype(proj.dtype).reshape(B, L, N_BRANCH, D_MODEL)
    return jnp.sum(g * proj, axis=2) @ w_out


def _heads(t):
    return t.reshape(t.shape[0], t.shape[1], NA_HEADS, HEAD_DIM)


def token_mixer(ux, uc, w_in, na_rpb, pool_w, pool_scale, q_norm, w_uq, kv_norm, w_ukv, w_branch, w_out, rope, with_ctx):
    B, L, _ = ux.shape
    Lc = uc.shape[1]
    qa_x, ka_x, va_x, up_x, uf_x, cq_x, ckv_x, kr_x, gt_x = jnp.split(ux @ w_in, IN_SPLITS, axis=-1)
    qa_c, ka_c, va_c, up_c, uf_c, cq_c, ckv_c, kr_c, gt_c = jnp.split(uc @ w_in, IN_SPLITS, axis=-1)
    ka_c, va_c = _heads(ka_c), _heads(va_c)
    qm_c, km_c, vm_c = mla_qkv(cq_c, ckv_c, kr_c, q_norm, w_uq, kv_norm, w_ukv, None)
    ya_x = neighborhood_attention(_heads(qa_x), _heads(ka_x), _heads(va_x), ka_c, va_c, na_rpb).reshape(B, L, NA_W)
    yb_x = multiscale_pool(up_x, pool_w, pool_scale)
    yc_x = fourier_mix(uf_x)
    qm_x, km_x, vm_x = mla_qkv(cq_x, ckv_x, kr_x, q_norm, w_uq, kv_norm, w_ukv, rope)
    yd_x = attend_blocked(qm_x, jnp.concatenate([km_x, km_c], axis=1),
                          jnp.concatenate([vm_x, vm_c], axis=1), MLA_SCALE).reshape(B, L, MLA_W)
    out_x = merge_branches((ya_x, yb_x, yc_x, yd_x), gt_x, w_branch, w_out)
    if not with_ctx:
        return out_x, None
    ya_c = attend(_heads(qa_c), ka_c, va_c, NA_SCALE).reshape(B, Lc, NA_W)
    yb_c = multiscale_pool(up_c, pool_w, pool_scale)
    yc_c = fourier_mix(uf_c)
    yd_c = attend(qm_c, km_c, vm_c, MLA_SCALE).reshape(B, Lc, MLA_W)
    out_c = merge_branches((ya_c, yb_c, yc_c, yd_c), gt_c, w_branch, w_out)
    return out_x, out_c


def expert_choice_ffn(h, w_router, w_gate, w_up, w_down):
    B, L, _ = h.shape
    cap = EC_CAPACITY * L // N_EXPERTS
    aff = jax.nn.softmax((h @ w_router).astype(jnp.float32), axis=-1)
    g, idx = lax.top_k(aff.transpose(0, 2, 1), cap)
    bidx = jnp.arange(B)[:, None, None]
    xe = h[bidx, idx]
    a = jnp.einsum('becd,edf->becf', xe, w_gate)
    u = jnp.einsum('becd,edf->becf', xe, w_up)
    ye = jnp.einsum('becf,efd->becd', jax.nn.silu(a) * u, w_down) * g[..., None].astype(h.dtype)
    return jnp.zeros_like(h).at[bidx, idx].add(ye)


def setup_inputs(seed: int = 0) -> dict:
    key = jax.random.key(seed)
    ks = jax.random.split(key, 26)
    f32 = jnp.float32

    def nrm(k, shape, s):
        return jax.random.normal(k, shape, f32) * s

    D = D_MODEL
    return {
        'x': nrm(ks[0], (BATCH, SEQ, D), 1.0),
        'c': nrm(ks[1], (BATCH, D), 1.0),
        'ctx': nrm(ks[2], (BATCH, CTX_LEN, D), 1.0),
        'c_ctx': nrm(ks[3], (D,), 1.0),
        'w_mod': nrm(ks[4], (DEPTH, D, 6 * D), 0.3 * D ** -0.5),
        'b_mod': nrm(ks[5], (DEPTH, 6 * D), 0.02),
        'w_in': nrm(ks[6], (DEPTH, D, IN_COLS), D ** -0.5),
        'na_rpb': nrm(ks[7], (DEPTH, NA_HEADS, 2 * NA_WIN_ROWS - 1, 2 * NA_WIN_COLS - 1), 0.1),
        'pool_w': nrm(ks[8], (DEPTH, POOL_GROUPS, POOL_CH, POOL_CH), POOL_CH ** -0.5),
        'pool_scale': 1.0 + nrm(ks[9], (DEPTH, POOL_W), 0.1),
        'mla_q_norm': 1.0 + nrm(ks[10], (DEPTH, MLA_Q_RANK), 0.01),
        'mla_w_uq': nrm(ks[11], (DEPTH, MLA_Q_RANK, MLA_HEADS * (MLA_NOPE + MLA_ROPE)), MLA_Q_RANK ** -0.5),
        'mla_kv_norm': 1.0 + nrm(ks[12], (DEPTH, MLA_KV_RANK), 0.01),
        'mla_w_ukv': nrm(ks[13], (DEPTH, MLA_KV_RANK, MLA_HEADS * (MLA_NOPE + MLA_V)), MLA_KV_RANK ** -0.5),
        'w_branch': nrm(ks[14], (DEPTH, N_BRANCH, BRANCH_W, D), BRANCH_W ** -0.5),
        'w_out': nrm(ks[15], (DEPTH, D, D), DN_BETA * D ** -0.5),
        'ln1_g': 1.0 + nrm(ks[16], (DEPTH, D), 0.01),
        'ln1_b': nrm(ks[17], (DEPTH, D), 0.01),
        'w_router': nrm(ks[18], (DEPTH, D, N_EXPERTS), D ** -0.5),
        'w_gate': nrm(ks[19], (DEPTH, N_EXPERTS, D, EXPERT_FF), D ** -0.5),
        'w_up': nrm(ks[20], (DEPTH, N_EXPERTS, D, EXPERT_FF), D ** -0.5),
        'w_down': nrm(ks[21], (DEPTH, N_EXPERTS, EXPERT_FF, D), DN_BETA * EXPERT_FF ** -0.5),
        'ln2_g': 1.0 + nrm(ks[22], (DEPTH, D), 0.01),
        'ln2_b': nrm(ks[23], (DEPTH, D), 0.01),
    }


def reference(x, c, ctx, c_ctx, w_mod, b_mod, w_in, na_rpb, pool_w, pool_scale, mla_q_norm, mla_w_uq,
              mla_kv_norm, mla_w_ukv, w_branch, w_out, ln1_g, ln1_b, w_router, w_gate, w_up, w_down,
              ln2_g, ln2_b):
    rope = axial_rope(x.shape[1])
    h, hc = x, ctx
    silu_c = jax.nn.silu(c)
    silu_cc = jax.nn.silu(c_ctx)
    for l in range(DEPTH):
        with_ctx = l < DEPTH - 1
        mod_x = (silu_c @ w_mod[l] + b_mod[l])[:, None, :]
        mod_c = silu_cc @ w_mod[l] + b_mod[l]
        sh1, sc1, g1, sh2, sc2, g2 = jnp.split(mod_x, 6, axis=-1)
        sh1c, sc1c, g1c, sh2c, sc2c, g2c = jnp.split(mod_c, 6, axis=-1)
        mx, mc = token_mixer(h * (1.0 + sc1) + sh1, hc * (1.0 + sc1c) + sh1c, w_in[l], na_rpb[l],
                             pool_w[l], pool_scale[l], mla_q_norm[l], mla_w_uq[l], mla_kv_norm[l],
                             mla_w_ukv[l], w_branch[l], w_out[l], rope, with_ctx)
        h = layer_norm(DN_ALPHA * h + g1 * mx, ln1_g[l], ln1_b[l])
        fx = expert_choice_ffn(h * (1.0 + sc2) + sh2, w_router[l], w_gate[l], w_up[l], w_down[l])
        h = layer_norm(DN_ALPHA * h + g2 * fx, ln2_g[l], ln2_b[l])
        if with_ctx:
            hc = layer_norm(DN_ALPHA * hc + g1c * mc, ln1_g[l], ln1_b[l])
            fc = expert_choice_ffn(hc * (1.0 + sc2c) + sh2c, w_router[l], w_gate[l], w_up[l], w_down[l])
            hc = layer_norm(DN_ALPHA * hc + g2c * fc, ln2_g[l], ln2_b[l])
    return h
```

```python
import contextlib
import numpy as np
import concourse.bass as bass
import concourse.mybir as mybir
from concourse.bass_utils import run_bass_kernel_spmd

F32 = mybir.dt.float32
F32R = mybir.dt.float32r
BF16 = mybir.dt.bfloat16
I32 = mybir.dt.int32
U32 = mybir.dt.uint32
AF = mybir.ActivationFunctionType
ALU = mybir.AluOpType
AX = mybir.AxisListType

_DSZ = {F32: 4, F32R: 4, BF16: 2, I32: 4, U32: 4, mybir.dt.float16: 2,
        mybir.dt.uint16: 2, mybir.dt.int16: 2, mybir.dt.uint8: 1, mybir.dt.int8: 1}


def _region(ap):
    esz = _DSZ[ap.dtype]
    steps = ap.ap
    off = int(ap.offset)
    sp = str(ap.space)
    if 'SB' in sp or 'PSUM' in sp:
        pstep = steps[0][0]
        if pstep == 0:
            pstep = 1 << 40
        plo = off // pstep
        phi = plo + steps[0][1]
        flo = off % pstep
        ext = 0
        for st, cnt in steps[1:]:
            ext += abs(st) * (cnt - 1)
        return (ap.name, plo, phi, flo * esz, (flo + ext + 1) * esz)
    ext = 0
    for st, cnt in steps:
        ext += abs(st) * (cnt - 1)
    return (ap.name, 0, 1, off * esz, (off + ext + 1) * esz)


def _ovl(a, b):
    return a[1] < b[2] and b[1] < a[2] and a[3] < b[4] and b[3] < a[4]


def _cov(a, b):
    return a[1] <= b[1] and a[2] >= b[2] and a[3] <= b[3] and a[4] >= b[4]


class Prog:
    CENG = ('pe', 'act', 'dve', 'pool')
    ENGS = ('pe', 'act', 'dve', 'pool', 'sp')

    def __init__(self, nc, ring_sizes=None):
        self.nc = nc
        self.es = contextlib.ExitStack()
        self.ops = {e: [] for e in self.ENGS}
        self.writers = {}
        self.readers = {}
        self.known = {e: {} for e in self.ENGS}
        self.ring = ring_sizes or {'sp': 24, 'pool': 12, 'act': 8}
        self.ndma = {k: 0 for k in self.ring}
        self.untracked = set()
        self.nops = 0
        self.order = []
        self.use_block = True

    def sbuf(self, name, shape, dtype):
        return self.es.enter_context(self.nc.sbuf_tensor(name, list(shape), dtype))

    def psum(self, name, shape, dtype):
        return self.es.enter_context(self.nc.psum_tensor(name, list(shape), dtype))

    def dram_in(self, name, shape, dtype):
        t = self.nc.dram_tensor(name, list(shape), dtype, kind="ExternalInput")
        self.untracked.add(name)
        return t.ap()

    def dram_out(self, name, shape, dtype):
        return self.nc.dram_tensor(name, list(shape), dtype, kind="ExternalOutput").ap()

    def dram_tmp(self, name, shape, dtype):
        return self.nc.dram_tensor(name, list(shape), dtype, kind="Internal").ap()

    def add(self, eng, fn, reads, writes, dma=False):
        self.nops += 1
        deps = {}

        def need(ev):
            k, v = ev
            if deps.get(k, -1) < v:
                deps[k] = v

        rr = [_region(a) for a in reads if a is not None and a.name not in self.untracked]
        ww = [_region(a) for a in writes]
        for r in rr:
            for (w, ev) in self.writers.get(r[0], ()):
                if _ovl(r, w):
                    need(ev)
        for r in ww:
            for (w, ev) in self.writers.get(r[0], ()):
                if _ovl(r, w):
                    need(ev)
            for (w, ev) in self.readers.get(r[0], ()):
                if _ovl(r, w):
                    need(ev)
        idx = len(self.ops[eng])
        if dma:
            ns = self.ring[eng]
            k = self.ndma[eng]
            self.ndma[eng] += 1
            slot = k % ns
            val = 16 * (k // ns + 1)
            if val > 16:
                need((('d', eng, slot), val - 16))
            event = (('d', eng, slot), val)
        else:
            event = (('c', eng), idx)
        waits = []
        kn = self.known[eng]
        for k, v in deps.items():
            if k == ('c', 'pe') and eng == 'pe':
                continue
            if kn.get(k, -1) >= v:
                continue
            kn[k] = v
            waits.append((k, v))
            if k[0] == 'c':
                dop = self.ops[k[1]][v]
                dop['marked'] = True
                for k2, v2 in dop['kn'].items():
                    if kn.get(k2, -1) < v2:
                        kn[k2] = v2
        op = dict(fn=fn, waits=waits, event=event, marked=False, dma=dma, eng=eng,
                  kn={k: v for k, v in kn.items() if k[0] == 'c'})
        self.ops[eng].append(op)
        self.order.append(op)
        for r in ww:
            lst = self.writers.setdefault(r[0], [])
            lst[:] = [(w, ev) for (w, ev) in lst if not _cov(r, w)]
            lst.append((r, event))
            rl = self.readers.get(r[0])
            if rl:
                rl[:] = [(w, ev) for (w, ev) in rl if not _cov(r, w)]
        for r in rr:
            lst = self.readers.setdefault(r[0], [])
            lst[:] = [(w, ev) for (w, ev) in lst if not (ev[0] == event[0] and _cov(r, w))]
            lst.append((r, event))
        return op

    def mm(self, out, lhsT, rhs, start=True, stop=True, **kw):
        self.add('pe', lambda e: e.matmul(out, lhsT, rhs, start=start, stop=stop, **kw),
                 [lhsT, rhs], [out])

    def tr(self, out, in_, ident):
        self.add('pe', lambda e: e.transpose(out, in_, ident), [in_, ident], [out])

    def act(self, out, in_, func, bias=None, scale=None, accum_out=None, eng='act'):
        kw = {}
        rd = [in_]
        wr = [out]
        if bias is not None:
            kw['bias'] = bias
            if not isinstance(bias, (int, float)):
                rd.append(bias)
        if scale is not None:
            kw['scale'] = scale
            if not isinstance(scale, (int, float)):
                rd.append(scale)
        if accum_out is not None:
            kw['accum_out'] = accum_out
            wr.append(accum_out)
        self.add('act', lambda e: e.activation(out, in_, func, **kw), rd, wr)

    def tt(self, eng, out, in0, in1, op):
        self.add(eng, lambda e: e.tensor_tensor(out, in0, in1, op), [in0, in1], [out])

    def ts(self, eng, out, in0, s1, s2, op0, op1=None, accum_out=None):
        rd = [in0]
        if not isinstance(s1, (int, float)):
            rd.append(s1)
        if s2 is not None and not isinstance(s2, (int, float)):
            rd.append(s2)
        wr = [out]
        kw = {}
        if op1 is not None:
            kw['op1'] = op1
        if accum_out is not None:
            kw['accum_out'] = accum_out
            wr.append(accum_out)
        self.add(eng, lambda e: e.tensor_scalar(out, in0, s1, s2, op0, **kw), rd, wr)

    def stt(self, eng, out, in0, scalar, in1, op0, op1):
        rd = [in0, in1]
        if not isinstance(scalar, (int, float)):
            rd.append(scalar)
        self.add(eng, lambda e: e.scalar_tensor_tensor(out, in0, scalar, in1, op0, op1), rd, [out])

    def copy(self, eng, out, in_):
        if eng == 'act':
            self.add('act', lambda e: e.activation(out, in_, AF.Identity), [in_], [out])
        else:
            self.add(eng, lambda e: e.tensor_copy(out, in_), [in_], [out])

    def memset(self, eng, out, val):
        self.add(eng, lambda e: e.memset(out, val), [], [out])

    def reduce(self, eng, out, in_, op, axis=AX.X):
        self.add(eng, lambda e: e.tensor_reduce(out, in_, axis, op), [in_], [out])

    def dma(self, eng, out, in_, **kw):
        self.add(eng, lambda e: e.dma_start(out, in_, **kw), [in_], [out], dma=True)

    def finish(self, out_aps):
        self.add('sp', lambda e: e.nop(), list(out_aps), [])

    def emit(self):
        nc = self.nc
        for e in self.CENG:
            c = 0
            for op in self.ops[e]:
                if op['marked']:
                    c += 1
                op['count'] = c
        csem = {e: self.es.enter_context(nc.semaphore("s_" + e)) for e in self.CENG}
        rsem = {r: [self.es.enter_context(nc.semaphore("d_%s%d" % (r, i))) for i in range(n)]
                for r, n in self.ring.items() if self.ndma[r] > 0}
        ops = self.ops
        stats = {e: [len(ops[e]), sum(len(o['waits']) for o in ops[e])] for e in self.ENGS}
        self.stats = stats

        def run(ename, eng):
            for op in ops[ename]:
                for (k, v) in op['waits']:
                    if k[0] == 'c':
                        eng.wait_ge(csem[k[1]], ops[k[1]][v]['count'])
                    else:
                        eng.wait_ge(rsem[k[1]][k[2]], v)
                inst = op['fn'](eng)
                if op['dma']:
                    k = op['event'][0]
                    inst.then_inc(rsem[k[1]][k[2]], 16)
                elif op['marked']:
                    inst.then_inc(csem[ename], 1)

        if not self.use_block:
            engs = {'pe': nc.tensor, 'act': nc.scalar, 'dve': nc.vector, 'pool': nc.gpsimd, 'sp': nc.sync}
            for op in self.order:
                ename = op['eng']
                eng = engs[ename]
                for (k, v) in op['waits']:
                    if k[0] == 'c':
                        eng.wait_ge(csem[k[1]], ops[k[1]][v]['count'])
                    else:
                        eng.wait_ge(rsem[k[1]][k[2]], v)
                inst = op['fn'](eng)
                if op['dma']:
                    k = op['event'][0]
                    inst.then_inc(rsem[k[1]][k[2]], 16)
                elif op['marked']:
                    inst.then_inc(csem[ename], 1)
            self.es.close()
            return nc
        with nc.Block() as block:
            @block.tensor
            def _(eng):
                run('pe', eng)

            @block.scalar
            def _(eng):
                run('act', eng)

            @block.vector
            def _(eng):
                run('dve', eng)

            @block.gpsimd
            def _(eng):
                run('pool', eng)

            @block.sync
            def _(eng):
                run('sp', eng)
        self.es.close()
        return nc

import ml_dtypes

D = 1024
SEQ = 2048
LC = 256
T = SEQ + LC
DEPTH = 4
NE = 16
CAP = 256
CAPC = 32
NSLOT = CAP + CAPC
FF = 2048
DN_ALPHA = (2 * DEPTH) ** 0.25
LN_EPS = 1e-5
NA_SCALE = 64 ** -0.5
MLA_SCALE = 96 ** -0.5
NEGM = -30000.0
TBS = [(0, 512), (512, 512), (1024, 512), (1536, 512), (2048, 256)]
C_QA, C_KA, C_VA, C_UP, C_UF, C_CQ, C_CKV, C_KRA, C_KRB, C_GT = 0, 256, 512, 768, 1024, 1280, 1536, 1664, 1696, 1728
WIN_COLS = 1728 + 4096
NSTRIP = 22 * 64
XPW = 8 + SEQ + 16 + LC + 8
XP_L = 8
XP_C = 8 + SEQ + 16


def host_constants():
    c = {}
    p = np.arange(128)
    a = p // 64
    kc = p % 64
    dd = np.arange(22)
    d = 10 - dd
    qc = np.arange(64)
    dr = a[:, None] + d[None, :]
    c0 = np.clip(qc - 8, 0, 48)
    colin = (kc[:, None] >= c0[None, :]) & (kc[:, None] < c0[None, :] + 16)
    mall = np.where((np.abs(dr) <= 7)[:, :, None] & colin[:, None, :], 0.0, NEGM)
    mint = np.where(((dr >= -4) & (dr <= 3))[:, :, None] & colin[:, None, :], 0.0, NEGM)
    c['nam'] = np.stack([mall.reshape(128, NSTRIP), mint.reshape(128, NSTRIP)]).astype(np.float32)
    ri = np.clip(dr + 7, 0, 14)
    ci = np.clip(kc[:, None] - qc[None, :], -15, 15) + 15
    c['_na_ri'] = np.broadcast_to(ri[:, :, None], (128, 22, 64)).reshape(128, NSTRIP)
    c['_na_ci'] = np.broadcast_to(ci[:, None, :], (128, 22, 64)).reshape(128, NSTRIP)
    n_freq = 8
    inv = (10000.0 ** (-np.arange(n_freq, dtype=np.float32) / n_freq)).astype(np.float32)
    t = np.arange(SEQ)
    row = (t // 64).astype(np.float32)
    col = (t % 64).astype(np.float32)
    ang = np.concatenate([row[:, None] * inv, col[:, None] * inv], axis=-1)
    cs, sn = np.cos(ang).T, np.sin(ang).T
    c['ropeC'] = np.concatenate([cs, cs], 0).astype(ml_dtypes.bfloat16)
    c['ropeS'] = np.concatenate([-sn, sn], 0).astype(ml_dtypes.bfloat16)
    k = np.arange(64)
    th = 2 * np.pi * np.outer(k, k) / 64.0
    cc, sc = np.cos(th), np.sin(th)
    bd = np.zeros((128, 256), np.float32)
    for g in range(2):
        bd[64 * g:64 * g + 64, 64 * g:64 * g + 64] = -cc
        bd[64 * g:64 * g + 64, 128 + 64 * g:128 + 64 * g + 64] = sc
    c['dftc'] = bd.astype(ml_dtypes.bfloat16)
    c['ident_f'] = np.eye(128, dtype=np.float32)
    c['ident_b'] = np.eye(128).astype(ml_dtypes.bfloat16)
    c['ustri'] = np.triu(np.ones((128, 128)), 1).astype(ml_dtypes.bfloat16)
    c['iota_f'] = np.broadcast_to(np.arange(NSLOT, dtype=np.float32)[None, :], (128, NSLOT)).copy()
    c['pidx'] = np.stack([np.arange(128), np.arange(128) + 128, np.arange(128) + 256], 1).astype(np.float32)
    c['lval'] = (128 * np.arange(16)[None, :] + np.arange(128)[:, None]).astype(np.float32)
    c['jrow'] = np.broadcast_to(np.arange(256, dtype=np.int32)[None, :], (128, 256)).copy()
    half = np.array([1, 2, 4, 8])
    pe = np.zeros((128, 2, 4, 8), np.float32)
    for ch in range(2):
        for pp in range(128):
            g = 2 * ch + pp // 64
            h = half[g]
            for reg, (L, left) in enumerate([(SEQ, True), (SEQ, False), (LC, True), (LC, False)]):
                for j in range(8):
                    tt = j if left else L - 8 + j
                    cnt = min(tt + h, L) - max(tt - h, 0)
                    pe[pp, ch, reg, j] = 1.0 / cnt
    c['pooledge'] = pe.reshape(128, 64)
    pw = np.zeros((128, 2), np.float32)
    for ch in range(2):
        for pp in range(128):
            pw[pp, ch] = 1.0 / (2 * half[2 * ch + pp // 64])
    c['poolinvw'] = pw
    return c


def vec_pj(v):
    return np.ascontiguousarray(v.reshape(-1, 128).T)


class Arena:
    def __init__(self, P, name, nbytes):
        self.t = P.sbuf(name, [128, nbytes // 4], F32)
        self.nbytes = nbytes
        self.off = 0

    def reset(self):
        self.off = 0

    def view(self, shape, dtype):
        esz = _DSZ[dtype]
        n = 1
        for s in shape[1:]:
            n *= s
        nb = (n * esz + 31) // 32 * 32
        assert self.off + nb <= self.nbytes, (self.off, nb, self.nbytes)
        v = self.t[0:shape[0], self.off // 4:(self.off + nb) // 4]
        self.off += nb
        if dtype != F32:
            v = v.bitcast(dtype)
        v = v[:, 0:n]
        if len(shape) == 3:
            v = v.rearrange("p (a b) -> p a b", a=shape[1])
        elif len(shape) == 4:
            v = v.rearrange("p (a b c) -> p a b c", a=shape[1], b=shape[2])
        return v


class PsumRing:
    def __init__(self, P, n=8, name="psb"):
        self.banks = [P.psum("%s%d" % (name, i), [128, 512], F32) for i in range(n)]
        self.i = 0

    def get(self):
        b = self.banks[self.i % len(self.banks)]
        self.i += 1
        return b


def ln_block(P, PS, zt, n, gcol, bcol, out_cb, ones_f, tmp, eng_alt):
    ps_s = PS.get()
    ps_q = PS.get()
    ones_b = tmp['ones_b']
    for j in range(8):
        zb = tmp['zb'][j % 2]
        P.copy('act' if j % 2 else 'dve', zb[:, 0:n], zt[:, j, 0:n])
        P.mm(ps_s[:, 0:n], ones_b[:], zb[:, 0:n], start=(j == 0), stop=(j == 7))
    for j in range(8):
        sq = tmp['sqb'][j % 2]
        P.act(sq[:, 0:n], zt[:, j, 0:n], AF.Square)
        P.mm(ps_q[:, 0:n], ones_b[:], sq[:, 0:n], start=(j == 0), stop=(j == 7))
    mean, rstd, nmr = tmp['mean'], tmp['rstd'], tmp['nmr']
    P.ts('dve', mean[:, 0:n], ps_s[:, 0:n], 1.0 / D, None, ALU.mult)
    P.tt('dve', nmr[:, 0:n], mean[:, 0:n], mean[:, 0:n], ALU.mult)
    P.stt('dve', rstd[:, 0:n], ps_q[:, 0:n], 1.0 / D, nmr[:, 0:n], ALU.mult, ALU.subtract)
    P.ts('dve', rstd[:, 0:n], rstd[:, 0:n], LN_EPS, None, ALU.add)
    P.act(rstd[:, 0:n], rstd[:, 0:n], AF.Sqrt)
    P.add('dve', lambda e: e.reciprocal(rstd[:, 0:n], rstd[:, 0:n]), [rstd[:, 0:n]], [rstd[:, 0:n]])
    P.stt('dve', nmr[:, 0:n], mean[:, 0:n], -1.0, rstd[:, 0:n], ALU.mult, ALU.mult)
    for j in range(8):
        P.tt('dve', zt[:, j, 0:n], zt[:, j, 0:n], rstd[:, 0:n], ALU.mult)
        P.tt('dve', zt[:, j, 0:n], zt[:, j, 0:n], nmr[:, 0:n], ALU.add)
        P.act(zt[:, j, 0:n], zt[:, j, 0:n], AF.Identity, bias=bcol[:, j:j + 1], scale=gcol[:, j:j + 1])
        out_cb(j, zt[:, j, 0:n])


def emit_prologue(P, c, final=False):
    AR, PS_T, PS_A, PS_B = c['AR'], c['PS_T'], c['PS_A'], c['PS_B']
    ones_f, lntmp, pix = c['ones_f'], c['lntmp'], c['pix']
    hT_in, ye, csT_p, gmT_p, modp_d, ln2p_d, e16_d = c['hT_in'], c['ye'], c['csT_p'], c['gmT_p'], c['modp'], c['ln2p'], c['eye16rep']
    mark = AR.off
    csb = AR.view([NE, T], BF16)
    gmb = AR.view([NE, T], BF16)
    csf = AR.view([NE, T], F32)
    e16 = AR.view([NE, NE * 128], BF16)
    modp = AR.view([128, 48, 2], F32)
    ln2s = AR.view([128, 2, 8], F32)
    GE = 4
    yeg = AR.view([128, GE, 3, D], BF16)
    stgs = [AR.view([128, GE, 2, 512], BF16) for _ in range(2)]
    sctr = [0]
    eqt = [AR.view([128, 512], BF16) for _ in range(2)]
    hb = AR.view([128, 8, 512], F32)
    zt = AR.view([128, 8, 512], F32)
    P.dma('sp', csf[:], csT_p[:, :])
    P.ts('dve', csf[:, SEQ:T], csf[:, SEQ:T], -float(CAP), None, ALU.add)
    P.copy('dve', csb[:], csf[:])
    P.dma('sp', csf[:], gmT_p[:, :])
    P.copy('dve', gmb[:], csf[:])
    P.dma('sp', e16[:], e16_d[:, :])
    P.dma('sp', modp[:], modp_d[:, :, :])
    P.dma('sp', ln2s[:], ln2p_d[:, :, :])
    G2 = 40
    for (t0, n) in TBS:
        x = 0 if t0 < SEQ else 1
        lat = t0 < SEQ
        if final and not lat:
            continue
        P.dma('sp', hb[:, :, 0:n], hT_in[:, t0:t0 + n].rearrange("(k p) n -> p k n", p=128))
        for g in range(NE // GE):
            stg = stgs[sctr[0] % 2]
            sctr[0] += 1
            for q in range(2):
                P.dma('sp' if q == 0 else 'act', yeg[:, :, q, :], ye[GE * g:GE * g + GE, 128 * q:128 * q + 128, :].rearrange("e p d -> p e d"))
            P.dma('sp', yeg[0:CAPC, :, 2, :], ye[GE * g:GE * g + GE, CAP:NSLOT, :].rearrange("e p d -> p e d"))
            for el in range(GE):
                e_ = GE * g + el
                pcs = PS_A.get()
                pgm = PS_A.get()
                P.mm(pcs[:, 0:n], e16[:, 128 * e_:128 * e_ + 128], csb[:, t0:t0 + n])
                P.mm(pgm[:, 0:n], e16[:, 128 * e_:128 * e_ + 128], gmb[:, t0:t0 + n])
                if lat:
                    for q in range(2):
                        et = eqt[q]
                        P.ts('dve', et[:, 0:n], pcs[:, 0:n], pix[:, q:q + 1], None, ALU.is_equal)
                        P.tt('dve', stg[:, el, q, 0:n], et[:, 0:n], pgm[:, 0:n], ALU.mult)
                else:
                    et = eqt[0]
                    P.ts('dve', et[0:CAPC, 0:n], pcs[0:CAPC, 0:n], pix[0:CAPC, 0:1], None, ALU.is_equal)
                    P.tt('dve', stg[0:CAPC, el, 0, 0:n], et[0:CAPC, 0:n], pgm[0:CAPC, 0:n], ALU.mult)
            for jo in range(8):
                po = PS_T.get()
                if lat:
                    for el in range(GE):
                        for q in range(2):
                            P.mm(po[:, 0:n], yeg[:, el, q, 128 * jo:128 * jo + 128], stg[:, el, q, 0:n],
                                 start=(el == 0 and q == 0), stop=(el == GE - 1 and q == 1))
                else:
                    for el in range(GE):
                        P.mm(po[:, 0:n], yeg[0:CAPC, el, 2, 128 * jo:128 * jo + 128], stg[0:CAPC, el, 0, 0:n],
                             start=(el == 0), stop=(el == GE - 1))
                if g == 0:
                    P.act(hb[:, jo, 0:n], hb[:, jo, 0:n], AF.Identity, scale=float(DN_ALPHA))
                    P.stt('dve', zt[:, jo, 0:n], po[:, 0:n], modp[:, G2 + jo, x:x + 1], hb[:, jo, 0:n], ALU.mult, ALU.add)
                else:
                    P.stt('dve', zt[:, jo, 0:n], po[:, 0:n], modp[:, G2 + jo, x:x + 1], zt[:, jo, 0:n], ALU.mult, ALU.add)

        def after_ln(j, ap, t0=t0, n=n, x=x):
            if final:
                P.dma('sp', c['hdst'][128 * j:128 * j + 128, t0:t0 + n], ap)
            else:
                P.dma('sp', c['hdst'][128 * j:128 * j + 128, t0:t0 + n], ap)
                P.act(c['uT'][:, j, t0:t0 + n], ap, AF.Identity, bias=c['modT'][:, j, x:x + 1], scale=c['ops1'][:, j, x:x + 1])
        ln_block(P, PS_A, zt, n, ln2s[:, 0, :], ln2s[:, 1, :], after_ln, ones_f, lntmp, ['dve', 'pool'])
    AR.off = mark


def build_D():
    nc = bass.Bass("TRN2", target_bir_lowering=False)
    P = Prog(nc, ring_sizes={'sp': 16, 'pool': 12, 'act': 4})
    c = {}
    c['hT_in'] = P.dram_in("hT_in", [D, T], F32)
    c['ye'] = P.dram_in("ye", [NE, NSLOT, D], BF16)
    c['csT_p'] = P.dram_in("csT_p", [NE, T], F32)
    c['gmT_p'] = P.dram_in("gmT_p", [NE, T], F32)
    c['modp'] = P.dram_in("modp", [128, 48, 2], F32)
    c['ln2p'] = P.dram_in("ln2p", [128, 2, 8], F32)
    c['eye16rep'] = P.dram_in("eye16rep", [NE, NE * 128], BF16)
    pidx = P.dram_in("pidx", [128, 3], F32)
    out = P.dram_out("out", [D, SEQ], F32)
    c['hdst'] = out
    c['ones_f'] = P.sbuf("ones_f", [128, 128], F32)
    c['pix'] = P.sbuf("pix", [128, 3], F32)
    c['lntmp'] = dict(sq=[P.sbuf("lnsq0", [128, 512], F32), P.sbuf("lnsq1", [128, 512], F32)], mean=P.sbuf("lnmean", [128, 512], F32),
                      rstd=P.sbuf("lnrstd", [128, 512], F32), nmr=P.sbuf("lnnmr", [128, 512], F32),
                      zb=[P.sbuf("lnzb0", [128, 512], BF16), P.sbuf("lnzb1", [128, 512], BF16)],
                      sqb=[P.sbuf("lnsqb0", [128, 512], BF16), P.sbuf("lnsqb1", [128, 512], BF16)])
    c['lntmp']['ones_b'] = P.sbuf("ones_b", [128, 128], BF16)
    P.memset('dve', c['lntmp']['ones_b'][:], 1.0)
    c['AR'] = Arena(P, "arena", 150 * 1024)
    c['PS_T'] = PsumRing(P, 4, "pst")
    c['PS_A'] = PsumRing(P, 2, "psa")
    c['PS_B'] = PsumRing(P, 1, "psbx")
    P.memset('dve', c['ones_f'][:], 1.0)
    P.dma('sp', c['pix'][:], pidx[:, :])
    emit_prologue(P, c, final=True)
    P.finish([out])
    P.emit()
    return nc, P


def build_B():
    TBB = 8 * NSLOT
    nc = bass.Bass("TRN2", target_bir_lowering=False)
    P = Prog(nc, ring_sizes={'sp': 16, 'pool': 12, 'act': 4})
    xe_in = P.dram_in("xe", [2, D, TBB], BF16)
    wg = P.dram_in("wg", [2, D, FF], F32)
    wu = P.dram_in("wu", [2, D, FF], F32)
    wd = P.dram_in("wd", [2, FF, D], F32)
    ye = P.dram_out("ye", [2, TBB, D], BF16)
    xes = P.sbuf("xes", [128, 8, TBB], BF16)
    wgs = P.sbuf("wgs", [128, 8, FF], BF16)
    wus = P.sbuf("wus", [128, 8, FF], BF16)
    wds = P.sbuf("wds", [128, 16, D], BF16)
    actT = P.sbuf("actT", [128, 16, 512], BF16)
    sil = [P.sbuf("sil%d" % i, [128, 512], F32) for i in range(2)]
    yo = [P.sbuf("yo%d" % i, [128, D], BF16) for i in range(2)]
    PS_T = PsumRing(P, 4, "pst")
    PS_A = PsumRing(P, 2, "psa")
    blocks = [(0, 512), (512, 512), (1024, 512), (1536, 512), (2048, 256)]
    yi = 0
    for e_ in range(2):
        P.dma('sp', xes[:], xe_in[e_].rearrange("(k p) n -> p k n", p=128))
        for pc in range(4):
            P.dma('pool', wgs[:, :, 512 * pc:512 * pc + 512], wg[e_, :, 512 * pc:512 * pc + 512].rearrange("(k p) n -> p k n", p=128))
            P.dma('pool', wus[:, :, 512 * pc:512 * pc + 512], wu[e_, :, 512 * pc:512 * pc + 512].rearrange("(k p) n -> p k n", p=128))
        for pc in range(4):
            P.dma('pool', wds[:, 4 * pc:4 * pc + 4, :], wd[e_, 512 * pc:512 * pc + 512, :].rearrange("(k p) n -> p k n", p=128))
        for (t0, n) in blocks:
            for f in range(16):
                pa = PS_T.get()
                pu = PS_T.get()
                for k in range(8):
                    P.mm(pa[:, 0:n], wgs[:, k, 128 * f:128 * f + 128], xes[:, k, t0:t0 + n], start=(k == 0), stop=(k == 7))
                for k in range(8):
                    P.mm(pu[:, 0:n], wus[:, k, 128 * f:128 * f + 128], xes[:, k, t0:t0 + n], start=(k == 0), stop=(k == 7))
                s_ = sil[f % 2]
                P.act(s_[:, 0:n], pa[:, 0:n], AF.Silu)
                P.tt('dve', actT[:, f, 0:n], s_[:, 0:n], pu[:, 0:n], ALU.mult)
            for m in range(n // 128):
                y_ = yo[yi % 2]
                yi += 1
                for hf in range(2):
                    po = PS_A.get()
                    for f in range(16):
                        P.mm(po[:, 0:512], actT[:, f, 128 * m:128 * m + 128], wds[:, f, 512 * hf:512 * hf + 512], start=(f == 0), stop=(f == 15))
                    P.copy('act' if hf == 0 else 'dve', y_[:, 512 * hf:512 * hf + 512], po[:, 0:512])
                P.dma('sp', ye[e_, t0 + 128 * m:t0 + 128 * m + 128, :], y_[:])
    P.finish([ye])
    P.emit()
    return nc, P


def build_A(prologue, stage=99, sub=99):
    nc = bass.Bass("TRN2", target_bir_lowering=False)
    P = Prog(nc, ring_sizes={'sp': 16, 'pool': 12, 'act': 4})
    I = {}

    def inp(name, shape, dt=F32):
        I[name] = P.dram_in(name, shape, dt)
        return I[name]
    hT_in = inp("hT_in", [D, T])
    cs2 = inp("cs2", [128, 8, 2])
    wmod = inp("wmod", [D, 6 * D])
    bmod = inp("bmod", [128, 48])
    w_in = inp("w_in", [D, WIN_COLS])
    nab = inp("nab", [4, 128, NSTRIP])
    nam = inp("nam", [2, 128, NSTRIP])
    pool_w = inp("pool_w", [4, 64, 64])
    pvec = inp("pvec", [128, 8])
    w_uq = inp("w_uq", [256, 512])
    w_ukv = inp("w_ukv", [128, 512])
    w_br = inp("w_br", [D, D])
    w_out = inp("w_out", [D, D])
    ln1 = inp("ln1", [128, 2, 8])
    w_rt = inp("w_rt", [D, NE])
    ropeC = inp("ropeC", [32, SEQ], BF16)
    ropeS = inp("ropeS", [32, SEQ], BF16)
    dftc = inp("dftc", [128, 256], BF16)
    ident_f = inp("ident_f", [128, 128])
    ident_b = inp("ident_b", [128, 128], BF16)
    ustri = inp("ustri", [128, 128], BF16)
    iota_f = inp("iota_f", [128, NSLOT])
    pidx = inp("pidx", [128, 3])
    lval = inp("lval", [128, 16])
    jrow = inp("jrow", [128, 256], I32)
    pooledge = inp("pooledge", [128, 64])
    poolinvw = inp("poolinvw", [128, 2])
    if prologue:
        ye = inp("ye", [NE, NSLOT, D], BF16)
        csT_p = inp("csT_p", [NE, T])
        gmT_p = inp("gmT_p", [NE, T])
        modp = inp("modp", [128, 48, 2])
        ln2p = inp("ln2p", [128, 2, 8])
        eye16rep = inp("eye16rep", [NE, NE * 128], BF16)
    h1T = P.dram_out("h1T", [D, T], F32)
    xeT = P.dram_out("xeT", [NE, D, NSLOT], BF16)
    csT_o = P.dram_out("csT_out", [NE, T], F32)
    gmT_o = P.dram_out("gmT_out", [NE, T], F32)
    modT_o = P.dram_out("modT_out", [128, 48, 2], F32)
    dbg = P.dram_out("dbg", [D, T], F32) if stage < 99 else None
    hs = P.dram_tmp("hs", [D, T], F32)
    outs = [h1T, xeT, csT_o, gmT_o, modT_o] + ([dbg] if dbg is not None else [])

    S = {}

    def sb(name, shape, dt=F32):
        S[name] = P.sbuf(name, shape, dt)
        return S[name]
    ones_f = sb("ones_f", [128, 128])
    ones_b = sb("ones_b", [128, 128], BF16)
    idf = sb("idf", [128, 128])
    idb = sb("idb", [128, 128], BF16)
    ust = sb("ust", [128, 128], BF16)
    iof = sb("iof", [128, NSLOT])
    pix = sb("pix", [128, 3])
    modT = sb("modT", [128, 48, 2])
    ops1 = sb("ops1", [128, 8, 2])
    ops2 = sb("ops2", [128, 8, 2])
    pv = sb("pv", [128, 8])
    ln1s = sb("ln1s", [128, 2, 8])
    scs = sb("scs", [128, 8, 2])
    bm = sb("bm", [128, 48])
    lntmp = dict(sq=[sb("lnsq0", [128, 512]), sb("lnsq1", [128, 512])], mean=sb("lnmean", [128, 512]),
                 rstd=sb("lnrstd", [128, 512]), nmr=sb("lnnmr", [128, 512]),
                 zb=[sb("lnzb0", [128, 512], BF16), sb("lnzb1", [128, 512], BF16)],
                 sqb=[sb("lnsqb0", [128, 512], BF16), sb("lnsqb1", [128, 512], BF16)], ones_b=ones_b)
    wring = [sb("wr%d" % i, [128, 8, 512], BF16) for i in range(3)]
    wri = [0]
    PT = [sb("pt%d" % i, [128, 512], BF16) for i in range(4)]
    pti = [0]
    rden = sb("rden", [128, 512])
    rbc = sb("rbc", [128, 512])
    AR = Arena(P, "arena", 150 * 1024)
    PS_T = PsumRing(P, 4, "pst")
    PS_A = PsumRing(P, 2, "psa")
    PS_B = PsumRing(P, 1, "psbx")
    pstb = P.psum("pstb", [128, 1024], BF16)

    def nextw():
        w = wring[wri[0] % 3]
        wri[0] += 1
        return w

    def nextpt():
        t = PT[pti[0] % 4]
        pti[0] += 1
        return t
    alt = [0]

    def ev_eng():
        alt[0] += 1
        return 'dve' if alt[0] % 2 else 'act'

    def loadw(src, lo, n, kch=8):
        w = nextw()
        P.dma('pool', w[:, 0:kch, 0:n], src[:, lo:lo + n].rearrange("(k p) n -> p k n", p=128))
        return w

    P.memset('dve', ones_f[:], 1.0)
    P.memset('dve', ones_b[:], 1.0)
    P.dma('sp', idf[:], ident_f[:, :])
    P.dma('sp', idb[:], ident_b[:, :])
    P.dma('sp', ust[:], ustri[:, :])
    P.dma('sp', iof[:], iota_f[:, :])
    P.dma('sp', pix[:], pidx[:, :])
    P.dma('sp', pv[:], pvec[:, :])
    P.dma('sp', ln1s[:], ln1[:, :, :])
    P.dma('sp', scs[:], cs2[:, :, :])
    P.dma('sp', bm[:], bmod[:, :])
    P.act(scs[:], scs[:], AF.Silu)
    mark0 = AR.off
    wm = [AR.view([128, 8, 1024], F32) for _ in range(2)]
    modrow = AR.view([2, 6 * D], F32)
    pm = PS_B.get()
    for blk in range(6):
        w = wm[blk % 2]
        P.dma('sp', w[:], wmod[:, 1024 * blk:1024 * blk + 1024].rearrange("(k p) n -> p k n", p=128))
        for hb2 in range(2):
            prow = PS_T.get()
            for k in range(8):
                P.mm(prow[0:2, 0:512], scs[:, k, :], w[:, k, 512 * hb2:512 * hb2 + 512], start=(k == 0), stop=(k == 7))
            P.copy('dve', modrow[:, 1024 * blk + 512 * hb2:1024 * blk + 512 * hb2 + 512], prow[0:2, 0:512])
    for j in range(48):
        P.mm(pm[:, 2 * j:2 * j + 2], modrow[:, 128 * j:128 * j + 128], idf[0:2, 0:2])
    for x in range(2):
        P.tt('dve', modT[:, :, x], pm[:, 0:96].rearrange("p (j x) -> p j x", x=2)[:, :, x], bm[:, :], ALU.add)
    P.ts('dve', ops1[:], modT[:, 8:16, :], 1.0, None, ALU.add)
    P.ts('dve', ops2[:], modT[:, 32:40, :], 1.0, None, ALU.add)
    P.dma('sp', modT_o[:, :, :], modT[:])
    AR.off = mark0
    SH1, G1, SH2, G2 = 0, 16, 24, 40

    uT = AR.view([128, 8, T], BF16)
    mark_u = AR.off
    if prologue:
        hsrc = hs
        emit_prologue(P, dict(AR=AR, PS_T=PS_T, PS_A=PS_A, PS_B=PS_B, ones_f=ones_f, lntmp=lntmp, pix=pix, hT_in=hT_in, ye=ye,
                              csT_p=csT_p, gmT_p=gmT_p, modp=modp, ln2p=ln2p, eye16rep=eye16rep, hdst=hs, uT=uT, ops1=ops1, modT=modT))
    else:
        hsrc = hT_in
        hb = [AR.view([128, 8, 512], F32) for _ in range(2)]
        for bi, (t0, n) in enumerate(TBS):
            x = 0 if t0 < SEQ else 1
            b = hb[bi % 2]
            P.dma('sp', b[:, :, 0:n], hT_in[:, t0:t0 + n].rearrange("(k p) n -> p k n", p=128))
            for j in range(8):
                P.ts('dve', uT[:, j, t0:t0 + n], b[:, j, 0:n], ops1[:, j, x:x + 1],
                     modT[:, SH1 + j, x:x + 1], ALU.mult, ALU.add)
    AR.off = mark_u
    yT = [AR.view([128, 2, T], BF16) for _ in range(4)]
    mark_y = AR.off

    def proj_fm(wv, m_lo, M, consume, po=0):
        for (t0, n) in TBS:
            ps = PS_T.get()
            for k in range(8):
                P.mm(ps[po:po + M, 0:n], wv[:, k, m_lo:m_lo + M], uT[:, k, t0:t0 + n], start=(k == 0), stop=(k == 7))
            consume(ps, t0, n)

    def finalize_attn(po, h, dst, t0, n):
        c = h // 2
        if h % 2 == 0:
            dp, lo, hi, op_ = 64, 0, 64, 64
        else:
            dp, lo, hi, op_ = 0, 64, 128, 0
        P.add('dve', lambda e: e.reciprocal(rden[dp:dp + 1, 0:n], po[dp:dp + 1, 0:n]), [po[dp:dp + 1, 0:n]], [rden[dp:dp + 1, 0:n]])
        pbc = PS_B.get()
        P.mm(pbc[lo:hi, 0:n], ones_f[dp:dp + 1, 0:64], rden[dp:dp + 1, 0:n])
        P.copy('act', rbc[lo:hi, 0:n], pbc[lo:hi, 0:n])
        P.tt('dve', dst[lo:hi, c, t0:t0 + n], po[lo:hi, 0:n], rbc[lo:hi, 0:n], ALU.mult)

    def init_vpad(v):
        P.memset('pool', v, 0.0)

    if stage >= 1:
        AR.off = mark_y
        qaT = AR.view([128, 2, T], BF16)
        kaT = AR.view([128, 2, T], BF16)
        va2 = AR.view([128, 18, 4, 128], BF16)
        wall = AR.view([128, NSTRIP], BF16)
        wint = AR.view([128, NSTRIP], BF16)
        nbf = AR.view([128, NSTRIP], F32)
        nmk = AR.view([128, 2, NSTRIP], F32)
        P.dma('sp', nmk[:], nam.rearrange("a p n -> p a n"))
        P.memset('pool', va2[:], 0.0)
        for h in range(4):
            cc_ = 64 if h % 2 == 0 else 0
            P.memset('pool', va2[:, :, h, cc_:cc_ + 1], 1.0)
        wv = loadw(w_in, C_QA, 512)
        for ci, dst in ((0, qaT), (256, kaT)):
            for c in range(2):
                proj_fm(wv, ci + 128 * c, 128,
                        lambda ps, t0, n, dst=dst, c=c: P.copy(ev_eng(), dst[:, c, t0:t0 + n], ps[:, 0:n]))
        wv = loadw(w_in, C_VA, 256)
        for tc in range(18):
            ps = PS_T.get()
            for k in range(8):
                P.mm(ps[:, 0:256], uT[:, k, 128 * tc:128 * tc + 128], wv[:, k, 0:256], start=(k == 0), stop=(k == 7))
            for hp in range(2):
                src = ps[:, 0:256].rearrange("p (h d) -> p h d", h=4)
                if hp == 0:
                    P.copy(ev_eng(), va2[:, tc, 0::2, 0:64], src[:, 0::2, :])
                else:
                    P.copy(ev_eng(), va2[:, tc, 1::2, 64:128], src[:, 1::2, :])
        for h in range(4):
            c, pb = h // 2, 64 * (h % 2)
            M = 65 if h % 2 == 0 else 128
            P.dma('sp', nbf[:], nab[h, :, :])
            for which, dstw in ((0, wall), (1, wint)):
                P.tt('pool', dstw[:], nbf[:], nmk[:, which, :], ALU.add)
                P.act(dstw[:], dstw[:], AF.Exp)
            for qb in range(4):
                lo_i = [0, 2, 6, 10][qb]
                hi_i = [5, 9, 13, 15][qb]
                seq = list(range(lo_i, hi_i + 1)) + [16, 17]
                po = PS_A.get()
                def sc_(i):
                    ps = PS_T.get()
                    P.mm(ps[:, 0:512], kaT[pb:pb + 64, c, 128 * i:128 * i + 128], qaT[pb:pb + 64, c, 512 * qb:512 * qb + 512])
                    return ps
                pq = [sc_(seq[ii]) for ii in range(min(3, len(seq)))]
                for idx, i in enumerate(seq):
                    ps = pq.pop(0)
                    if idx + 3 < len(seq):
                        pq.append(sc_(seq[idx + 3]))
                    pt = nextpt()
                    P.act(pt[:], ps[:, 0:512], AF.Exp, scale=NA_SCALE)
                    if i < 16:
                        s0 = (10 - (2 * i - 8 * qb)) * 64
                        me = 'dve'
                        if qb == 0:
                            wa = wall if i <= 3 else wint
                            P.tt(me, pt[:, 0:256], pt[:, 0:256], wa[:, s0:s0 + 256], ALU.mult)
                            P.tt(me, pt[:, 256:512], pt[:, 256:512], wint[:, s0 + 256:s0 + 512], ALU.mult)
                        elif qb == 3:
                            wa = wall if i >= 12 else wint
                            P.tt(me, pt[:, 0:320], pt[:, 0:320], wint[:, s0:s0 + 320], ALU.mult)
                            P.tt(me, pt[:, 320:512], pt[:, 320:512], wa[:, s0 + 320:s0 + 512], ALU.mult)
                        else:
                            P.tt(me, pt[:], pt[:], wint[:, s0:s0 + 512], ALU.mult)
                    P.mm(po[0:M, 0:512], va2[:, i, h, 0:M], pt[:], start=(idx == 0), stop=(idx == len(seq) - 1))
                finalize_attn(po, h, yT[0], 512 * qb, 512)
            po = PS_A.get()
            for idx, i in enumerate([16, 17]):
                ps = PS_T.get()
                P.mm(ps[:, 0:256], kaT[pb:pb + 64, c, 128 * i:128 * i + 128], qaT[pb:pb + 64, c, SEQ:T])
                pt = nextpt()
                P.act(pt[:, 0:256], ps[:, 0:256], AF.Exp, scale=NA_SCALE)
                P.mm(po[0:M, 0:256], va2[:, i, h, 0:M], pt[:, 0:256], start=(idx == 0), stop=(idx == 1))
            finalize_attn(po, h, yT[0], SEQ, 256)


    if stage >= 2:
        AR.off = mark_y
        xp = AR.view([128, 2, XPW], F32)
        la = AR.view([128, XPW], F32)
        lb = AR.view([128, XPW], F32)
        ybf = AR.view([128, 2, T], BF16)
        pwbd = AR.view([128, 2, 128], BF16)
        pwf = AR.view([128, 2, 128], F32)
        pe = AR.view([128, 64], F32)
        piw = AR.view([128, 2], F32)
        etmp = AR.view([128, 8], F32)
        P.memset('pool', xp[:], 0.0)
        P.memset('pool', la[:], 0.0)
        P.memset('pool', lb[:], 0.0)
        P.memset('pool', pwf[:], 0.0)
        for g in range(4):
            o = 64 * (g % 2)
            P.dma('sp', pwf[o:o + 64, g // 2, o:o + 64], pool_w[g, :, :])
        P.copy('dve', pwbd[:], pwf[:])
        P.dma('sp', pe[:], pooledge[:, :])
        P.dma('sp', piw[:], poolinvw[:, :])
        wv = loadw(w_in, C_UP, 256)

        def up_consume(ps, t0, n, c):
            off = XP_L + t0 if t0 < SEQ else XP_C
            P.copy(ev_eng(), xp[:, c, off:off + n], ps[:, 0:n])
        for c in range(2):
            proj_fm(wv, 128 * c, 128, lambda ps, t0, n, c=c: up_consume(ps, t0, n, c))
        W = XPW
        for c in range(2):
            x = xp[:, c, :]
            P.tt('dve', la[:, 1:W], x[:, 1:W], x[:, 0:W - 1], ALU.add)
            P.tt('dve', lb[:, 1:W - 1], la[:, 2:W], la[:, 0:W - 2], ALU.add)
            if c == 1:
                P.tt('dve', la[:, 2:W - 2], lb[:, 4:W], lb[:, 0:W - 4], ALU.add)
                P.tt('dve', lb[:, 4:W - 4], la[:, 8:W], la[:, 0:W - 8], ALU.add)
            for (pl, src) in ((0, la), (64, lb)):
                for (off, L, toff) in ((XP_L, SEQ, 0), (XP_C, LC, SEQ)):
                    P.stt('dve', ybf[pl:pl + 64, c, toff:toff + L], src[pl:pl + 64, off:off + L], piw[pl:pl + 64, c:c + 1],
                          x[pl:pl + 64, off:off + L], ALU.mult, ALU.subtract)
                for reg, (off, toff) in enumerate(((XP_L, 0), (XP_L + SEQ - 8, SEQ - 8), (XP_C, SEQ), (XP_C + LC - 8, T - 8))):
                    ec = (c * 4 + reg) * 8
                    P.tt('dve', etmp[pl:pl + 64, :], src[pl:pl + 64, off:off + 8], pe[pl:pl + 64, ec:ec + 8], ALU.mult)
                    P.tt('dve', ybf[pl:pl + 64, c, toff:toff + 8], etmp[pl:pl + 64, :], x[pl:pl + 64, off:off + 8], ALU.subtract)
            for (t0, n) in TBS:
                ps = PS_T.get()
                P.mm(ps[:, 0:n], pwbd[:, c, :], ybf[:, c, t0:t0 + n])
                P.ts('dve', yT[1][:, c, t0:t0 + n], ps[:, 0:n], pv[:, c:c + 1], None, ALU.mult)

    if stage >= 3:
        AR.off = mark_y
        ufT = AR.view([128, 2, T], BF16)
        AB = AR.view([128, 18, 2, 256], BF16)
        dfc = AR.view([128, 256], BF16)
        lv = AR.view([128, 16], F32)
        lofs = AR.view([128, 16, 8, 2], F32)
        jr = AR.view([128, 256], I32)
        tabC = AR.view([128, 16, 256], BF16)
        tabS = AR.view([128, 16, 256], BF16)
        ki = [AR.view([128, 256], I32) for _ in range(2)]
        P.dma('sp', dfc[:], dftc[:, :])
        P.dma('sp', lv[:], lval[:, :])
        P.dma('sp', jr[:], jrow[:, :])
        for jb in range(8):
            P.ts('dve', lofs[:, :, jb, 0], lv[:], 256.0 * jb, 512.0, ALU.mult, ALU.add)
            P.ts('dve', lofs[:, :, jb, 1], lv[:], 256.0 * jb, None, ALU.mult)
        wv = loadw(w_in, C_UF, 256)
        for c in range(2):
            proj_fm(wv, 128 * c, 128, lambda ps, t0, n, c=c: P.copy(ev_eng(), ufT[:, c, t0:t0 + n], ps[:, 0:n]))
        for tc in range(18):
            for c in range(2):
                ps = PS_T.get()
                P.mm(ps[:, 0:256], ufT[:, c, 128 * tc:128 * tc + 128], dfc[:, 0:256])
                P.copy(ev_eng(), AB[:, tc, c, :], ps[:, 0:256])
        kc_ = [0]

        def gen_tab(dst, a, ofs_ap, ofs_imm, mask, scale):
            k = ki[kc_[0] % 2]
            kc_[0] += 1
            if ofs_ap is not None:
                P.ts('dve', k[:], jr[:], lv[:, a:a + 1], ofs_ap, ALU.mult, ALU.add)
            else:
                P.ts('dve', k[:], jr[:], lv[:, a:a + 1], float(ofs_imm), ALU.mult, ALU.add)
            P.ts('dve', k[:], k[:], mask, None, ALU.bitwise_and)
            P.act(dst, k[:], AF.Sin, bias=mpi[:, 0:1], scale=scale)
        mpi = AR.view([128, 1], F32)
        P.memset('dve', mpi[:], -float(np.pi))
        for jb in range(8):
            for a in range(16):
                gen_tab(tabC[:, a, :], a, lofs[:, a, jb, 0:1], None, 2047, 2.0 * np.pi / 2048.0)
                gen_tab(tabS[:, a, :], a, lofs[:, a, jb, 1:2], None, 2047, 2.0 * np.pi / 2048.0)
            for c in range(2):
                po = PS_A.get()
                for a in range(16):
                    P.mm(po[:, 0:256], AB[:, a, c, 0:128], tabC[:, a, :], start=(a == 0), stop=False)
                    P.mm(po[:, 0:256], AB[:, a, c, 128:256], tabS[:, a, :], start=False, stop=(a == 15))
                P.ts('dve', yT[2][:, c, 256 * jb:256 * jb + 256], po[:, 0:256], float((SEQ * 64.0) ** -0.5), None, ALU.mult)
        for a in range(2):
            gen_tab(tabC[:, a, :], a, None, 64, 255, 2.0 * np.pi / 256.0)
            gen_tab(tabS[:, a, :], a, None, 0, 255, 2.0 * np.pi / 256.0)
        for c in range(2):
            po = PS_A.get()
            for a in range(2):
                P.mm(po[:, 0:256], AB[:, 16 + a, c, 0:128], tabC[:, a, :], start=(a == 0), stop=False)
                P.mm(po[:, 0:256], AB[:, 16 + a, c, 128:256], tabS[:, a, :], start=False, stop=(a == 1))
            P.ts('dve', yT[2][:, c, SEQ:T], po[:, 0:256], float((LC * 64.0) ** -0.5), None, ALU.mult)


    if stage >= 4:
        AR.off = mark_y
        cqn = AR.view([128, 2, T], BF16)
        ckvn = AR.view([128, T], BF16)
        kro = AR.view([128, T], BF16)
        rC = AR.view([128, SEQ], BF16)
        rS = AR.view([128, SEQ], BF16)
        qm = AR.view([128, T], BF16)
        km = AR.view([128, T], BF16)
        vm = AR.view([128, 18, 128], BF16)
        wuq = AR.view([128, 2, 512], BF16)
        wukv = AR.view([128, 512], BF16)
        sqb = [AR.view([128, 512], BF16) for _ in range(2)]
        rst = AR.view([128, 512], F32)
        rt1 = AR.view([128, 512], F32)
        rt2 = AR.view([128, 512], F32)
        P.dma('sp', rC[64:96, :], ropeC[:, :])
        P.dma('sp', rS[64:96, :], ropeS[:, :])
        P.dma('pool', wuq[:], w_uq.rearrange("(k p) n -> p k n", p=128))
        P.dma('pool', wukv[:], w_ukv[:, :])
        wv = loadw(w_in, C_CQ, 448)

        def rope_apply(dst, psa, psb, t0, n):
            if t0 < SEQ:
                P.tt('dve', rt1[64:96, 0:n], psa[64:96, 0:n], rC[64:96, t0:t0 + n], ALU.mult)
                P.tt('dve', rt2[64:96, 0:n], psb[64:96, 0:n], rS[64:96, t0:t0 + n], ALU.mult)
                P.tt('dve', dst[64:96, t0:t0 + n], rt1[64:96, 0:n], rt2[64:96, 0:n], ALU.add)
            else:
                P.copy('act', dst[64:96, t0:t0 + n], psa[64:96, 0:n])
        for (t0, n) in TBS:
            pc = [PS_T.get(), PS_T.get()]
            pss = PS_B.get()
            for c in range(2):
                for k in range(8):
                    P.mm(pc[c][:, 0:n], wv[:, k, 128 * c:128 * c + 128], uT[:, k, t0:t0 + n], start=(k == 0), stop=(k == 7))
                P.act(sqb[c][:, 0:n], pc[c][:, 0:n], AF.Square)
                P.mm(pss[:, 0:n], ones_b[:], sqb[c][:, 0:n], start=(c == 0), stop=(c == 1))
            P.ts('dve', rst[:, 0:n], pss[:, 0:n], 1.0 / 256.0, LN_EPS, ALU.mult, ALU.add)
            P.act(rst[:, 0:n], rst[:, 0:n], AF.Sqrt)
            P.add('dve', lambda e, n=n: e.reciprocal(rst[:, 0:n], rst[:, 0:n]), [rst[:, 0:n]], [rst[:, 0:n]])
            for c in range(2):
                P.stt('dve', cqn[:, c, t0:t0 + n], pc[c][:, 0:n], pv[:, 2 + c:3 + c], rst[:, 0:n], ALU.mult, ALU.mult)
            pk = PS_T.get()
            pss = PS_B.get()
            for k in range(8):
                P.mm(pk[:, 0:n], wv[:, k, 256:384], uT[:, k, t0:t0 + n], start=(k == 0), stop=(k == 7))
            P.act(sqb[0][:, 0:n], pk[:, 0:n], AF.Square)
            P.mm(pss[:, 0:n], ones_b[:], sqb[0][:, 0:n])
            P.ts('dve', rst[:, 0:n], pss[:, 0:n], 1.0 / 128.0, LN_EPS, ALU.mult, ALU.add)
            P.act(rst[:, 0:n], rst[:, 0:n], AF.Sqrt)
            P.add('dve', lambda e, n=n: e.reciprocal(rst[:, 0:n], rst[:, 0:n]), [rst[:, 0:n]], [rst[:, 0:n]])
            P.stt('dve', ckvn[:, t0:t0 + n], pk[:, 0:n], pv[:, 4:5], rst[:, 0:n], ALU.mult, ALU.mult)
            pa = PS_T.get()
            pb_ = PS_T.get()
            for k in range(8):
                P.mm(pa[64:96, 0:n], wv[:, k, 384:416], uT[:, k, t0:t0 + n], start=(k == 0), stop=(k == 7))
            for k in range(8):
                P.mm(pb_[64:96, 0:n], wv[:, k, 416:448], uT[:, k, t0:t0 + n], start=(k == 0), stop=(k == 7))
            rope_apply(kro, pa, pb_, t0, n)
        for h in range(4):
            M = 65 if h % 2 == 0 else 128
            vo = 0 if h % 2 == 0 else 64
            P.memset('pool', vm[:], 0.0)
            oc = 64 if h % 2 == 0 else 0
            P.memset('pool', vm[:, :, oc:oc + 1], 1.0)
            for (t0, n) in TBS:
                pa = PS_T.get()
                pb_ = PS_T.get()
                for k in range(2):
                    P.mm(pa[0:96, 0:n], wuq[:, k, 128 * h:128 * h + 96], cqn[:, k, t0:t0 + n], start=(k == 0), stop=(k == 1))
                for k in range(2):
                    P.mm(pb_[64:96, 0:n], wuq[:, k, 128 * h + 96:128 * h + 128], cqn[:, k, t0:t0 + n], start=(k == 0), stop=(k == 1))
                P.copy('act', qm[0:64, t0:t0 + n], pa[0:64, 0:n])
                rope_apply(qm, pa, pb_, t0, n)
                pk = PS_T.get()
                P.mm(pk[0:64, 0:n], wukv[:, 128 * h:128 * h + 64], ckvn[:, t0:t0 + n])
                P.copy('act', km[0:64, t0:t0 + n], pk[0:64, 0:n])
                P.copy('pool', km[64:96, t0:t0 + n], kro[64:96, t0:t0 + n])
            for g0 in range(0, 18, 8):
                gn = min(8, 18 - g0)
                pvv = PS_T.get()
                for tc in range(g0, g0 + gn):
                    P.mm(pvv[:, 64 * (tc - g0):64 * (tc - g0) + 64], ckvn[:, 128 * tc:128 * tc + 128],
                         wukv[:, 128 * h + 64:128 * h + 128])
                P.copy(ev_eng(), vm[:, g0:g0 + gn, vo:vo + 64], pvv[:, 0:64 * gn].rearrange("p (a d) -> p a d", d=64))
            for (t0, n) in TBS:
                seq = list(range(18)) if t0 < SEQ else [16, 17]
                po = PS_A.get()
                def sc_(i):
                    ps = PS_T.get()
                    P.mm(ps[:, 0:n], km[0:96, 128 * i:128 * i + 128], qm[0:96, t0:t0 + n])
                    return ps
                pq = [sc_(seq[ii]) for ii in range(min(3, len(seq)))]
                for idx, i in enumerate(seq):
                    ps = pq.pop(0)
                    if idx + 3 < len(seq):
                        pq.append(sc_(seq[idx + 3]))
                    pt = nextpt()
                    P.act(pt[:, 0:n], ps[:, 0:n], AF.Exp, scale=MLA_SCALE)
                    P.mm(po[0:M, 0:n], vm[:, i, 0:M], pt[:, 0:n], start=(idx == 0), stop=(idx == len(seq) - 1))
                finalize_attn(po, h, yT[3], t0, n)


    u2tok = None
    if stage >= 5:
        AR.off = mark_y
        merged = AR.view([128, 8, T], BF16)
        wbr = AR.view([128, 2, 4, 128], BF16)
        sg = [AR.view([128, 512], F32) for _ in range(2)]
        macc = AR.view([128, 512], F32)
        mt = AR.view([128, 512], F32)
        for jd in range(8):
            wg_ = nextw()
            for nb in range(4):
                P.dma('pool', wg_[:, :, 128 * nb:128 * nb + 128],
                      w_in[:, C_GT + 1024 * nb + 128 * jd:C_GT + 1024 * nb + 128 * jd + 128].rearrange("(k p) n -> p k n", p=128))
                P.dma('pool', wbr[:, :, nb, :],
                      w_br[256 * nb:256 * nb + 256, 128 * jd:128 * jd + 128].rearrange("(k p) n -> p k n", p=128))
            for (t0, n) in TBS:
                for nb in range(4):
                    pg = PS_T.get()
                    for k in range(8):
                        P.mm(pg[:, 0:n], wg_[:, k, 128 * nb:128 * nb + 128], uT[:, k, t0:t0 + n], start=(k == 0), stop=(k == 7))
                    pp = PS_T.get()
                    for k in range(2):
                        P.mm(pp[:, 0:n], wbr[:, k, nb, :], yT[nb][:, k, t0:t0 + n], start=(k == 0), stop=(k == 1))
                    s_ = sg[nb % 2]
                    P.act(s_[:, 0:n], pg[:, 0:n], AF.Sigmoid)
                    if nb == 0:
                        P.tt('dve', macc[:, 0:n], s_[:, 0:n], pp[:, 0:n], ALU.mult)
                    elif nb < 3:
                        P.tt('dve', mt[:, 0:n], s_[:, 0:n], pp[:, 0:n], ALU.mult)
                        P.tt('dve', macc[:, 0:n], macc[:, 0:n], mt[:, 0:n], ALU.add)
                    else:
                        P.tt('dve', mt[:, 0:n], s_[:, 0:n], pp[:, 0:n], ALU.mult)
                        P.tt('dve', merged[:, jd, t0:t0 + n], macc[:, 0:n], mt[:, 0:n], ALU.add)
        AR.off = mark_u
        u2tok = AR.view([128, 18, D], BF16)
        assert AR.off <= mark_y
        AR.off = mark_y + 8 * T * 2 + 64
        wo0 = nextw()
        wo1 = nextw()
        P.dma('pool', wo0[:, :, 0:512], w_out[:, 0:512].rearrange("(k p) n -> p k n", p=128))
        P.dma('pool', wo1[:, :, 0:512], w_out[:, 512:1024].rearrange("(k p) n -> p k n", p=128))
        wrt = AR.view([128, 8, NE], F32)
        P.dma('sp', wrt[:], w_rt.rearrange("(k p) n -> p k n", p=128))
        u2b = AR.view([128, 8, 512], BF16)
        lgT = AR.view([NE, T], F32)
        mark_m = AR.off
        AR.off = 0
        hbk = [AR.view([128, 8, 512], F32)]
        zts = [AR.view([128, 8, 512], F32)]
        AR.off = mark_m
        zts.append(AR.view([128, 8, 512], F32))
        for bi, (t0, n) in enumerate(TBS):
            x = 0 if t0 < SEQ else 1
            hb_ = hbk[0]
            zt = zts[bi % 2]
            P.dma('sp', hb_[:, :, 0:n], hsrc[:, t0:t0 + n].rearrange("(k p) n -> p k n", p=128))
            for jo in range(8):
                po = PS_T.get()
                for k in range(8):
                    P.mm(po[:, 0:n], (wo0 if jo < 4 else wo1)[:, k, 128 * (jo % 4):128 * (jo % 4) + 128], merged[:, k, t0:t0 + n], start=(k == 0), stop=(k == 7))
                P.act(hb_[:, jo, 0:n], hb_[:, jo, 0:n], AF.Identity, scale=float(DN_ALPHA))
                P.stt('dve', zt[:, jo, 0:n], po[:, 0:n], modT[:, G1 + jo, x:x + 1], hb_[:, jo, 0:n], ALU.mult, ALU.add)

            def after_ln(j, ap, t0=t0, n=n, x=x):
                P.dma('sp', h1T[128 * j:128 * j + 128, t0:t0 + n], ap)
                P.act(zt[:, j, 0:n], ap, AF.Identity, bias=modT[:, SH2 + j, x:x + 1], scale=ops2[:, j, x:x + 1])
                P.copy('act', u2b[:, j, 0:n], zt[:, j, 0:n])
            ln_block(P, PS_A, zt, n, ln1s[:, 0, :], ln1s[:, 1, :], after_ln, ones_f, lntmp, ['dve', 'pool'])
            pl = PS_B.get()
            for k in range(8):
                P.mm(pl[0:NE, 0:n], wrt[:, k, :], zt[:, k, 0:n], start=(k == 0), stop=(k == 7))
            P.copy('dve', lgT[:, t0:t0 + n], pl[0:NE, 0:n])
            for q in range(n // 128):
                tc = t0 // 128 + q
                for k in range(8):
                    P.tr(pstb[:, 128 * k:128 * k + 128], u2b[:, k, 128 * q:128 * q + 128], idb[:])
                P.copy('act', u2tok[:, tc, :], pstb[:, :])

    if stage >= 6:
        AR.off = mark_y
        mask = AR.view([128, 18, NE], F32)
        maskb = AR.view([128, 18, NE], BF16)
        cs = AR.view([128, 18, NE], F32)
        csm = AR.view([128, 18, NE], F32)
        affT = AR.view([NE, T], F32)
        maskT = AR.view([NE, T], F32)
        maskTb = AR.view([NE, T], BF16)
        gmT = AR.view([NE, T], F32)
        m8 = AR.view([NE, 8], F32)
        m8c = AR.view([NE, 8], F32)
        rsm = AR.view([NE, 512], F32)
        assert AR.off <= mark_m - NE * 0 - T * 4
        P.act(affT[:], lgT[:], AF.Exp)
        for (t0, n) in TBS:
            ps = PS_T.get()
            P.mm(ps[0:NE, 0:n], ones_f[0:NE, 0:NE], affT[:, t0:t0 + n])
            P.add('dve', lambda e, ps=ps, n=n: e.reciprocal(rsm[:, 0:n], ps[0:NE, 0:n]), [ps[0:NE, 0:n]], [rsm[:, 0:n]])
            P.tt('dve', affT[:, t0:t0 + n], affT[:, t0:t0 + n], rsm[:, 0:n], ALU.mult)
        if sub >= 1:
            mark_r = AR.off
            AR.off = 0
            wk = AR.view([NE, SEQ], F32)
            wkc = AR.view([NE, LC], F32)
            csT = AR.view([NE, T], F32)
            P.copy('dve', wk[:], affT[:, 0:SEQ])
            P.copy('dve', wkc[:], affT[:, SEQ:T])
            for r in range(CAP // 8):
                P.add('dve', lambda e: e.max(m8[:], wk[:]), [wk[:]], [m8[:]])
                if r < CAP // 8 - 1:
                    P.add('dve', lambda e: e.match_replace(wk[:], m8[:], wk[:], -1.0), [wk[:], m8[:]], [wk[:]])
            for r in range(CAPC // 8):
                P.add('dve', lambda e: e.max(m8c[:], wkc[:]), [wkc[:]], [m8c[:]])
                if r < CAPC // 8 - 1:
                    P.add('dve', lambda e: e.match_replace(wkc[:], m8c[:], wkc[:], -1.0), [wkc[:], m8c[:]], [wkc[:]])
        if sub >= 2:
            P.ts('dve', maskT[:, 0:SEQ], affT[:, 0:SEQ], m8[:, 7:8], None, ALU.is_ge)
            P.ts('dve', maskT[:, SEQ:T], affT[:, SEQ:T], m8c[:, 7:8], None, ALU.is_ge)
            P.tt('dve', gmT[:], maskT[:], affT[:], ALU.mult)
            P.dma('sp', gmT_o[:, :], gmT[:])
        if sub >= 3:
            import os
            S3 = float(os.environ.get('SUB3', '9'))
            if S3 >= 0.1:
                P.copy('dve', maskTb[:], maskT[:])
            pmk = PS_T.get()
            if S3 >= 0.2:
                for tc in range(18):
                    P.mm(pmk[:, NE * tc:NE * tc + NE], maskTb[:, 128 * tc:128 * tc + 128], idb[0:NE, 0:NE])
            if S3 >= 0.3:
                P.copy('dve', mask[:], pmk[:, 0:18 * NE].rearrange("p (a b) -> p a b", b=NE))
            if S3 >= 0.4:
                P.copy('pool', maskb[:], mask[:])
            for tc in range(18 if S3 >= 2 else 0):
                base = 0 if tc < 16 else 16
                pc_ = PS_B.get()
                for c2 in range(base, tc):
                    P.mm(pc_[:, 0:NE], ones_b[:], maskb[:, c2, :], start=(c2 == base), stop=False)
                P.mm(pc_[:, 0:NE], ust[:], maskb[:, tc, :], start=(tc == base), stop=True)
                P.ts('dve', cs[:, tc, :], pc_[:, 0:NE], 0.0 if tc < 16 else float(CAP), None, ALU.add)
                if S3 < 3:
                    continue
                pt_ = PS_T.get()
                for c2 in range(base, tc):
                    P.mm(pt_[0:NE, 0:128], maskb[:, c2, :], ones_b[:], start=(c2 == base), stop=False)
                P.mm(pt_[0:NE, 0:128], maskb[:, tc, :], ust[:], start=(tc == base), stop=True)
                P.ts('dve', csT[:, 128 * tc:128 * tc + 128], pt_[0:NE, 0:128], 0.0 if tc < 16 else float(CAP), None, ALU.add)
            if S3 >= 3:
                P.dma('sp', csT_o[:, :], csT[:])
        if sub >= 4:
            P.tt('dve', csm[:], cs[:], mask[:], ALU.mult)
            P.tt('dve', csm[:], csm[:], mask[:], ALU.add)
            P.ts('dve', csm[:], csm[:], -1.0, None, ALU.add)
            AR.off = 0
            selL = [AR.view([128, 16, CAP], BF16) for _ in range(2)]
            selC = [AR.view([128, 2, CAPC], BF16) for _ in range(2)]
            xs = [AR.view([128, 8, NSLOT], BF16) for _ in range(2)]
            assert AR.off <= mark_u
            for e_ in range(NE):
                sl, sc_, x_ = selL[e_ % 2], selC[e_ % 2], xs[e_ % 2]
                P.tt('dve', sl[:], iof[:, 0:CAP].unsqueeze(1).to_broadcast([128, 16, CAP]),
                     csm[:, 0:16, e_:e_ + 1].to_broadcast([128, 16, CAP]), ALU.is_equal)
                P.tt('dve', sc_[:], iof[:, CAP:NSLOT].unsqueeze(1).to_broadcast([128, 2, CAPC]),
                     csm[:, 16:18, e_:e_ + 1].to_broadcast([128, 2, CAPC]), ALU.is_equal)
                for j in range(8):
                    px = PS_T.get()
                    for tc in range(16):
                        P.mm(px[:, 0:CAP], u2tok[:, tc, 128 * j:128 * j + 128], sl[:, tc, :], start=(tc == 0), stop=(tc == 15))
                    for tc in range(16, 18):
                        P.mm(px[:, CAP:NSLOT], u2tok[:, tc, 128 * j:128 * j + 128], sc_[:, tc - 16, :], start=(tc == 16), stop=(tc == 17))
                    P.copy(ev_eng(), x_[:, j, :], px[:, 0:NSLOT])
                P.dma('sp', xeT[e_].rearrange("(j p) s -> p j s", p=128), x_[:])
    def dump_fm(src, nchunk, row0=0):
        for c in range(nchunk):
            tmpd = lntmp['sq'][c % 2]
            for (t0, n) in TBS:
                P.copy('dve', tmpd[:, 0:n], src[:, c, t0:t0 + n])
                P.dma('sp', dbg[row0 + 128 * c:row0 + 128 * c + 128, t0:t0 + n], tmpd[:, 0:n])
    if stage < 99:
        for bi in range(4):
            if stage >= [1, 2, 3, 4][bi]:
                dump_fm(yT[bi], 2, 256 * bi)
    P.finish(outs)
    P.emit()
    return nc, P


def build_F(nl=DEPTH):
    stage = 99
    sub = 99
    nc = bass.Bass("TRN2", target_bir_lowering=False)
    P = Prog(nc, ring_sizes={'sp': 16, 'pool': 12, 'act': 4})
    I = {}

    def inp(name, shape, dt=F32):
        I[name] = P.dram_in(name, shape, dt)
        return I[name]
    hT_in = inp("hT_in", [D, T])
    cs2 = inp("cs2", [128, 8, 2])
    wmod_all = inp("wmod", [DEPTH] + [D, 6 * D])
    bmod_all = inp("bmod", [DEPTH] + [128, 48])
    w_in_all = inp("w_in", [DEPTH] + [D, WIN_COLS])
    nab_all = inp("nab", [DEPTH] + [4, 128, NSTRIP])
    nam = inp("nam", [2, 128, NSTRIP])
    pool_w_all = inp("pool_w", [DEPTH] + [4, 64, 64])
    pvec_all = inp("pvec", [DEPTH] + [128, 8])
    w_uq_all = inp("w_uq", [DEPTH] + [256, 512])
    w_ukv_all = inp("w_ukv", [DEPTH] + [128, 512])
    w_br_all = inp("w_br", [DEPTH] + [D, D])
    w_out_all = inp("w_out", [DEPTH] + [D, D])
    ln1_all = inp("ln1", [DEPTH] + [128, 2, 8])
    w_rt_all = inp("w_rt", [DEPTH] + [D, NE])
    ropeC = inp("ropeC", [32, SEQ], BF16)
    ropeS = inp("ropeS", [32, SEQ], BF16)
    dftc = inp("dftc", [128, 256], BF16)
    ident_f = inp("ident_f", [128, 128])
    ident_b = inp("ident_b", [128, 128], BF16)
    ustri = inp("ustri", [128, 128], BF16)
    iota_f = inp("iota_f", [128, NSLOT])
    pidx = inp("pidx", [128, 3])
    lval = inp("lval", [128, 16])
    jrow = inp("jrow", [128, 256], I32)
    pooledge = inp("pooledge", [128, 64])
    poolinvw = inp("poolinvw", [128, 2])
    ln2_all = inp("ln2", [DEPTH, 128, 2, 8])
    eye16rep = inp("eye16rep", [NE, NE * 128], BF16)
    wg_all = inp("wg", [DEPTH, NE, D, FF])
    wu_all = inp("wu", [DEPTH, NE, D, FF])
    wd_all = inp("wd", [DEPTH, NE, FF, D])
    out_d = P.dram_out("out", [D, SEQ], F32)
    h1T = P.dram_tmp("h1s", [D, T], F32)
    hs = P.dram_tmp("hs", [D, T], F32)
    ye = P.dram_tmp("ye_scr", [NE, NSLOT, D], BF16)
    csT_o = P.dram_tmp("csT_scr", [NE, T], F32)
    gmT_o = P.dram_tmp("gmT_scr", [NE, T], F32)
    mod_scr = P.dram_tmp("mod_scr", [DEPTH, 128, 48, 2], F32)
    tab_scr = P.dram_tmp("tab_scr", [8, 2, 128, 16 * 256], BF16)
    dbg = None
    outs = [out_d]
    S = {}

    def sb(name, shape, dt=F32):
        S[name] = P.sbuf(name, shape, dt)
        return S[name]
    ones_f = sb("ones_f", [128, 128])
    ones_b = sb("ones_b", [128, 128], BF16)
    idf = sb("idf", [128, 128])
    idb = sb("idb", [128, 128], BF16)
    ust = sb("ust", [128, 128], BF16)
    iof = sb("iof", [128, NSLOT])
    pix = sb("pix", [128, 3])
    modT = sb("modT", [128, 48, 2])
    ops1 = sb("ops1", [128, 8, 2])
    ops2 = sb("ops2", [128, 8, 2])
    pv = sb("pv", [128, 8])
    ln1s = sb("ln1s", [128, 2, 8])
    scs = sb("scs", [128, 8, 2])
    bm = sb("bm", [128, 48])
    lntmp = dict(sq=[sb("lnsq0", [128, 512]), sb("lnsq1", [128, 512])], mean=sb("lnmean", [128, 512]),
                 rstd=sb("lnrstd", [128, 512]), nmr=sb("lnnmr", [128, 512]),
                 zb=[sb("lnzb0", [128, 512], BF16), sb("lnzb1", [128, 512], BF16)],
                 sqb=[sb("lnsqb0", [128, 512], BF16), sb("lnsqb1", [128, 512], BF16)], ones_b=ones_b)
    wring = [sb("wr%d" % i, [128, 8, 512], BF16) for i in range(3)]
    wri = [0]
    PT = [sb("pt%d" % i, [128, 512], BF16) for i in range(4)]
    pti = [0]
    rden = sb("rden", [128, 512])
    rbc = sb("rbc", [128, 512])
    AR = Arena(P, "arena", 150 * 1024)
    PS_T = PsumRing(P, 4, "pst")
    PS_A = PsumRing(P, 2, "psa")
    PS_B = PsumRing(P, 1, "psbx")
    pstb = P.psum("pstb", [128, 1024], BF16)

    def nextw():
        w = wring[wri[0] % 3]
        wri[0] += 1
        return w

    def nextpt():
        t = PT[pti[0] % 4]
        pti[0] += 1
        return t
    alt = [0]

    def ev_eng():
        alt[0] += 1
        return 'dve' if alt[0] % 2 else 'act'

    def loadw(src, lo, n, kch=8):
        w = nextw()
        P.dma('pool', w[:, 0:kch, 0:n], src[:, lo:lo + n].rearrange("(k p) n -> p k n", p=128))
        return w

    P.memset('dve', ones_f[:], 1.0)
    P.memset('dve', ones_b[:], 1.0)
    P.dma('sp', idf[:], ident_f[:, :])
    P.dma('sp', idb[:], ident_b[:, :])
    P.dma('sp', ust[:], ustri[:, :])
    P.dma('sp', iof[:], iota_f[:, :])
    P.dma('sp', pix[:], pidx[:, :])
    P.dma('sp', scs[:], cs2[:, :, :])
    for l in range(nl):
        prologue = l > 0
        wmod, bmod, w_in, nab, pool_w, pvec = wmod_all[l], bmod_all[l], w_in_all[l], nab_all[l], pool_w_all[l], pvec_all[l]
        w_uq, w_ukv, w_br, w_out, ln1, w_rt = w_uq_all[l], w_ukv_all[l], w_br_all[l], w_out_all[l], ln1_all[l], w_rt_all[l]
        AR.off = 0
        P.dma('sp', pv[:], pvec[:, :])
        P.dma('sp', ln1s[:], ln1[:, :, :])
        P.dma('sp', bm[:], bmod[:, :])
        P.dma('sp', scs[:], cs2[:, :, :])
        P.act(scs[:], scs[:], AF.Silu)
        mark0 = AR.off
        wm = [AR.view([128, 8, 1024], F32) for _ in range(2)]
        modrow = AR.view([2, 6 * D], F32)
        pm = PS_B.get()
        for blk in range(6):
            w = wm[blk % 2]
            P.dma('sp', w[:], wmod[:, 1024 * blk:1024 * blk + 1024].rearrange("(k p) n -> p k n", p=128))
            for hb2 in range(2):
                prow = PS_T.get()
                for k in range(8):
                    P.mm(prow[0:2, 0:512], scs[:, k, :], w[:, k, 512 * hb2:512 * hb2 + 512], start=(k == 0), stop=(k == 7))
                P.copy('dve', modrow[:, 1024 * blk + 512 * hb2:1024 * blk + 512 * hb2 + 512], prow[0:2, 0:512])
        for j in range(48):
            P.mm(pm[:, 2 * j:2 * j + 2], modrow[:, 128 * j:128 * j + 128], idf[0:2, 0:2])
        for x in range(2):
            P.tt('dve', modT[:, :, x], pm[:, 0:96].rearrange("p (j x) -> p j x", x=2)[:, :, x], bm[:, :], ALU.add)
        P.ts('dve', ops1[:], modT[:, 8:16, :], 1.0, None, ALU.add)
        P.ts('dve', ops2[:], modT[:, 32:40, :], 1.0, None, ALU.add)
        P.dma('sp', mod_scr[l], modT[:])
        AR.off = mark0
        SH1, G1, SH2, G2 = 0, 16, 24, 40

        uT = AR.view([128, 8, T], BF16)
        mark_u = AR.off
        if prologue:
            hsrc = hs
            emit_prologue(P, dict(AR=AR, PS_T=PS_T, PS_A=PS_A, PS_B=PS_B, ones_f=ones_f, lntmp=lntmp, pix=pix, hT_in=h1T, ye=ye,
                                  csT_p=csT_o, gmT_p=gmT_o, modp=mod_scr[l - 1], ln2p=ln2_all[l - 1], eye16rep=eye16rep, hdst=hs, uT=uT, ops1=ops1, modT=modT))
        else:
            hsrc = hT_in
            hb = [AR.view([128, 8, 512], F32) for _ in range(2)]
            for bi, (t0, n) in enumerate(TBS):
                x = 0 if t0 < SEQ else 1
                b = hb[bi % 2]
                P.dma('sp', b[:, :, 0:n], hT_in[:, t0:t0 + n].rearrange("(k p) n -> p k n", p=128))
                for j in range(8):
                    P.ts('dve', uT[:, j, t0:t0 + n], b[:, j, 0:n], ops1[:, j, x:x + 1],
                         modT[:, SH1 + j, x:x + 1], ALU.mult, ALU.add)
        AR.off = mark_u
        yT = [AR.view([128, 2, T], BF16) for _ in range(4)]
        mark_y = AR.off

        def proj_fm(wv, m_lo, M, consume, po=0):
            for (t0, n) in TBS:
                ps = PS_T.get()
                for k in range(8):
                    P.mm(ps[po:po + M, 0:n], wv[:, k, m_lo:m_lo + M], uT[:, k, t0:t0 + n], start=(k == 0), stop=(k == 7))
                consume(ps, t0, n)

        def finalize_attn(po, h, dst, t0, n):
            c = h // 2
            if h % 2 == 0:
                dp, lo, hi, op_ = 64, 0, 64, 64
            else:
                dp, lo, hi, op_ = 0, 64, 128, 0
            P.add('dve', lambda e: e.reciprocal(rden[dp:dp + 1, 0:n], po[dp:dp + 1, 0:n]), [po[dp:dp + 1, 0:n]], [rden[dp:dp + 1, 0:n]])
            pbc = PS_B.get()
            P.mm(pbc[lo:hi, 0:n], ones_f[dp:dp + 1, 0:64], rden[dp:dp + 1, 0:n])
            P.copy('act', rbc[lo:hi, 0:n], pbc[lo:hi, 0:n])
            P.tt('dve', dst[lo:hi, c, t0:t0 + n], po[lo:hi, 0:n], rbc[lo:hi, 0:n], ALU.mult)

        def init_vpad(v):
            P.memset('pool', v, 0.0)

        if stage >= 1:
            AR.off = mark_y
            qaT = AR.view([128, 2, T], BF16)
            kaT = AR.view([128, 2, T], BF16)
            va2 = AR.view([128, 18, 4, 128], BF16)
            wall = AR.view([128, NSTRIP], BF16)
            wint = AR.view([128, NSTRIP], BF16)
            nbf = AR.view([128, NSTRIP], F32)
            nmk = AR.view([128, 2, NSTRIP], F32)
            P.dma('sp', nmk[:], nam.rearrange("a p n -> p a n"))
            P.memset('pool', va2[:], 0.0)
            for h in range(4):
                cc_ = 64 if h % 2 == 0 else 0
                P.memset('pool', va2[:, :, h, cc_:cc_ + 1], 1.0)
            wv = loadw(w_in, C_QA, 512)
            for ci, dst in ((0, qaT), (256, kaT)):
                for c in range(2):
                    proj_fm(wv, ci + 128 * c, 128,
                            lambda ps, t0, n, dst=dst, c=c: P.copy(ev_eng(), dst[:, c, t0:t0 + n], ps[:, 0:n]))
            wv = loadw(w_in, C_VA, 256)
            for tc in range(18):
                ps = PS_T.get()
                for k in range(8):
                    P.mm(ps[:, 0:256], uT[:, k, 128 * tc:128 * tc + 128], wv[:, k, 0:256], start=(k == 0), stop=(k == 7))
                for hp in range(2):
                    src = ps[:, 0:256].rearrange("p (h d) -> p h d", h=4)
                    if hp == 0:
                        P.copy(ev_eng(), va2[:, tc, 0::2, 0:64], src[:, 0::2, :])
                    else:
                        P.copy(ev_eng(), va2[:, tc, 1::2, 64:128], src[:, 1::2, :])
            for h in range(4):
                c, pb = h // 2, 64 * (h % 2)
                M = 65 if h % 2 == 0 else 128
                P.dma('sp', nbf[:], nab[h, :, :])
                for which, dstw in ((0, wall), (1, wint)):
                    P.tt('pool', dstw[:], nbf[:], nmk[:, which, :], ALU.add)
                    P.act(dstw[:], dstw[:], AF.Exp)
                for qb in range(4):
                    lo_i = [0, 2, 6, 10][qb]
                    hi_i = [5, 9, 13, 15][qb]
                    seq = list(range(lo_i, hi_i + 1)) + [16, 17]
                    po = PS_A.get()
                    def sc_(i):
                        ps = PS_T.get()
                        P.mm(ps[:, 0:512], kaT[pb:pb + 64, c, 128 * i:128 * i + 128], qaT[pb:pb + 64, c, 512 * qb:512 * qb + 512])
                        return ps
                    pq = [sc_(seq[ii]) for ii in range(min(3, len(seq)))]
                    for idx, i in enumerate(seq):
                        ps = pq.pop(0)
                        if idx + 3 < len(seq):
                            pq.append(sc_(seq[idx + 3]))
                        pt = nextpt()
                        P.act(pt[:], ps[:, 0:512], AF.Exp, scale=NA_SCALE)
                        if i < 16:
                            s0 = (10 - (2 * i - 8 * qb)) * 64
                            me = 'dve'
                            if qb == 0:
                                wa = wall if i <= 3 else wint
                                P.tt(me, pt[:, 0:256], pt[:, 0:256], wa[:, s0:s0 + 256], ALU.mult)
                                P.tt(me, pt[:, 256:512], pt[:, 256:512], wint[:, s0 + 256:s0 + 512], ALU.mult)
                            elif qb == 3:
                                wa = wall if i >= 12 else wint
                                P.tt(me, pt[:, 0:320], pt[:, 0:320], wint[:, s0:s0 + 320], ALU.mult)
                                P.tt(me, pt[:, 320:512], pt[:, 320:512], wa[:, s0 + 320:s0 + 512], ALU.mult)
                            else:
                                P.tt(me, pt[:], pt[:], wint[:, s0:s0 + 512], ALU.mult)
                        P.mm(po[0:M, 0:512], va2[:, i, h, 0:M], pt[:], start=(idx == 0), stop=(idx == len(seq) - 1))
                    finalize_attn(po, h, yT[0], 512 * qb, 512)
                po = PS_A.get()
                for idx, i in enumerate([16, 17]):
                    ps = PS_T.get()
                    P.mm(ps[:, 0:256], kaT[pb:pb + 64, c, 128 * i:128 * i + 128], qaT[pb:pb + 64, c, SEQ:T])
                    pt = nextpt()
                    P.act(pt[:, 0:256], ps[:, 0:256], AF.Exp, scale=NA_SCALE)
                    P.mm(po[0:M, 0:256], va2[:, i, h, 0:M], pt[:, 0:256], start=(idx == 0), stop=(idx == 1))
                finalize_attn(po, h, yT[0], SEQ, 256)


        if stage >= 2:
            AR.off = mark_y
            xp = AR.view([128, 2, XPW], F32)
            la = AR.view([128, XPW], F32)
            lb = AR.view([128, XPW], F32)
            ybf = AR.view([128, 2, T], BF16)
            pwbd = AR.view([128, 2, 128], BF16)
            pwf = AR.view([128, 2, 128], F32)
            pe = AR.view([128, 64], F32)
            piw = AR.view([128, 2], F32)
            etmp = AR.view([128, 8], F32)
            P.memset('pool', xp[:], 0.0)
            P.memset('pool', la[:], 0.0)
            P.memset('pool', lb[:], 0.0)
            P.memset('pool', pwf[:], 0.0)
            for g in range(4):
                o = 64 * (g % 2)
                P.dma('sp', pwf[o:o + 64, g // 2, o:o + 64], pool_w[g, :, :])
            P.copy('dve', pwbd[:], pwf[:])
            P.dma('sp', pe[:], pooledge[:, :])
            P.dma('sp', piw[:], poolinvw[:, :])
            wv = loadw(w_in, C_UP, 256)

            def up_consume(ps, t0, n, c):
                off = XP_L + t0 if t0 < SEQ else XP_C
                P.copy(ev_eng(), xp[:, c, off:off + n], ps[:, 0:n])
            for c in range(2):
                proj_fm(wv, 128 * c, 128, lambda ps, t0, n, c=c: up_consume(ps, t0, n, c))
            W = XPW
            for c in range(2):
                x = xp[:, c, :]
                P.tt('dve', la[:, 1:W], x[:, 1:W], x[:, 0:W - 1], ALU.add)
                P.tt('dve', lb[:, 1:W - 1], la[:, 2:W], la[:, 0:W - 2], ALU.add)
                if c == 1:
                    P.tt('dve', la[:, 2:W - 2], lb[:, 4:W], lb[:, 0:W - 4], ALU.add)
                    P.tt('dve', lb[:, 4:W - 4], la[:, 8:W], la[:, 0:W - 8], ALU.add)
                for (pl, src) in ((0, la), (64, lb)):
                    for (off, L, toff) in ((XP_L, SEQ, 0), (XP_C, LC, SEQ)):
                        P.stt('dve', ybf[pl:pl + 64, c, toff:toff + L], src[pl:pl + 64, off:off + L], piw[pl:pl + 64, c:c + 1],
                              x[pl:pl + 64, off:off + L], ALU.mult, ALU.subtract)
                    for reg, (off, toff) in enumerate(((XP_L, 0), (XP_L + SEQ - 8, SEQ - 8), (XP_C, SEQ), (XP_C + LC - 8, T - 8))):
                        ec = (c * 4 + reg) * 8
                        P.tt('dve', etmp[pl:pl + 64, :], src[pl:pl + 64, off:off + 8], pe[pl:pl + 64, ec:ec + 8], ALU.mult)
                        P.tt('dve', ybf[pl:pl + 64, c, toff:toff + 8], etmp[pl:pl + 64, :], x[pl:pl + 64, off:off + 8], ALU.subtract)
                for (t0, n) in TBS:
                    ps = PS_T.get()
                    P.mm(ps[:, 0:n], pwbd[:, c, :], ybf[:, c, t0:t0 + n])
                    P.ts('dve', yT[1][:, c, t0:t0 + n], ps[:, 0:n], pv[:, c:c + 1], None, ALU.mult)

        if stage >= 3:
            AR.off = mark_y
            ufT = AR.view([128, 2, T], BF16)
            AB = AR.view([128, 18, 2, 256], BF16)
            dfc = AR.view([128, 256], BF16)
            lv = AR.view([128, 16], F32)
            lofs = AR.view([128, 16, 8, 2], F32)
            jr = AR.view([128, 256], I32)
            tabCs = [AR.view([128, 16, 256], BF16) for _ in range(2)]
            tabSs = [AR.view([128, 16, 256], BF16) for _ in range(2)]
            tabC, tabS = tabCs[0], tabSs[0]
            ki = [AR.view([128, 256], I32) for _ in range(2)]
            P.dma('sp', dfc[:], dftc[:, :])
            P.dma('sp', lv[:], lval[:, :])
            P.dma('sp', jr[:], jrow[:, :])
            for jb in range(8):
                P.ts('dve', lofs[:, :, jb, 0], lv[:], 256.0 * jb, 512.0, ALU.mult, ALU.add)
                P.ts('dve', lofs[:, :, jb, 1], lv[:], 256.0 * jb, None, ALU.mult)
            wv = loadw(w_in, C_UF, 256)
            for c in range(2):
                proj_fm(wv, 128 * c, 128, lambda ps, t0, n, c=c: P.copy(ev_eng(), ufT[:, c, t0:t0 + n], ps[:, 0:n]))
            for tc in range(18):
                for c in range(2):
                    ps = PS_T.get()
                    P.mm(ps[:, 0:256], ufT[:, c, 128 * tc:128 * tc + 128], dfc[:, 0:256])
                    P.copy(ev_eng(), AB[:, tc, c, :], ps[:, 0:256])
            kc_ = [0]

            def gen_tab(dst, a, ofs_ap, ofs_imm, mask, scale):
                k = ki[kc_[0] % 2]
                kc_[0] += 1
                if ofs_ap is not None:
                    P.ts('dve', k[:], jr[:], lv[:, a:a + 1], ofs_ap, ALU.mult, ALU.add)
                else:
                    P.ts('dve', k[:], jr[:], lv[:, a:a + 1], float(ofs_imm), ALU.mult, ALU.add)
                P.ts('dve', k[:], k[:], mask, None, ALU.bitwise_and)
                P.act(dst, k[:], AF.Sin, bias=mpi[:, 0:1], scale=scale)
            mpi = AR.view([128, 1], F32)
            P.memset('dve', mpi[:], -float(np.pi))
            for jb in range(8):
                tabC, tabS = tabCs[jb % 2], tabSs[jb % 2]
                if l == 0:
                    for a in range(16):
                        gen_tab(tabC[:, a, :], a, lofs[:, a, jb, 0:1], None, 2047, 2.0 * np.pi / 2048.0)
                        gen_tab(tabS[:, a, :], a, lofs[:, a, jb, 1:2], None, 2047, 2.0 * np.pi / 2048.0)
                    P.dma('sp', tab_scr[jb, 0].rearrange("p (a b) -> p a b", b=256), tabC[:])
                    P.dma('sp', tab_scr[jb, 1].rearrange("p (a b) -> p a b", b=256), tabS[:])
                else:
                    P.dma('sp', tabC[:], tab_scr[jb, 0].rearrange("p (a b) -> p a b", b=256))
                    P.dma('sp', tabS[:], tab_scr[jb, 1].rearrange("p (a b) -> p a b", b=256))
                for c in range(2):
                    po = PS_A.get()
                    for a in range(16):
                        P.mm(po[:, 0:256], AB[:, a, c, 0:128], tabC[:, a, :], start=(a == 0), stop=False)
                        P.mm(po[:, 0:256], AB[:, a, c, 128:256], tabS[:, a, :], start=False, stop=(a == 15))
                    P.ts('dve', yT[2][:, c, 256 * jb:256 * jb + 256], po[:, 0:256], float((SEQ * 64.0) ** -0.5), None, ALU.mult)
            for a in range(2):
                gen_tab(tabC[:, a, :], a, None, 64, 255, 2.0 * np.pi / 256.0)
                gen_tab(tabS[:, a, :], a, None, 0, 255, 2.0 * np.pi / 256.0)
            for c in range(2):
                po = PS_A.get()
                for a in range(2):
                    P.mm(po[:, 0:256], AB[:, 16 + a, c, 0:128], tabC[:, a, :], start=(a == 0), stop=False)
                    P.mm(po[:, 0:256], AB[:, 16 + a, c, 128:256], tabS[:, a, :], start=False, stop=(a == 1))
                P.ts('dve', yT[2][:, c, SEQ:T], po[:, 0:256], float((LC * 64.0) ** -0.5), None, ALU.mult)


        if stage >= 4:
            AR.off = mark_y
            cqn = AR.view([128, 2, T], BF16)
            ckvn = AR.view([128, T], BF16)
            kro = AR.view([128, T], BF16)
            rC = AR.view([128, SEQ], BF16)
            rS = AR.view([128, SEQ], BF16)
            qm = AR.view([128, T], BF16)
            km = AR.view([128, T], BF16)
            vm = AR.view([128, 18, 128], BF16)
            wuq = AR.view([128, 2, 512], BF16)
            wukv = AR.view([128, 512], BF16)
            sqb = [AR.view([128, 512], BF16) for _ in range(2)]
            rst = AR.view([128, 512], F32)
            rt1 = AR.view([128, 512], F32)
            rt2 = AR.view([128, 512], F32)
            P.dma('sp', rC[64:96, :], ropeC[:, :])
            P.dma('sp', rS[64:96, :], ropeS[:, :])
            P.dma('pool', wuq[:], w_uq.rearrange("(k p) n -> p k n", p=128))
            P.dma('pool', wukv[:], w_ukv[:, :])
            wv = loadw(w_in, C_CQ, 448)

            def rope_apply(dst, psa, psb, t0, n):
                if t0 < SEQ:
                    P.tt('dve', rt1[64:96, 0:n], psa[64:96, 0:n], rC[64:96, t0:t0 + n], ALU.mult)
                    P.tt('dve', rt2[64:96, 0:n], psb[64:96, 0:n], rS[64:96, t0:t0 + n], ALU.mult)
                    P.tt('dve', dst[64:96, t0:t0 + n], rt1[64:96, 0:n], rt2[64:96, 0:n], ALU.add)
                else:
                    P.copy('act', dst[64:96, t0:t0 + n], psa[64:96, 0:n])
            for (t0, n) in TBS:
                pc = [PS_T.get(), PS_T.get()]
                pss = PS_B.get()
                for c in range(2):
                    for k in range(8):
                        P.mm(pc[c][:, 0:n], wv[:, k, 128 * c:128 * c + 128], uT[:, k, t0:t0 + n], start=(k == 0), stop=(k == 7))
                    P.act(sqb[c][:, 0:n], pc[c][:, 0:n], AF.Square)
                    P.mm(pss[:, 0:n], ones_b[:], sqb[c][:, 0:n], start=(c == 0), stop=(c == 1))
                P.ts('dve', rst[:, 0:n], pss[:, 0:n], 1.0 / 256.0, LN_EPS, ALU.mult, ALU.add)
                P.act(rst[:, 0:n], rst[:, 0:n], AF.Sqrt)
                P.add('dve', lambda e, n=n: e.reciprocal(rst[:, 0:n], rst[:, 0:n]), [rst[:, 0:n]], [rst[:, 0:n]])
                for c in range(2):
                    P.stt('dve', cqn[:, c, t0:t0 + n], pc[c][:, 0:n], pv[:, 2 + c:3 + c], rst[:, 0:n], ALU.mult, ALU.mult)
                pk = PS_T.get()
                pss = PS_B.get()
                for k in range(8):
                    P.mm(pk[:, 0:n], wv[:, k, 256:384], uT[:, k, t0:t0 + n], start=(k == 0), stop=(k == 7))
                P.act(sqb[0][:, 0:n], pk[:, 0:n], AF.Square)
                P.mm(pss[:, 0:n], ones_b[:], sqb[0][:, 0:n])
                P.ts('dve', rst[:, 0:n], pss[:, 0:n], 1.0 / 128.0, LN_EPS, ALU.mult, ALU.add)
                P.act(rst[:, 0:n], rst[:, 0:n], AF.Sqrt)
                P.add('dve', lambda e, n=n: e.reciprocal(rst[:, 0:n], rst[:, 0:n]), [rst[:, 0:n]], [rst[:, 0:n]])
                P.stt('dve', ckvn[:, t0:t0 + n], pk[:, 0:n], pv[:, 4:5], rst[:, 0:n], ALU.mult, ALU.mult)
                pa = PS_T.get()
                pb_ = PS_T.get()
                for k in range(8):
                    P.mm(pa[64:96, 0:n], wv[:, k, 384:416], uT[:, k, t0:t0 + n], start=(k == 0), stop=(k == 7))
                for k in range(8):
                    P.mm(pb_[64:96, 0:n], wv[:, k, 416:448], uT[:, k, t0:t0 + n], start=(k == 0), stop=(k == 7))
                rope_apply(kro, pa, pb_, t0, n)
            for h in range(4):
                M = 65 if h % 2 == 0 else 128
                vo = 0 if h % 2 == 0 else 64
                P.memset('pool', vm[:], 0.0)
                oc = 64 if h % 2 == 0 else 0
                P.memset('pool', vm[:, :, oc:oc + 1], 1.0)
                for (t0, n) in TBS:
                    pa = PS_T.get()
                    pb_ = PS_T.get()
                    for k in range(2):
                        P.mm(pa[0:96, 0:n], wuq[:, k, 128 * h:128 * h + 96], cqn[:, k, t0:t0 + n], start=(k == 0), stop=(k == 1))
                    for k in range(2):
                        P.mm(pb_[64:96, 0:n], wuq[:, k, 128 * h + 96:128 * h + 128], cqn[:, k, t0:t0 + n], start=(k == 0), stop=(k == 1))
                    P.copy('act', qm[0:64, t0:t0 + n], pa[0:64, 0:n])
                    rope_apply(qm, pa, pb_, t0, n)
                    pk = PS_T.get()
                    P.mm(pk[0:64, 0:n], wukv[:, 128 * h:128 * h + 64], ckvn[:, t0:t0 + n])
                    P.copy('act', km[0:64, t0:t0 + n], pk[0:64, 0:n])
                    P.copy('pool', km[64:96, t0:t0 + n], kro[64:96, t0:t0 + n])
                for g0 in range(0, 18, 8):
                    gn = min(8, 18 - g0)
                    pvv = PS_T.get()
                    for tc in range(g0, g0 + gn):
                        P.mm(pvv[:, 64 * (tc - g0):64 * (tc - g0) + 64], ckvn[:, 128 * tc:128 * tc + 128],
                             wukv[:, 128 * h + 64:128 * h + 128])
                    P.copy(ev_eng(), vm[:, g0:g0 + gn, vo:vo + 64], pvv[:, 0:64 * gn].rearrange("p (a d) -> p a d", d=64))
                for (t0, n) in TBS:
                    seq = list(range(18)) if t0 < SEQ else [16, 17]
                    po = PS_A.get()
                    def sc_(i):
                        ps = PS_T.get()
                        P.mm(ps[:, 0:n], km[0:96, 128 * i:128 * i + 128], qm[0:96, t0:t0 + n])
                        return ps
                    pq = [sc_(seq[ii]) for ii in range(min(3, len(seq)))]
                    for idx, i in enumerate(seq):
                        ps = pq.pop(0)
                        if idx + 3 < len(seq):
                            pq.append(sc_(seq[idx + 3]))
                        pt = nextpt()
                        P.act(pt[:, 0:n], ps[:, 0:n], AF.Exp, scale=MLA_SCALE)
                        P.mm(po[0:M, 0:n], vm[:, i, 0:M], pt[:, 0:n], start=(idx == 0), stop=(idx == len(seq) - 1))
                    finalize_attn(po, h, yT[3], t0, n)


        u2tok = None
        if stage >= 5:
            AR.off = mark_y
            merged = AR.view([128, 8, T], BF16)
            wbr = AR.view([128, 2, 4, 128], BF16)
            sg = [AR.view([128, 512], F32) for _ in range(2)]
            macc = AR.view([128, 512], F32)
            mt = AR.view([128, 512], F32)
            for jd in range(8):
                wg_ = nextw()
                for nb in range(4):
                    P.dma('pool', wg_[:, :, 128 * nb:128 * nb + 128],
                          w_in[:, C_GT + 1024 * nb + 128 * jd:C_GT + 1024 * nb + 128 * jd + 128].rearrange("(k p) n -> p k n", p=128))
                    P.dma('pool', wbr[:, :, nb, :],
                          w_br[256 * nb:256 * nb + 256, 128 * jd:128 * jd + 128].rearrange("(k p) n -> p k n", p=128))
                for (t0, n) in TBS:
                    for nb in range(4):
                        pg = PS_T.get()
                        for k in range(8):
                            P.mm(pg[:, 0:n], wg_[:, k, 128 * nb:128 * nb + 128], uT[:, k, t0:t0 + n], start=(k == 0), stop=(k == 7))
                        pp = PS_T.get()
                        for k in range(2):
                            P.mm(pp[:, 0:n], wbr[:, k, nb, :], yT[nb][:, k, t0:t0 + n], start=(k == 0), stop=(k == 1))
                        s_ = sg[nb % 2]
                        P.act(s_[:, 0:n], pg[:, 0:n], AF.Sigmoid)
                        if nb == 0:
                            P.tt('dve', macc[:, 0:n], s_[:, 0:n], pp[:, 0:n], ALU.mult)
                        elif nb < 3:
                            P.tt('dve', mt[:, 0:n], s_[:, 0:n], pp[:, 0:n], ALU.mult)
                            P.tt('dve', macc[:, 0:n], macc[:, 0:n], mt[:, 0:n], ALU.add)
                        else:
                            P.tt('dve', mt[:, 0:n], s_[:, 0:n], pp[:, 0:n], ALU.mult)
                            P.tt('dve', merged[:, jd, t0:t0 + n], macc[:, 0:n], mt[:, 0:n], ALU.add)
            AR.off = mark_u
            u2tok = AR.view([128, 18, D], BF16)
            assert AR.off <= mark_y
            AR.off = mark_y + 8 * T * 2 + 64
            wo0 = nextw()
            wo1 = nextw()
            P.dma('pool', wo0[:, :, 0:512], w_out[:, 0:512].rearrange("(k p) n -> p k n", p=128))
            P.dma('pool', wo1[:, :, 0:512], w_out[:, 512:1024].rearrange("(k p) n -> p k n", p=128))
            wrt = AR.view([128, 8, NE], F32)
            P.dma('sp', wrt[:], w_rt.rearrange("(k p) n -> p k n", p=128))
            u2b = AR.view([128, 8, 512], BF16)
            lgT = AR.view([NE, T], F32)
            mark_m = AR.off
            AR.off = 0
            hbk = [AR.view([128, 8, 512], F32)]
            zts = [AR.view([128, 8, 512], F32)]
            AR.off = mark_m
            zts.append(AR.view([128, 8, 512], F32))
            for bi, (t0, n) in enumerate(TBS):
                x = 0 if t0 < SEQ else 1
                hb_ = hbk[0]
                zt = zts[bi % 2]
                P.dma('sp', hb_[:, :, 0:n], hsrc[:, t0:t0 + n].rearrange("(k p) n -> p k n", p=128))
                for jo in range(8):
                    po = PS_T.get()
                    for k in range(8):
                        P.mm(po[:, 0:n], (wo0 if jo < 4 else wo1)[:, k, 128 * (jo % 4):128 * (jo % 4) + 128], merged[:, k, t0:t0 + n], start=(k == 0), stop=(k == 7))
                    P.act(hb_[:, jo, 0:n], hb_[:, jo, 0:n], AF.Identity, scale=float(DN_ALPHA))
                    P.stt('dve', zt[:, jo, 0:n], po[:, 0:n], modT[:, G1 + jo, x:x + 1], hb_[:, jo, 0:n], ALU.mult, ALU.add)

                def after_ln(j, ap, t0=t0, n=n, x=x):
                    P.dma('sp', h1T[128 * j:128 * j + 128, t0:t0 + n], ap)
                    P.act(zt[:, j, 0:n], ap, AF.Identity, bias=modT[:, SH2 + j, x:x + 1], scale=ops2[:, j, x:x + 1])
                    P.copy('act', u2b[:, j, 0:n], zt[:, j, 0:n])
                ln_block(P, PS_A, zt, n, ln1s[:, 0, :], ln1s[:, 1, :], after_ln, ones_f, lntmp, ['dve', 'pool'])
                pl = PS_B.get()
                for k in range(8):
                    P.mm(pl[0:NE, 0:n], wrt[:, k, :], zt[:, k, 0:n], start=(k == 0), stop=(k == 7))
                P.copy('dve', lgT[:, t0:t0 + n], pl[0:NE, 0:n])
                for q in range(n // 128):
                    tc = t0 // 128 + q
                    for k in range(8):
                        P.tr(pstb[:, 128 * k:128 * k + 128], u2b[:, k, 128 * q:128 * q + 128], idb[:])
                    P.copy('act', u2tok[:, tc, :], pstb[:, :])

        if stage >= 6:
            AR.off = mark_y
            mask = AR.view([128, 18, NE], F32)
            maskb = AR.view([128, 18, NE], BF16)
            cs = AR.view([128, 18, NE], F32)
            csm = AR.view([128, 18, NE], F32)
            ffn_mark = AR.off
            affT = AR.view([NE, T], F32)
            maskT = AR.view([NE, T], F32)
            maskTb = AR.view([NE, T], BF16)
            gmT = AR.view([NE, T], F32)
            m8 = AR.view([NE, 8], F32)
            m8c = AR.view([NE, 8], F32)
            rsm = AR.view([NE, 512], F32)
            assert AR.off <= mark_m - NE * 0 - T * 4
            P.act(affT[:], lgT[:], AF.Exp)
            for (t0, n) in TBS:
                ps = PS_T.get()
                P.mm(ps[0:NE, 0:n], ones_f[0:NE, 0:NE], affT[:, t0:t0 + n])
                P.add('dve', lambda e, ps=ps, n=n: e.reciprocal(rsm[:, 0:n], ps[0:NE, 0:n]), [ps[0:NE, 0:n]], [rsm[:, 0:n]])
                P.tt('dve', affT[:, t0:t0 + n], affT[:, t0:t0 + n], rsm[:, 0:n], ALU.mult)
            if sub >= 1:
                mark_r = AR.off
                AR.off = 0
                wk = AR.view([NE, SEQ], F32)
                wkc = AR.view([NE, LC], F32)
                csT = AR.view([NE, T], F32)
                P.copy('dve', wk[:], affT[:, 0:SEQ])
                P.copy('dve', wkc[:], affT[:, SEQ:T])
                for r in range(CAP // 8):
                    P.add('dve', lambda e: e.max(m8[:], wk[:]), [wk[:]], [m8[:]])
                    if r < CAP // 8 - 1:
                        P.add('dve', lambda e: e.match_replace(wk[:], m8[:], wk[:], -1.0), [wk[:], m8[:]], [wk[:]])
                for r in range(CAPC // 8):
                    P.add('dve', lambda e: e.max(m8c[:], wkc[:]), [wkc[:]], [m8c[:]])
                    if r < CAPC // 8 - 1:
                        P.add('dve', lambda e: e.match_replace(wkc[:], m8c[:], wkc[:], -1.0), [wkc[:], m8c[:]], [wkc[:]])
            if sub >= 2:
                P.ts('dve', maskT[:, 0:SEQ], affT[:, 0:SEQ], m8[:, 7:8], None, ALU.is_ge)
                P.ts('dve', maskT[:, SEQ:T], affT[:, SEQ:T], m8c[:, 7:8], None, ALU.is_ge)
                P.tt('dve', gmT[:], maskT[:], affT[:], ALU.mult)
                P.dma('sp', gmT_o[:, :], gmT[:])
            if sub >= 3:
                import os
                S3 = float(os.environ.get('SUB3', '9'))
                if S3 >= 0.1:
                    P.copy('dve', maskTb[:], maskT[:])
                pmk = PS_T.get()
                if S3 >= 0.2:
                    for tc in range(18):
                        P.mm(pmk[:, NE * tc:NE * tc + NE], maskTb[:, 128 * tc:128 * tc + 128], idb[0:NE, 0:NE])
                if S3 >= 0.3:
                    P.copy('dve', mask[:], pmk[:, 0:18 * NE].rearrange("p (a b) -> p a b", b=NE))
                if S3 >= 0.4:
                    P.copy('pool', maskb[:], mask[:])
                for tc in range(18 if S3 >= 2 else 0):
                    base = 0 if tc < 16 else 16
                    pc_ = PS_B.get()
                    for c2 in range(base, tc):
                        P.mm(pc_[:, 0:NE], ones_b[:], maskb[:, c2, :], start=(c2 == base), stop=False)
                    P.mm(pc_[:, 0:NE], ust[:], maskb[:, tc, :], start=(tc == base), stop=True)
                    P.ts('dve', cs[:, tc, :], pc_[:, 0:NE], 0.0 if tc < 16 else float(CAP), None, ALU.add)
                    if S3 < 3:
                        continue
                    pt_ = PS_T.get()
                    for c2 in range(base, tc):
                        P.mm(pt_[0:NE, 0:128], maskb[:, c2, :], ones_b[:], start=(c2 == base), stop=False)
                    P.mm(pt_[0:NE, 0:128], maskb[:, tc, :], ust[:], start=(tc == base), stop=True)
                    P.ts('dve', csT[:, 128 * tc:128 * tc + 128], pt_[0:NE, 0:128], 0.0 if tc < 16 else float(CAP), None, ALU.add)
                if S3 >= 3:
                    P.dma('sp', csT_o[:, :], csT[:])
            if sub >= 4:
                AR.off = ffn_mark
                wdb = AR.view([128, 16, D], BF16)
                actT = AR.view([128, 16, NSLOT], BF16)
                silb = [AR.view([128, NSLOT], F32) for _ in range(2)]
                yob = [AR.view([128, D], BF16) for _ in range(2)]
                wub = [wring[2], AR.view([128, 8, 512], BF16)]
                wgb = [wring[0], wring[1]]
                P.tt('dve', csm[:], cs[:], mask[:], ALU.mult)
                P.tt('dve', csm[:], csm[:], mask[:], ALU.add)
                P.ts('dve', csm[:], csm[:], -1.0, None, ALU.add)
                AR.off = 0
                selL = [AR.view([128, 16, CAP], BF16) for _ in range(2)]
                selC = [AR.view([128, 2, CAPC], BF16) for _ in range(2)]
                xs = [AR.view([128, 8, NSLOT], BF16) for _ in range(2)]
                assert AR.off <= mark_u
                for e_ in range(NE):
                    sl, sc_, x_ = selL[e_ % 2], selC[e_ % 2], xs[e_ % 2]
                    P.tt('dve', sl[:], iof[:, 0:CAP].unsqueeze(1).to_broadcast([128, 16, CAP]),
                         csm[:, 0:16, e_:e_ + 1].to_broadcast([128, 16, CAP]), ALU.is_equal)
                    P.tt('dve', sc_[:], iof[:, CAP:NSLOT].unsqueeze(1).to_broadcast([128, 2, CAPC]),
                         csm[:, 16:18, e_:e_ + 1].to_broadcast([128, 2, CAPC]), ALU.is_equal)
                    for j in range(8):
                        px = PS_T.get()
                        for tc in range(16):
                            P.mm(px[:, 0:CAP], u2tok[:, tc, 128 * j:128 * j + 128], sl[:, tc, :], start=(tc == 0), stop=(tc == 15))
                        for tc in range(16, 18):
                            P.mm(px[:, CAP:NSLOT], u2tok[:, tc, 128 * j:128 * j + 128], sc_[:, tc - 16, :], start=(tc == 16), stop=(tc == 17))
                        P.copy(ev_eng(), x_[:, j, :], px[:, 0:NSLOT])
                    for pc in range(4):
                        wgp = wring[(2 * pc) % 3] if False else wgb[pc % 2]
                        wup = wub[pc % 2]
                        P.dma('pool', wgp[:], wg_all[l, e_, :, 512 * pc:512 * pc + 512].rearrange("(k p) n -> p k n", p=128))
                        P.dma('pool', wup[:], wu_all[l, e_, :, 512 * pc:512 * pc + 512].rearrange("(k p) n -> p k n", p=128))
                        for f in range(4):
                            pa = PS_T.get()
                            pu = PS_T.get()
                            for k in range(8):
                                P.mm(pa[:, 0:NSLOT], wgp[:, k, 128 * f:128 * f + 128], x_[:, k, :], start=(k == 0), stop=(k == 7))
                            for k in range(8):
                                P.mm(pu[:, 0:NSLOT], wup[:, k, 128 * f:128 * f + 128], x_[:, k, :], start=(k == 0), stop=(k == 7))
                            s_ = silb[f % 2]
                            P.act(s_[:], pa[:, 0:NSLOT], AF.Silu)
                            P.tt('dve', actT[:, 4 * pc + f, :], s_[:], pu[:, 0:NSLOT], ALU.mult)
                    for pc in range(4):
                        P.dma('pool', wdb[:, 4 * pc:4 * pc + 4, :], wd_all[l, e_, 512 * pc:512 * pc + 512, :].rearrange("(k p) n -> p k n", p=128))
                    for qi, (s0, sn) in enumerate(((0, 128), (128, 128), (256, 32))):
                        y_ = yob[qi % 2]
                        for hf in range(2):
                            po = PS_A.get()
                            for f in range(16):
                                P.mm(po[0:sn, 0:512], actT[:, f, s0:s0 + sn], wdb[:, f, 512 * hf:512 * hf + 512], start=(f == 0), stop=(f == 15))
                            P.copy('act' if hf == 0 else 'dve', y_[0:sn, 512 * hf:512 * hf + 512], po[0:sn, 0:512])
                        P.dma('sp', ye[e_, s0:s0 + sn, :], y_[0:sn, :])
    AR.off = 0
    emit_prologue(P, dict(AR=AR, PS_T=PS_T, PS_A=PS_A, PS_B=PS_B, ones_f=ones_f, lntmp=lntmp, pix=pix, hT_in=h1T, ye=ye,
                          csT_p=csT_o, gmT_p=gmT_o, modp=mod_scr[nl - 1], ln2p=ln2_all[nl - 1], eye16rep=eye16rep, hdst=out_d), final=True)
    P.finish(outs)
    P.emit()
    return nc, P


def host_inputs_F(inp, b, consts):
    perm_a = np.concatenate([np.arange(0, 32, 2), np.arange(1, 32, 2)])
    perm_b = np.concatenate([np.arange(1, 32, 2), np.arange(0, 32, 2)])
    m = {}
    w_in = inp['w_in']
    kr = w_in[:, :, 1664:1696]
    m['w_in'] = np.ascontiguousarray(np.concatenate([w_in[:, :, :1664], kr[:, :, perm_a], kr[:, :, perm_b], w_in[:, :, 1696:]], axis=2))
    m['cs2'] = np.ascontiguousarray(np.stack([vec_pj(inp['c'][b]), vec_pj(inp['c_ctx'])], axis=-1))
    m['wmod'] = inp['w_mod']
    m['bmod'] = np.ascontiguousarray(np.stack([vec_pj(inp['b_mod'][l]) for l in range(DEPTH)]))
    m['nab'] = np.ascontiguousarray(inp['na_rpb'][:, :, consts['_na_ri'], consts['_na_ci']])
    m['pool_w'] = inp['pool_w']
    pvv = np.zeros((DEPTH, 128, 8), np.float32)
    for l in range(DEPTH):
        pvv[l, :, 0:2] = vec_pj(inp['pool_scale'][l]); pvv[l, :, 2:4] = vec_pj(inp['mla_q_norm'][l]); pvv[l, :, 4:5] = vec_pj(inp['mla_kv_norm'][l])
    m['pvec'] = pvv
    wuq = inp['mla_w_uq'].reshape(DEPTH, 256, 4, 96)
    m['w_uq'] = np.ascontiguousarray(np.concatenate([wuq[..., :64], wuq[..., 64:][..., perm_a], wuq[..., 64:][..., perm_b]], axis=-1).reshape(DEPTH, 256, 512))
    m['w_ukv'] = inp['mla_w_ukv']
    m['w_br'] = np.ascontiguousarray(inp['w_branch'].reshape(DEPTH, 1024, 1024))
    m['w_out'] = inp['w_out']
    m['ln1'] = np.ascontiguousarray(np.stack([np.stack([vec_pj(inp['ln1_g'][l]), vec_pj(inp['ln1_b'][l])], axis=1) for l in range(DEPTH)]))
    m['ln2'] = np.ascontiguousarray(np.stack([np.stack([vec_pj(inp['ln2_g'][l]), vec_pj(inp['ln2_b'][l])], axis=1) for l in range(DEPTH)]))
    m['w_rt'] = inp['w_router']
    m['wg'] = inp['w_gate']; m['wu'] = inp['w_up']; m['wd'] = inp['w_down']
    m['eye16rep'] = np.kron(np.eye(NE), np.ones((1, 128))).astype(ml_dtypes.bfloat16)
    for k in ('nam', 'ropeC', 'ropeS', 'dftc', 'ident_f', 'ident_b', 'ustri', 'iota_f', 'pidx', 'lval', 'jrow', 'pooledge', 'poolinvw'):
        m[k] = consts[k]
    m['hT_in'] = np.ascontiguousarray(np.concatenate([inp['x'][b].T, inp['ctx'][b].T], axis=1))
    return m


_PROGS = {}


def kernel(**inp):
    inp = {k: np.asarray(v) for k, v in inp.items()}
    consts = host_constants()
    NCORE = 8
    if 'F' not in _PROGS:
        _PROGS['F'] = build_F(DEPTH)[0]
    maps = []
    shared = None
    for b in range(NCORE):
        if shared is None:
            shared = host_inputs_F(inp, b, consts)
            m = shared
        else:
            m = dict(shared)
            m['cs2'] = np.ascontiguousarray(np.stack([vec_pj(inp['c'][b]), vec_pj(inp['c_ctx'])], axis=-1))
            m['hT_in'] = np.ascontiguousarray(np.concatenate([inp['x'][b].T, inp['ctx'][b].T], axis=1))
        maps.append(m)
    res = run_bass_kernel_spmd(_PROGS['F'], maps, core_ids=list(range(NCORE))).results
    out = np.stack([np.ascontiguousarray(res[b]['out'].T) for b in range(NCORE)], axis=0)
    return out.astype(np.float32)
```

```python
import contextlib
import numpy as np
import concourse.bass as bass
import concourse.mybir as mybir
from concourse.bass_utils import run_bass_kernel_spmd

F32 = mybir.dt.float32
F32R = mybir.dt.float32r
BF16 = mybir.dt.bfloat16
I32 = mybir.dt.int32
U32 = mybir.dt.uint32
AF = mybir.ActivationFunctionType
ALU = mybir.AluOpType
AX = mybir.AxisListType

_DSZ = {F32: 4, F32R: 4, BF16: 2, I32: 4, U32: 4, mybir.dt.float16: 2,
        mybir.dt.uint16: 2, mybir.dt.int16: 2, mybir.dt.uint8: 1, mybir.dt.int8: 1}


def _region(ap):
    esz = _DSZ[ap.dtype]
    steps = ap.ap
    off = int(ap.offset)
    sp = str(ap.space)
    if 'SB' in sp or 'PSUM' in sp:
        pstep = steps[0][0]
        if pstep == 0:
            pstep = 1 << 40
        plo = off // pstep
        phi = plo + steps[0][1]
        flo = off % pstep
        ext = 0
        for st, cnt in steps[1:]:
            ext += abs(st) * (cnt - 1)
        return (ap.name, plo, phi, flo * esz, (flo + ext + 1) * esz)
    ext = 0
    for st, cnt in steps:
        ext += abs(st) * (cnt - 1)
    return (ap.name, 0, 1, off * esz, (off + ext + 1) * esz)


def _ovl(a, b):
    return a[1] < b[2] and b[1] < a[2] and a[3] < b[4] and b[3] < a[4]


def _cov(a, b):
    return a[1] <= b[1] and a[2] >= b[2] and a[3] <= b[3] and a[4] >= b[4]


class Prog:
    CENG = ('pe', 'act', 'dve', 'pool')
    ENGS = ('pe', 'act', 'dve', 'pool', 'sp')

    def __init__(self, nc, ring_sizes=None):
        self.nc = nc
        self.es = contextlib.ExitStack()
        self.ops = {e: [] for e in self.ENGS}
        self.writers = {}
        self.readers = {}
        self.known = {e: {} for e in self.ENGS}
        self.ring = ring_sizes or {'sp': 24, 'pool': 12, 'act': 8}
        self.ndma = {k: 0 for k in self.ring}
        self.untracked = set()
        self.nops = 0
        self.order = []
        self.use_block = True

    def sbuf(self, name, shape, dtype):
        return self.es.enter_context(self.nc.sbuf_tensor(name, list(shape), dtype))

    def psum(self, name, shape, dtype):
        return self.es.enter_context(self.nc.psum_tensor(name, list(shape), dtype))

    def dram_in(self, name, shape, dtype):
        t = self.nc.dram_tensor(name, list(shape), dtype, kind="ExternalInput")
        self.untracked.add(name)
        return t.ap()

    def dram_out(self, name, shape, dtype):
        return self.nc.dram_tensor(name, list(shape), dtype, kind="ExternalOutput").ap()

    def dram_tmp(self, name, shape, dtype):
        return self.nc.dram_tensor(name, list(shape), dtype, kind="Internal").ap()

    def add(self, eng, fn, reads, writes, dma=False):
        self.nops += 1
        deps = {}

        def need(ev):
            k, v = ev
            if deps.get(k, -1) < v:
                deps[k] = v

        rr = [_region(a) for a in reads if a is not None and a.name not in self.untracked]
        ww = [_region(a) for a in writes]
        for r in rr:
            for (w, ev) in self.writers.get(r[0], ()):
                if _ovl(r, w):
                    need(ev)
        for r in ww:
            for (w, ev) in self.writers.get(r[0], ()):
                if _ovl(r, w):
                    need(ev)
            for (w, ev) in self.readers.get(r[0], ()):
                if _ovl(r, w):
                    need(ev)
        idx = len(self.ops[eng])
        if dma:
            ns = self.ring[eng]
            k = self.ndma[eng]
            self.ndma[eng] += 1
            slot = k % ns
            val = 16 * (k // ns + 1)
            if val > 16:
                need((('d', eng, slot), val - 16))
            event = (('d', eng, slot), val)
        else:
            event = (('c', eng), idx)
        waits = []
        kn = self.known[eng]
        for k, v in deps.items():
            if k == ('c', 'pe') and eng == 'pe':
                continue
            if kn.get(k, -1) >= v:
                continue
            kn[k] = v
            waits.append((k, v))
            if k[0] == 'c':
                dop = self.ops[k[1]][v]
                dop['marked'] = True
                for k2, v2 in dop['kn'].items():
                    if kn.get(k2, -1) < v2:
                        kn[k2] = v2
        op = dict(fn=fn, waits=waits, event=event, marked=False, dma=dma, eng=eng,
                  kn={k: v for k, v in kn.items() if k[0] == 'c'})
        self.ops[eng].append(op)
        self.order.append(op)
        for r in ww:
            lst = self.writers.setdefault(r[0], [])
            lst[:] = [(w, ev) for (w, ev) in lst if not _cov(r, w)]
            lst.append((r, event))
            rl = self.readers.get(r[0])
            if rl:
                rl[:] = [(w, ev) for (w, ev) in rl if not _cov(r, w)]
        for r in rr:
            lst = self.readers.setdefault(r[0], [])
            lst[:] = [(w, ev) for (w, ev) in lst if not (ev[0] == event[0] and _cov(r, w))]
            lst.append((r, event))
        return op

    def mm(self, out, lhsT, rhs, start=True, stop=True, **kw):
        self.add('pe', lambda e: e.matmul(out, lhsT, rhs, start=start, stop=stop, **kw),
                 [lhsT, rhs], [out])

    def tr(self, out, in_, ident):
        self.add('pe', lambda e: e.transpose(out, in_, ident), [in_, ident], [out])

    def act(self, out, in_, func, bias=None, scale=None, accum_out=None, eng='act'):
        kw = {}
        rd = [in_]
        wr = [out]
        if bias is not None:
            kw['bias'] = bias
            if not isinstance(bias, (int, float)):
                rd.append(bias)
        if scale is not None:
            kw['scale'] = scale
            if not isinstance(scale, (int, float)):
                rd.append(scale)
        if accum_out is not None:
            kw['accum_out'] = accum_out
            wr.append(accum_out)
        self.add('act', lambda e: e.activation(out, in_, func, **kw), rd, wr)

    def tt(self, eng, out, in0, in1, op):
        self.add(eng, lambda e: e.tensor_tensor(out, in0, in1, op), [in0, in1], [out])

    def ts(self, eng, out, in0, s1, s2, op0, op1=None, accum_out=None):
        rd = [in0]
        if not isinstance(s1, (int, float)):
            rd.append(s1)
        if s2 is not None and not isinstance(s2, (int, float)):
            rd.append(s2)
        wr = [out]
        kw = {}
        if op1 is not None:
            kw['op1'] = op1
        if accum_out is not None:
            kw['accum_out'] = accum_out
            wr.append(accum_out)
        self.add(eng, lambda e: e.tensor_scalar(out, in0, s1, s2, op0, **kw), rd, wr)

    def stt(self, eng, out, in0, scalar, in1, op0, op1):
        rd = [in0, in1]
        if not isinstance(scalar, (int, float)):
            rd.append(scalar)
        self.add(eng, lambda e: e.scalar_tensor_tensor(out, in0, scalar, in1, op0, op1), rd, [out])

    def copy(self, eng, out, in_):
        if eng == 'act':
            self.add('act', lambda e: e.activation(out, in_, AF.Identity), [in_], [out])
        else:
            self.add(eng, lambda e: e.tensor_copy(out, in_), [in_], [out])

    def memset(self, eng, out, val):
        self.add(eng, lambda e: e.memset(out, val), [], [out])

    def reduce(self, eng, out, in_, op, axis=AX.X):
        self.add(eng, lambda e: e.tensor_reduce(out, in_, axis, op), [in_], [out])

    def dma(self, eng, out, in_, **kw):
        self.add(eng, lambda e: e.dma_start(out, in_, **kw), [in_], [out], dma=True)

    def finish(self, out_aps):
        self.add('sp', lambda e: e.nop(), list(out_aps), [])

    def emit(self):
        nc = self.nc
        for e in self.CENG:
            c = 0
            for op in self.ops[e]:
                if op['marked']:
                    c += 1
                op['count'] = c
        csem = {e: self.es.enter_context(nc.semaphore("s_" + e)) for e in self.CENG}
        rsem = {r: [self.es.enter_context(nc.semaphore("d_%s%d" % (r, i))) for i in range(n)]
                for r, n in self.ring.items() if self.ndma[r] > 0}
        ops = self.ops
        stats = {e: [len(ops[e]), sum(len(o['waits']) for o in ops[e])] for e in self.ENGS}
        self.stats = stats

        def run(ename, eng):
            for op in ops[ename]:
                for (k, v) in op['waits']:
                    if k[0] == 'c':
                        eng.wait_ge(csem[k[1]], ops[k[1]][v]['count'])
                    else:
                        eng.wait_ge(rsem[k[1]][k[2]], v)
                inst = op['fn'](eng)
                if op['dma']:
                    k = op['event'][0]
                    inst.then_inc(rsem[k[1]][k[2]], 16)
                elif op['marked']:
                    inst.then_inc(csem[ename], 1)

        if not self.use_block:
            engs = {'pe': nc.tensor, 'act': nc.scalar, 'dve': nc.vector, 'pool': nc.gpsimd, 'sp': nc.sync}
            for op in self.order:
                ename = op['eng']
                eng = engs[ename]
                for (k, v) in op['waits']:
                    if k[0] == 'c':
                        eng.wait_ge(csem[k[1]], ops[k[1]][v]['count'])
                    else:
                        eng.wait_ge(rsem[k[1]][k[2]], v)
                inst = op['fn'](eng)
                if op['dma']:
                    k = op['event'][0]
                    inst.then_inc(rsem[k[1]][k[2]], 16)
                elif op['marked']:
                    inst.then_inc(csem[ename], 1)
            self.es.close()
            return nc
        with nc.Block() as block:
            @block.tensor
            def _(eng):
                run('pe', eng)

            @block.scalar
            def _(eng):
                run('act', eng)

            @block.vector
            def _(eng):
                run('dve', eng)

            @block.gpsimd
            def _(eng):
                run('pool', eng)

            @block.sync
            def _(eng):
                run('sp', eng)
        self.es.close()
        return nc

import ml_dtypes

D = 1024
SEQ = 2048
LC = 256
T = SEQ + LC
DEPTH = 4
NE = 16
CAP = 256
CAPC = 32
NSLOT = CAP + CAPC
FF = 2048
DN_ALPHA = (2 * DEPTH) ** 0.25
LN_EPS = 1e-5
NA_SCALE = 64 ** -0.5
MLA_SCALE = 96 ** -0.5
NEGM = -30000.0
TBS = [(0, 512), (512, 512), (1024, 512), (1536, 512), (2048, 256)]
C_QA, C_KA, C_VA, C_UP, C_UF, C_CQ, C_CKV, C_KRA, C_KRB, C_GT = 0, 256, 512, 768, 1024, 1280, 1536, 1664, 1696, 1728
WIN_COLS = 1728 + 4096
NSTRIP = 22 * 64
XPW = 8 + SEQ + 16 + LC + 8
XP_L = 8
XP_C = 8 + SEQ + 16


def host_constants():
    c = {}
    p = np.arange(128)
    a = p // 64
    kc = p % 64
    dd = np.arange(22)
    d = 10 - dd
    qc = np.arange(64)
    dr = a[:, None] + d[None, :]
    c0 = np.clip(qc - 8, 0, 48)
    colin = (kc[:, None] >= c0[None, :]) & (kc[:, None] < c0[None, :] + 16)
    mall = np.where((np.abs(dr) <= 7)[:, :, None] & colin[:, None, :], 0.0, NEGM)
    mint = np.where(((dr >= -4) & (dr <= 3))[:, :, None] & colin[:, None, :], 0.0, NEGM)
    c['nam'] = np.stack([mall.reshape(128, NSTRIP), mint.reshape(128, NSTRIP)]).astype(np.float32)
    ri = np.clip(dr + 7, 0, 14)
    ci = np.clip(kc[:, None] - qc[None, :], -15, 15) + 15
    c['_na_ri'] = np.broadcast_to(ri[:, :, None], (128, 22, 64)).reshape(128, NSTRIP)
    c['_na_ci'] = np.broadcast_to(ci[:, None, :], (128, 22, 64)).reshape(128, NSTRIP)
    n_freq = 8
    inv = (10000.0 ** (-np.arange(n_freq, dtype=np.float32) / n_freq)).astype(np.float32)
    t = np.arange(SEQ)
    row = (t // 64).astype(np.float32)
    col = (t % 64).astype(np.float32)
    ang = np.concatenate([row[:, None] * inv, col[:, None] * inv], axis=-1)
    cs, sn = np.cos(ang).T, np.sin(ang).T
    c['ropeC'] = np.concatenate([cs, cs], 0).astype(ml_dtypes.bfloat16)
    c['ropeS'] = np.concatenate([-sn, sn], 0).astype(ml_dtypes.bfloat16)
    k = np.arange(64)
    th = 2 * np.pi * np.outer(k, k) / 64.0
    cc, sc = np.cos(th), np.sin(th)
    bd = np.zeros((128, 256), np.float32)
    for g in range(2):
        bd[64 * g:64 * g + 64, 64 * g:64 * g + 64] = -cc
        bd[64 * g:64 * g + 64, 128 + 64 * g:128 + 64 * g + 64] = sc
    c['dftc'] = bd.astype(ml_dtypes.bfloat16)
    c['ident_f'] = np.eye(128, dtype=np.float32)
    c['ident_b'] = np.eye(128).astype(ml_dtypes.bfloat16)
    c['ustri'] = np.triu(np.ones((128, 128)), 1).astype(ml_dtypes.bfloat16)
    c['iota_f'] = np.broadcast_to(np.arange(NSLOT, dtype=np.float32)[None, :], (128, NSLOT)).copy()
    c['pidx'] = np.stack([np.arange(128), np.arange(128) + 128, np.arange(128) + 256], 1).astype(np.float32)
    c['lval'] = (128 * np.arange(16)[None, :] + np.arange(128)[:, None]).astype(np.float32)
    c['jrow'] = np.broadcast_to(np.arange(256, dtype=np.int32)[None, :], (128, 256)).copy()
    half = np.array([1, 2, 4, 8])
    pe = np.zeros((128, 2, 4, 8), np.float32)
    for ch in range(2):
        for pp in range(128):
            g = 2 * ch + pp // 64
            h = half[g]
            for reg, (L, left) in enumerate([(SEQ, True), (SEQ, False), (LC, True), (LC, False)]):
                for j in range(8):
                    tt = j if left else L - 8 + j
                    cnt = min(tt + h, L) - max(tt - h, 0)
                    pe[pp, ch, reg, j] = 1.0 / cnt
    c['pooledge'] = pe.reshape(128, 64)
    pw = np.zeros((128, 2), np.float32)
    for ch in range(2):
        for pp in range(128):
            pw[pp, ch] = 1.0 / (2 * half[2 * ch + pp // 64])
    c['poolinvw'] = pw
    return c


def vec_pj(v):
    return np.ascontiguousarray(v.reshape(-1, 128).T)


class Arena:
    def __init__(self, P, name, nbytes):
        self.t = P.sbuf(name, [128, nbytes // 4], F32)
        self.nbytes = nbytes
        self.off = 0

    def reset(self):
        self.off = 0

    def view(self, shape, dtype):
        esz = _DSZ[dtype]
        n = 1
        for s in shape[1:]:
            n *= s
        nb = (n * esz + 31) // 32 * 32
        assert self.off + nb <= self.nbytes, (self.off, nb, self.nbytes)
        v = self.t[0:shape[0], self.off // 4:(self.off + nb) // 4]
        self.off += nb
        if dtype != F32:
            v = v.bitcast(dtype)
        v = v[:, 0:n]
        if len(shape) == 3:
            v = v.rearrange("p (a b) -> p a b", a=shape[1])
        elif len(shape) == 4:
            v = v.rearrange("p (a b c) -> p a b c", a=shape[1], b=shape[2])
        return v


class PsumRing:
    def __init__(self, P, n=8, name="psb"):
        self.banks = [P.psum("%s%d" % (name, i), [128, 512], F32) for i in range(n)]
        self.i = 0

    def get(self):
        b = self.banks[self.i % len(self.banks)]
        self.i += 1
        return b


def ln_block(P, PS, zt, n, gcol, bcol, out_cb, ones_f, tmp, eng_alt):
    ps_s = PS.get()
    ps_q = PS.get()
    ones_b = tmp['ones_b']
    for j in range(8):
        zb = tmp['zb'][j % 2]
        P.copy('act' if j % 2 else 'dve', zb[:, 0:n], zt[:, j, 0:n])
        P.mm(ps_s[:, 0:n], ones_b[:], zb[:, 0:n], start=(j == 0), stop=(j == 7))
    for j in range(8):
        sq = tmp['sqb'][j % 2]
        P.act(sq[:, 0:n], zt[:, j, 0:n], AF.Square)
        P.mm(ps_q[:, 0:n], ones_b[:], sq[:, 0:n], start=(j == 0), stop=(j == 7))
    mean, rstd, nmr = tmp['mean'], tmp['rstd'], tmp['nmr']
    P.ts('dve', mean[:, 0:n], ps_s[:, 0:n], 1.0 / D, None, ALU.mult)
    P.tt('dve', nmr[:, 0:n], mean[:, 0:n], mean[:, 0:n], ALU.mult)
    P.stt('dve', rstd[:, 0:n], ps_q[:, 0:n], 1.0 / D, nmr[:, 0:n], ALU.mult, ALU.subtract)
    P.ts('dve', rstd[:, 0:n], rstd[:, 0:n], LN_EPS, None, ALU.add)
    P.act(rstd[:, 0:n], rstd[:, 0:n], AF.Sqrt)
    P.add('dve', lambda e: e.reciprocal(rstd[:, 0:n], rstd[:, 0:n]), [rstd[:, 0:n]], [rstd[:, 0:n]])
    P.stt('dve', nmr[:, 0:n], mean[:, 0:n], -1.0, rstd[:, 0:n], ALU.mult, ALU.mult)
    for j in range(8):
        P.tt('dve', zt[:, j, 0:n], zt[:, j, 0:n], rstd[:, 0:n], ALU.mult)
        P.tt('dve', zt[:, j, 0:n], zt[:, j, 0:n], nmr[:, 0:n], ALU.add)
        P.act(zt[:, j, 0:n], zt[:, j, 0:n], AF.Identity, bias=bcol[:, j:j + 1], scale=gcol[:, j:j + 1])
        out_cb(j, zt[:, j, 0:n])


def emit_prologue(P, c, final=False):
    AR, PS_T, PS_A, PS_B = c['AR'], c['PS_T'], c['PS_A'], c['PS_B']
    ones_f, lntmp, pix = c['ones_f'], c['lntmp'], c['pix']
    hT_in, ye, csT_p, gmT_p, modp_d, ln2p_d, e16_d = c['hT_in'], c['ye'], c['csT_p'], c['gmT_p'], c['modp'], c['ln2p'], c['eye16rep']
    mark = AR.off
    csb = AR.view([NE, T], BF16)
    gmb = AR.view([NE, T], BF16)
    csf = AR.view([NE, T], F32)
    e16 = AR.view([NE, NE * 128], BF16)
    modp = AR.view([128, 48, 2], F32)
    ln2s = AR.view([128, 2, 8], F32)
    GE = 4
    yeg = AR.view([128, GE, 3, D], BF16)
    stg = AR.view([128, GE, 2, 512], BF16)
    eqt = [AR.view([128, 512], BF16) for _ in range(2)]
    hb = AR.view([128, 8, 512], F32)
    zt = AR.view([128, 8, 512], F32)
    P.dma('sp', csf[:], csT_p[:, :])
    P.ts('dve', csf[:, SEQ:T], csf[:, SEQ:T], -float(CAP), None, ALU.add)
    P.copy('dve', csb[:], csf[:])
    P.dma('sp', csf[:], gmT_p[:, :])
    P.copy('dve', gmb[:], csf[:])
    P.dma('sp', e16[:], e16_d[:, :])
    P.dma('sp', modp[:], modp_d[:, :, :])
    P.dma('sp', ln2s[:], ln2p_d[:, :, :])
    G2 = 40
    for (t0, n) in TBS:
        x = 0 if t0 < SEQ else 1
        lat = t0 < SEQ
        if final and not lat:
            continue
        P.dma('sp', hb[:, :, 0:n], hT_in[:, t0:t0 + n].rearrange("(k p) n -> p k n", p=128))
        for g in range(NE // GE):
            for q in range(2):
                P.dma('sp' if q == 0 else 'act', yeg[:, :, q, :], ye[GE * g:GE * g + GE, 128 * q:128 * q + 128, :].rearrange("e p d -> p e d"))
            P.dma('sp', yeg[0:CAPC, :, 2, :], ye[GE * g:GE * g + GE, CAP:NSLOT, :].rearrange("e p d -> p e d"))
            for el in range(GE):
                e_ = GE * g + el
                pcs = PS_A.get()
                pgm = PS_A.get()
                P.mm(pcs[:, 0:n], e16[:, 128 * e_:128 * e_ + 128], csb[:, t0:t0 + n])
                P.mm(pgm[:, 0:n], e16[:, 128 * e_:128 * e_ + 128], gmb[:, t0:t0 + n])
                if lat:
                    for q in range(2):
                        et = eqt[q]
                        P.ts('dve', et[:, 0:n], pcs[:, 0:n], pix[:, q:q + 1], None, ALU.is_equal)
                        P.tt('dve', stg[:, el, q, 0:n], et[:, 0:n], pgm[:, 0:n], ALU.mult)
                else:
                    et = eqt[0]
                    P.ts('dve', et[0:CAPC, 0:n], pcs[0:CAPC, 0:n], pix[0:CAPC, 0:1], None, ALU.is_equal)
                    P.tt('dve', stg[0:CAPC, el, 0, 0:n], et[0:CAPC, 0:n], pgm[0:CAPC, 0:n], ALU.mult)
            for jo in range(8):
                po = PS_T.get()
                if lat:
                    for el in range(GE):
                        for q in range(2):
                            P.mm(po[:, 0:n], yeg[:, el, q, 128 * jo:128 * jo + 128], stg[:, el, q, 0:n],
                                 start=(el == 0 and q == 0), stop=(el == GE - 1 and q == 1))
                else:
                    for el in range(GE):
                        P.mm(po[:, 0:n], yeg[0:CAPC, el, 2, 128 * jo:128 * jo + 128], stg[0:CAPC, el, 0, 0:n],
                             start=(el == 0), stop=(el == GE - 1))
                if g == 0:
                    P.act(hb[:, jo, 0:n], hb[:, jo, 0:n], AF.Identity, scale=float(DN_ALPHA))
                    P.stt('dve', zt[:, jo, 0:n], po[:, 0:n], modp[:, G2 + jo, x:x + 1], hb[:, jo, 0:n], ALU.mult, ALU.add)
                else:
                    P.stt('dve', zt[:, jo, 0:n], po[:, 0:n], modp[:, G2 + jo, x:x + 1], zt[:, jo, 0:n], ALU.mult, ALU.add)

        def after_ln(j, ap, t0=t0, n=n, x=x):
            if final:
                P.dma('sp', c['hdst'][128 * j:128 * j + 128, t0:t0 + n], ap)
            else:
                P.dma('sp', c['hdst'][128 * j:128 * j + 128, t0:t0 + n], ap)
                P.act(c['uT'][:, j, t0:t0 + n], ap, AF.Identity, bias=c['modT'][:, j, x:x + 1], scale=c['ops1'][:, j, x:x + 1])
        ln_block(P, PS_A, zt, n, ln2s[:, 0, :], ln2s[:, 1, :], after_ln, ones_f, lntmp, ['dve', 'pool'])
    AR.off = mark


def build_D():
    nc = bass.Bass("TRN2", target_bir_lowering=False)
    P = Prog(nc, ring_sizes={'sp': 16, 'pool': 12, 'act': 4})
    c = {}
    c['hT_in'] = P.dram_in("hT_in", [D, T], F32)
    c['ye'] = P.dram_in("ye", [NE, NSLOT, D], BF16)
    c['csT_p'] = P.dram_in("csT_p", [NE, T], F32)
    c['gmT_p'] = P.dram_in("gmT_p", [NE, T], F32)
    c['modp'] = P.dram_in("modp", [128, 48, 2], F32)
    c['ln2p'] = P.dram_in("ln2p", [128, 2, 8], F32)
    c['eye16rep'] = P.dram_in("eye16rep", [NE, NE * 128], BF16)
    pidx = P.dram_in("pidx", [128, 3], F32)
    out = P.dram_out("out", [D, SEQ], F32)
    c['hdst'] = out
    c['ones_f'] = P.sbuf("ones_f", [128, 128], F32)
    c['pix'] = P.sbuf("pix", [128, 3], F32)
    c['lntmp'] = dict(sq=[P.sbuf("lnsq0", [128, 512], F32), P.sbuf("lnsq1", [128, 512], F32)], mean=P.sbuf("lnmean", [128, 512], F32),
                      rstd=P.sbuf("lnrstd", [128, 512], F32), nmr=P.sbuf("lnnmr", [128, 512], F32),
                      zb=[P.sbuf("lnzb0", [128, 512], BF16), P.sbuf("lnzb1", [128, 512], BF16)],
                      sqb=[P.sbuf("lnsqb0", [128, 512], BF16), P.sbuf("lnsqb1", [128, 512], BF16)])
    c['lntmp']['ones_b'] = P.sbuf("ones_b", [128, 128], BF16)
    P.memset('dve', c['lntmp']['ones_b'][:], 1.0)
    c['AR'] = Arena(P, "arena", 150 * 1024)
    c['PS_T'] = PsumRing(P, 4, "pst")
    c['PS_A'] = PsumRing(P, 2, "psa")
    c['PS_B'] = PsumRing(P, 1, "psbx")
    P.memset('dve', c['ones_f'][:], 1.0)
    P.dma('sp', c['pix'][:], pidx[:, :])
    emit_prologue(P, c, final=True)
    P.finish([out])
    P.emit()
    return nc, P


def build_B():
    TBB = 8 * NSLOT
    nc = bass.Bass("TRN2", target_bir_lowering=False)
    P = Prog(nc, ring_sizes={'sp': 16, 'pool': 12, 'act': 4})
    xe_in = P.dram_in("xe", [2, D, TBB], BF16)
    wg = P.dram_in("wg", [2, D, FF], F32)
    wu = P.dram_in("wu", [2, D, FF], F32)
    wd = P.dram_in("wd", [2, FF, D], F32)
    ye = P.dram_out("ye", [2, TBB, D], BF16)
    xes = P.sbuf("xes", [128, 8, TBB], BF16)
    wgs = P.sbuf("wgs", [128, 8, FF], BF16)
    wus = P.sbuf("wus", [128, 8, FF], BF16)
    wds = P.sbuf("wds", [128, 16, D], BF16)
    actT = P.sbuf("actT", [128, 16, 512], BF16)
    sil = [P.sbuf("sil%d" % i, [128, 512], F32) for i in range(2)]
    yo = [P.sbuf("yo%d" % i, [128, D], BF16) for i in range(2)]
    PS_T = PsumRing(P, 4, "pst")
    PS_A = PsumRing(P, 2, "psa")
    blocks = [(0, 512), (512, 512), (1024, 512), (1536, 512), (2048, 256)]
    yi = 0
    for e_ in range(2):
        P.dma('sp', xes[:], xe_in[e_].rearrange("(k p) n -> p k n", p=128))
        for pc in range(4):
            P.dma('pool', wgs[:, :, 512 * pc:512 * pc + 512], wg[e_, :, 512 * pc:512 * pc + 512].rearrange("(k p) n -> p k n", p=128))
            P.dma('pool', wus[:, :, 512 * pc:512 * pc + 512], wu[e_, :, 512 * pc:512 * pc + 512].rearrange("(k p) n -> p k n", p=128))
        for pc in range(4):
            P.dma('pool', wds[:, 4 * pc:4 * pc + 4, :], wd[e_, 512 * pc:512 * pc + 512, :].rearrange("(k p) n -> p k n", p=128))
        for (t0, n) in blocks:
            for f in range(16):
                pa = PS_T.get()
                pu = PS_T.get()
                for k in range(8):
                    P.mm(pa[:, 0:n], wgs[:, k, 128 * f:128 * f + 128], xes[:, k, t0:t0 + n], start=(k == 0), stop=(k == 7))
                for k in range(8):
                    P.mm(pu[:, 0:n], wus[:, k, 128 * f:128 * f + 128], xes[:, k, t0:t0 + n], start=(k == 0), stop=(k == 7))
                s_ = sil[f % 2]
                P.act(s_[:, 0:n], pa[:, 0:n], AF.Silu)
                P.tt('dve', actT[:, f, 0:n], s_[:, 0:n], pu[:, 0:n], ALU.mult)
            for m in range(n // 128):
                y_ = yo[yi % 2]
                yi += 1
                for hf in range(2):
                    po = PS_A.get()
                    for f in range(16):
                        P.mm(po[:, 0:512], actT[:, f, 128 * m:128 * m + 128], wds[:, f, 512 * hf:512 * hf + 512], start=(f == 0), stop=(f == 15))
                    P.copy('act' if hf == 0 else 'dve', y_[:, 512 * hf:512 * hf + 512], po[:, 0:512])
                P.dma('sp', ye[e_, t0 + 128 * m:t0 + 128 * m + 128, :], y_[:])
    P.finish([ye])
    P.emit()
    return nc, P


def build_A(prologue, stage=99, sub=99):
    nc = bass.Bass("TRN2", target_bir_lowering=False)
    P = Prog(nc, ring_sizes={'sp': 16, 'pool': 12, 'act': 4})
    I = {}

    def inp(name, shape, dt=F32):
        I[name] = P.dram_in(name, shape, dt)
        return I[name]
    hT_in = inp("hT_in", [D, T])
    cs2 = inp("cs2", [128, 8, 2])
    wmod = inp("wmod", [D, 6 * D])
    bmod = inp("bmod", [128, 48])
    w_in = inp("w_in", [D, WIN_COLS])
    nab = inp("nab", [4, 128, NSTRIP])
    nam = inp("nam", [2, 128, NSTRIP])
    pool_w = inp("pool_w", [4, 64, 64])
    pvec = inp("pvec", [128, 8])
    w_uq = inp("w_uq", [256, 512])
    w_ukv = inp("w_ukv", [128, 512])
    w_br = inp("w_br", [D, D])
    w_out = inp("w_out", [D, D])
    ln1 = inp("ln1", [128, 2, 8])
    w_rt = inp("w_rt", [D, NE])
    ropeC = inp("ropeC", [32, SEQ], BF16)
    ropeS = inp("ropeS", [32, SEQ], BF16)
    dftc = inp("dftc", [128, 256], BF16)
    ident_f = inp("ident_f", [128, 128])
    ident_b = inp("ident_b", [128, 128], BF16)
    ustri = inp("ustri", [128, 128], BF16)
    iota_f = inp("iota_f", [128, NSLOT])
    pidx = inp("pidx", [128, 3])
    lval = inp("lval", [128, 16])
    jrow = inp("jrow", [128, 256], I32)
    pooledge = inp("pooledge", [128, 64])
    poolinvw = inp("poolinvw", [128, 2])
    if prologue:
        ye = inp("ye", [NE, NSLOT, D], BF16)
        csT_p = inp("csT_p", [NE, T])
        gmT_p = inp("gmT_p", [NE, T])
        modp = inp("modp", [128, 48, 2])
        ln2p = inp("ln2p", [128, 2, 8])
        eye16rep = inp("eye16rep", [NE, NE * 128], BF16)
    h1T = P.dram_out("h1T", [D, T], F32)
    xeT = P.dram_out("xeT", [NE, D, NSLOT], BF16)
    csT_o = P.dram_out("csT_out", [NE, T], F32)
    gmT_o = P.dram_out("gmT_out", [NE, T], F32)
    modT_o = P.dram_out("modT_out", [128, 48, 2], F32)
    dbg = P.dram_out("dbg", [D, T], F32) if stage < 99 else None
    hs = P.dram_tmp("hs", [D, T], F32)
    outs = [h1T, xeT, csT_o, gmT_o, modT_o] + ([dbg] if dbg is not None else [])

    S = {}

    def sb(name, shape, dt=F32):
        S[name] = P.sbuf(name, shape, dt)
        return S[name]
    ones_f = sb("ones_f", [128, 128])
    ones_b = sb("ones_b", [128, 128], BF16)
    idf = sb("idf", [128, 128])
    idb = sb("idb", [128, 128], BF16)
    ust = sb("ust", [128, 128], BF16)
    iof = sb("iof", [128, NSLOT])
    pix = sb("pix", [128, 3])
    modT = sb("modT", [128, 48, 2])
    ops1 = sb("ops1", [128, 8, 2])
    ops2 = sb("ops2", [128, 8, 2])
    pv = sb("pv", [128, 8])
    ln1s = sb("ln1s", [128, 2, 8])
    scs = sb("scs", [128, 8, 2])
    bm = sb("bm", [128, 48])
    lntmp = dict(sq=[sb("lnsq0", [128, 512]), sb("lnsq1", [128, 512])], mean=sb("lnmean", [128, 512]),
                 rstd=sb("lnrstd", [128, 512]), nmr=sb("lnnmr", [128, 512]),
                 zb=[sb("lnzb0", [128, 512], BF16), sb("lnzb1", [128, 512], BF16)],
                 sqb=[sb("lnsqb0", [128, 512], BF16), sb("lnsqb1", [128, 512], BF16)], ones_b=ones_b)
    wring = [sb("wr%d" % i, [128, 8, 512], BF16) for i in range(3)]
    wri = [0]
    PT = [sb("pt%d" % i, [128, 512], BF16) for i in range(4)]
    pti = [0]
    rden = sb("rden", [128, 512])
    rbc = sb("rbc", [128, 512])
    AR = Arena(P, "arena", 150 * 1024)
    PS_T = PsumRing(P, 4, "pst")
    PS_A = PsumRing(P, 2, "psa")
    PS_B = PsumRing(P, 1, "psbx")
    pstb = P.psum("pstb", [128, 1024], BF16)

    def nextw():
        w = wring[wri[0] % 3]
        wri[0] += 1
        return w

    def nextpt():
        t = PT[pti[0] % 4]
        pti[0] += 1
        return t
    alt = [0]

    def ev_eng():
        alt[0] += 1
        return 'dve' if alt[0] % 2 else 'act'

    def loadw(src, lo, n, kch=8):
        w = nextw()
        P.dma('pool', w[:, 0:kch, 0:n], src[:, lo:lo + n].rearrange("(k p) n -> p k n", p=128))
        return w

    P.memset('dve', ones_f[:], 1.0)
    P.memset('dve', ones_b[:], 1.0)
    P.dma('sp', idf[:], ident_f[:, :])
    P.dma('sp', idb[:], ident_b[:, :])
    P.dma('sp', ust[:], ustri[:, :])
    P.dma('sp', iof[:], iota_f[:, :])
    P.dma('sp', pix[:], pidx[:, :])
    P.dma('sp', pv[:], pvec[:, :])
    P.dma('sp', ln1s[:], ln1[:, :, :])
    P.dma('sp', scs[:], cs2[:, :, :])
    P.dma('sp', bm[:], bmod[:, :])
    P.act(scs[:], scs[:], AF.Silu)
    mark0 = AR.off
    wm = [AR.view([128, 8, 1024], F32) for _ in range(2)]
    modrow = AR.view([2, 6 * D], F32)
    pm = PS_B.get()
    for blk in range(6):
        w = wm[blk % 2]
        P.dma('sp', w[:], wmod[:, 1024 * blk:1024 * blk + 1024].rearrange("(k p) n -> p k n", p=128))
        for hb2 in range(2):
            prow = PS_T.get()
            for k in range(8):
                P.mm(prow[0:2, 0:512], scs[:, k, :], w[:, k, 512 * hb2:512 * hb2 + 512], start=(k == 0), stop=(k == 7))
            P.copy('dve', modrow[:, 1024 * blk + 512 * hb2:1024 * blk + 512 * hb2 + 512], prow[0:2, 0:512])
    for j in range(48):
        P.mm(pm[:, 2 * j:2 * j + 2], modrow[:, 128 * j:128 * j + 128], idf[0:2, 0:2])
    for x in range(2):
        P.tt('dve', modT[:, :, x], pm[:, 0:96].rearrange("p (j x) -> p j x", x=2)[:, :, x], bm[:, :], ALU.add)
    P.ts('dve', ops1[:], modT[:, 8:16, :], 1.0, None, ALU.add)
    P.ts('dve', ops2[:], modT[:, 32:40, :], 1.0, None, ALU.add)
    P.dma('sp', modT_o[:, :, :], modT[:])
    AR.off = mark0
    SH1, G1, SH2, G2 = 0, 16, 24, 40

    uT = AR.view([128, 8, T], BF16)
    mark_u = AR.off
    if prologue:
        hsrc = hs
        emit_prologue(P, dict(AR=AR, PS_T=PS_T, PS_A=PS_A, PS_B=PS_B, ones_f=ones_f, lntmp=lntmp, pix=pix, hT_in=hT_in, ye=ye,
                              csT_p=csT_p, gmT_p=gmT_p, modp=modp, ln2p=ln2p, eye16rep=eye16rep, hdst=hs, uT=uT, ops1=ops1, modT=modT))
    else:
        hsrc = hT_in
        hb = [AR.view([128, 8, 512], F32) for _ in range(2)]
        for bi, (t0, n) in enumerate(TBS):
            x = 0 if t0 < SEQ else 1
            b = hb[bi % 2]
            P.dma('sp', b[:, :, 0:n], hT_in[:, t0:t0 + n].rearrange("(k p) n -> p k n", p=128))
            for j in range(8):
                P.ts('dve', uT[:, j, t0:t0 + n], b[:, j, 0:n], ops1[:, j, x:x + 1],
                     modT[:, SH1 + j, x:x + 1], ALU.mult, ALU.add)
    AR.off = mark_u
    yT = [AR.view([128, 2, T], BF16) for _ in range(4)]
    mark_y = AR.off

    def proj_fm(wv, m_lo, M, consume, po=0):
        for (t0, n) in TBS:
            ps = PS_T.get()
            for k in range(8):
                P.mm(ps[po:po + M, 0:n], wv[:, k, m_lo:m_lo + M], uT[:, k, t0:t0 + n], start=(k == 0), stop=(k == 7))
            consume(ps, t0, n)

    def finalize_attn(po, h, dst, t0, n):
        c = h // 2
        if h % 2 == 0:
            dp, lo, hi, op_ = 64, 0, 64, 64
        else:
            dp, lo, hi, op_ = 0, 64, 128, 0
        P.add('dve', lambda e: e.reciprocal(rden[dp:dp + 1, 0:n], po[dp:dp + 1, 0:n]), [po[dp:dp + 1, 0:n]], [rden[dp:dp + 1, 0:n]])
        pbc = PS_B.get()
        P.mm(pbc[lo:hi, 0:n], ones_f[dp:dp + 1, 0:64], rden[dp:dp + 1, 0:n])
        P.copy('act', rbc[lo:hi, 0:n], pbc[lo:hi, 0:n])
        P.tt('dve', dst[lo:hi, c, t0:t0 + n], po[lo:hi, 0:n], rbc[lo:hi, 0:n], ALU.mult)

    def init_vpad(v):
        P.memset('dve', v, 0.0)

    if stage >= 1:
        AR.off = mark_y
        qaT = AR.view([128, 2, T], BF16)
        kaT = AR.view([128, 2, T], BF16)
        va2 = AR.view([128, 18, 4, 128], BF16)
        wall = AR.view([128, NSTRIP], BF16)
        wint = AR.view([128, NSTRIP], BF16)
        nbf = AR.view([128, NSTRIP], F32)
        nmk = AR.view([128, 2, NSTRIP], F32)
        P.dma('sp', nmk[:], nam.rearrange("a p n -> p a n"))
        P.memset('dve', va2[:], 0.0)
        for h in range(4):
            cc_ = 64 if h % 2 == 0 else 0
            P.memset('dve', va2[:, :, h, cc_:cc_ + 1], 1.0)
        wv = loadw(w_in, C_QA, 512)
        for ci, dst in ((0, qaT), (256, kaT)):
            for c in range(2):
                proj_fm(wv, ci + 128 * c, 128,
                        lambda ps, t0, n, dst=dst, c=c: P.copy(ev_eng(), dst[:, c, t0:t0 + n], ps[:, 0:n]))
        wv = loadw(w_in, C_VA, 256)
        for tc in range(18):
            ps = PS_T.get()
            for k in range(8):
                P.mm(ps[:, 0:256], uT[:, k, 128 * tc:128 * tc + 128], wv[:, k, 0:256], start=(k == 0), stop=(k == 7))
            for hp in range(2):
                src = ps[:, 0:256].rearrange("p (h d) -> p h d", h=4)
                if hp == 0:
                    P.copy(ev_eng(), va2[:, tc, 0::2, 0:64], src[:, 0::2, :])
                else:
                    P.copy(ev_eng(), va2[:, tc, 1::2, 64:128], src[:, 1::2, :])
        for h in range(4):
            c, pb = h // 2, 64 * (h % 2)
            M = 65 if h % 2 == 0 else 128
            P.dma('sp', nbf[:], nab[h, :, :])
            for which, dstw in ((0, wall), (1, wint)):
                P.tt('dve', dstw[:], nbf[:], nmk[:, which, :], ALU.add)
                P.act(dstw[:], dstw[:], AF.Exp)
            for qb in range(4):
                lo_i = [0, 2, 6, 10][qb]
                hi_i = [5, 9, 13, 15][qb]
                seq = list(range(lo_i, hi_i + 1)) + [16, 17]
                po = PS_A.get()
                def sc_(i):
                    ps = PS_T.get()
                    P.mm(ps[:, 0:512], kaT[pb:pb + 64, c, 128 * i:128 * i + 128], qaT[pb:pb + 64, c, 512 * qb:512 * qb + 512])
                    return ps
                pq = [sc_(seq[0]), sc_(seq[1])]
                for idx, i in enumerate(seq):
                    ps = pq.pop(0)
                    if idx + 2 < len(seq):
                        pq.append(sc_(seq[idx + 2]))
                    pt = nextpt()
                    P.act(pt[:], ps[:, 0:512], AF.Exp, scale=NA_SCALE)
                    if i < 16:
                        s0 = (10 - (2 * i - 8 * qb)) * 64
                        me = 'dve'
                        if qb == 0:
                            wa = wall if i <= 3 else wint
                            P.tt(me, pt[:, 0:256], pt[:, 0:256], wa[:, s0:s0 + 256], ALU.mult)
                            P.tt(me, pt[:, 256:512], pt[:, 256:512], wint[:, s0 + 256:s0 + 512], ALU.mult)
                        elif qb == 3:
                            wa = wall if i >= 12 else wint
                            P.tt(me, pt[:, 0:320], pt[:, 0:320], wint[:, s0:s0 + 320], ALU.mult)
                            P.tt(me, pt[:, 320:512], pt[:, 320:512], wa[:, s0 + 320:s0 + 512], ALU.mult)
                        else:
                            P.tt(me, pt[:], pt[:], wint[:, s0:s0 + 512], ALU.mult)
                    P.mm(po[0:M, 0:512], va2[:, i, h, 0:M], pt[:], start=(idx == 0), stop=(idx == len(seq) - 1))
                finalize_attn(po, h, yT[0], 512 * qb, 512)
            po = PS_A.get()
            for idx, i in enumerate([16, 17]):
                ps = PS_T.get()
                P.mm(ps[:, 0:256], kaT[pb:pb + 64, c, 128 * i:128 * i + 128], qaT[pb:pb + 64, c, SEQ:T])
                pt = nextpt()
                P.act(pt[:, 0:256], ps[:, 0:256], AF.Exp, scale=NA_SCALE)
                P.mm(po[0:M, 0:256], va2[:, i, h, 0:M], pt[:, 0:256], start=(idx == 0), stop=(idx == 1))
            finalize_attn(po, h, yT[0], SEQ, 256)


    if stage >= 2:
        AR.off = mark_y
        xp = AR.view([128, 2, XPW], F32)
        la = AR.view([128, XPW], F32)
        lb = AR.view([128, XPW], F32)
        ybf = AR.view([128, 2, T], BF16)
        pwbd = AR.view([128, 2, 128], BF16)
        pwf = AR.view([128, 2, 128], F32)
        pe = AR.view([128, 64], F32)
        piw = AR.view([128, 2], F32)
        etmp = AR.view([128, 8], F32)
        P.memset('dve', xp[:], 0.0)
        P.memset('dve', la[:], 0.0)
        P.memset('dve', lb[:], 0.0)
        P.memset('dve', pwf[:], 0.0)
        for g in range(4):
            o = 64 * (g % 2)
            P.dma('sp', pwf[o:o + 64, g // 2, o:o + 64], pool_w[g, :, :])
        P.copy('dve', pwbd[:], pwf[:])
        P.dma('sp', pe[:], pooledge[:, :])
        P.dma('sp', piw[:], poolinvw[:, :])
        wv = loadw(w_in, C_UP, 256)

        def up_consume(ps, t0, n, c):
            off = XP_L + t0 if t0 < SEQ else XP_C
            P.copy(ev_eng(), xp[:, c, off:off + n], ps[:, 0:n])
        for c in range(2):
            proj_fm(wv, 128 * c, 128, lambda ps, t0, n, c=c: up_consume(ps, t0, n, c))
        W = XPW
        for c in range(2):
            x = xp[:, c, :]
            P.tt('dve', la[:, 1:W], x[:, 1:W], x[:, 0:W - 1], ALU.add)
            P.tt('dve', lb[:, 1:W - 1], la[:, 2:W], la[:, 0:W - 2], ALU.add)
            if c == 1:
                P.tt('dve', la[:, 2:W - 2], lb[:, 4:W], lb[:, 0:W - 4], ALU.add)
                P.tt('dve', lb[:, 4:W - 4], la[:, 8:W], la[:, 0:W - 8], ALU.add)
            for (pl, src) in ((0, la), (64, lb)):
                for (off, L, toff) in ((XP_L, SEQ, 0), (XP_C, LC, SEQ)):
                    P.stt('dve', ybf[pl:pl + 64, c, toff:toff + L], src[pl:pl + 64, off:off + L], piw[pl:pl + 64, c:c + 1],
                          x[pl:pl + 64, off:off + L], ALU.mult, ALU.subtract)
                for reg, (off, toff) in enumerate(((XP_L, 0), (XP_L + SEQ - 8, SEQ - 8), (XP_C, SEQ), (XP_C + LC - 8, T - 8))):
                    ec = (c * 4 + reg) * 8
                    P.tt('dve', etmp[pl:pl + 64, :], src[pl:pl + 64, off:off + 8], pe[pl:pl + 64, ec:ec + 8], ALU.mult)
                    P.tt('dve', ybf[pl:pl + 64, c, toff:toff + 8], etmp[pl:pl + 64, :], x[pl:pl + 64, off:off + 8], ALU.subtract)
            for (t0, n) in TBS:
                ps = PS_T.get()
                P.mm(ps[:, 0:n], pwbd[:, c, :], ybf[:, c, t0:t0 + n])
                P.ts('dve', yT[1][:, c, t0:t0 + n], ps[:, 0:n], pv[:, c:c + 1], None, ALU.mult)

    if stage >= 3:
        AR.off = mark_y
        ufT = AR.view([128, 2, T], BF16)
        AB = AR.view([128, 18, 2, 256], BF16)
        dfc = AR.view([128, 256], BF16)
        lv = AR.view([128, 16], F32)
        lofs = AR.view([128, 16, 8, 2], F32)
        jr = AR.view([128, 256], I32)
        tabC = AR.view([128, 16, 256], BF16)
        tabS = AR.view([128, 16, 256], BF16)
        ki = [AR.view([128, 256], I32) for _ in range(2)]
        P.dma('sp', dfc[:], dftc[:, :])
        P.dma('sp', lv[:], lval[:, :])
        P.dma('sp', jr[:], jrow[:, :])
        for jb in range(8):
            P.ts('dve', lofs[:, :, jb, 0], lv[:], 256.0 * jb, 512.0, ALU.mult, ALU.add)
            P.ts('dve', lofs[:, :, jb, 1], lv[:], 256.0 * jb, None, ALU.mult)
        wv = loadw(w_in, C_UF, 256)
        for c in range(2):
            proj_fm(wv, 128 * c, 128, lambda ps, t0, n, c=c: P.copy(ev_eng(), ufT[:, c, t0:t0 + n], ps[:, 0:n]))
        for tc in range(18):
            for c in range(2):
                ps = PS_T.get()
                P.mm(ps[:, 0:256], ufT[:, c, 128 * tc:128 * tc + 128], dfc[:, 0:256])
                P.copy(ev_eng(), AB[:, tc, c, :], ps[:, 0:256])
        kc_ = [0]

        def gen_tab(dst, a, ofs_ap, ofs_imm, mask, scale):
            k = ki[kc_[0] % 2]
            kc_[0] += 1
            if ofs_ap is not None:
                P.ts('dve', k[:], jr[:], lv[:, a:a + 1], ofs_ap, ALU.mult, ALU.add)
            else:
                P.ts('dve', k[:], jr[:], lv[:, a:a + 1], float(ofs_imm), ALU.mult, ALU.add)
            P.ts('dve', k[:], k[:], mask, None, ALU.bitwise_and)
            P.act(dst, k[:], AF.Sin, bias=mpi[:, 0:1], scale=scale)
        mpi = AR.view([128, 1], F32)
        P.memset('dve', mpi[:], -float(np.pi))
        for jb in range(8):
            for a in range(16):
                gen_tab(tabC[:, a, :], a, lofs[:, a, jb, 0:1], None, 2047, 2.0 * np.pi / 2048.0)
                gen_tab(tabS[:, a, :], a, lofs[:, a, jb, 1:2], None, 2047, 2.0 * np.pi / 2048.0)
            for c in range(2):
                po = PS_A.get()
                for a in range(16):
                    P.mm(po[:, 0:256], AB[:, a, c, 0:128], tabC[:, a, :], start=(a == 0), stop=False)
                    P.mm(po[:, 0:256], AB[:, a, c, 128:256], tabS[:, a, :], start=False, stop=(a == 15))
                P.ts('dve', yT[2][:, c, 256 * jb:256 * jb + 256], po[:, 0:256], float((SEQ * 64.0) ** -0.5), None, ALU.mult)
        for a in range(2):
            gen_tab(tabC[:, a, :], a, None, 64, 255, 2.0 * np.pi / 256.0)
            gen_tab(tabS[:, a, :], a, None, 0, 255, 2.0 * np.pi / 256.0)
        for c in range(2):
            po = PS_A.get()
            for a in range(2):
                P.mm(po[:, 0:256], AB[:, 16 + a, c, 0:128], tabC[:, a, :], start=(a == 0), stop=False)
                P.mm(po[:, 0:256], AB[:, 16 + a, c, 128:256], tabS[:, a, :], start=False, stop=(a == 1))
            P.ts('dve', yT[2][:, c, SEQ:T], po[:, 0:256], float((LC * 64.0) ** -0.5), None, ALU.mult)


    if stage >= 4:
        AR.off = mark_y
        cqn = AR.view([128, 2, T], BF16)
        ckvn = AR.view([128, T], BF16)
        kro = AR.view([128, T], BF16)
        rC = AR.view([128, SEQ], BF16)
        rS = AR.view([128, SEQ], BF16)
        qm = AR.view([128, T], BF16)
        km = AR.view([128, T], BF16)
        vm = AR.view([128, 18, 128], BF16)
        wuq = AR.view([128, 2, 512], BF16)
        wukv = AR.view([128, 512], BF16)
        sqb = [AR.view([128, 512], BF16) for _ in range(2)]
        rst = AR.view([128, 512], F32)
        rt1 = AR.view([128, 512], F32)
        rt2 = AR.view([128, 512], F32)
        P.dma('sp', rC[64:96, :], ropeC[:, :])
        P.dma('sp', rS[64:96, :], ropeS[:, :])
        P.dma('pool', wuq[:], w_uq.rearrange("(k p) n -> p k n", p=128))
        P.dma('pool', wukv[:], w_ukv[:, :])
        wv = loadw(w_in, C_CQ, 448)

        def rope_apply(dst, psa, psb, t0, n):
            if t0 < SEQ:
                P.tt('dve', rt1[64:96, 0:n], psa[64:96, 0:n], rC[64:96, t0:t0 + n], ALU.mult)
                P.tt('dve', rt2[64:96, 0:n], psb[64:96, 0:n], rS[64:96, t0:t0 + n], ALU.mult)
                P.tt('dve', dst[64:96, t0:t0 + n], rt1[64:96, 0:n], rt2[64:96, 0:n], ALU.add)
            else:
                P.copy('act', dst[64:96, t0:t0 + n], psa[64:96, 0:n])
        for (t0, n) in TBS:
            pc = [PS_T.get(), PS_T.get()]
            pss = PS_B.get()
            for c in range(2):
                for k in range(8):
                    P.mm(pc[c][:, 0:n], wv[:, k, 128 * c:128 * c + 128], uT[:, k, t0:t0 + n], start=(k == 0), stop=(k == 7))
                P.act(sqb[c][:, 0:n], pc[c][:, 0:n], AF.Square)
                P.mm(pss[:, 0:n], ones_b[:], sqb[c][:, 0:n], start=(c == 0), stop=(c == 1))
            P.ts('dve', rst[:, 0:n], pss[:, 0:n], 1.0 / 256.0, LN_EPS, ALU.mult, ALU.add)
            P.act(rst[:, 0:n], rst[:, 0:n], AF.Sqrt)
            P.add('dve', lambda e, n=n: e.reciprocal(rst[:, 0:n], rst[:, 0:n]), [rst[:, 0:n]], [rst[:, 0:n]])
            for c in range(2):
                P.stt('dve', cqn[:, c, t0:t0 + n], pc[c][:, 0:n], pv[:, 2 + c:3 + c], rst[:, 0:n], ALU.mult, ALU.mult)
            pk = PS_T.get()
            pss = PS_B.get()
            for k in range(8):
                P.mm(pk[:, 0:n], wv[:, k, 256:384], uT[:, k, t0:t0 + n], start=(k == 0), stop=(k == 7))
            P.act(sqb[0][:, 0:n], pk[:, 0:n], AF.Square)
            P.mm(pss[:, 0:n], ones_b[:], sqb[0][:, 0:n])
            P.ts('dve', rst[:, 0:n], pss[:, 0:n], 1.0 / 128.0, LN_EPS, ALU.mult, ALU.add)
            P.act(rst[:, 0:n], rst[:, 0:n], AF.Sqrt)
            P.add('dve', lambda e, n=n: e.reciprocal(rst[:, 0:n], rst[:, 0:n]), [rst[:, 0:n]], [rst[:, 0:n]])
            P.stt('dve', ckvn[:, t0:t0 + n], pk[:, 0:n], pv[:, 4:5], rst[:, 0:n], ALU.mult, ALU.mult)
            pa = PS_T.get()
            pb_ = PS_T.get()
            for k in range(8):
                P.mm(pa[64:96, 0:n], wv[:, k, 384:416], uT[:, k, t0:t0 + n], start=(k == 0), stop=(k == 7))
            for k in range(8):
                P.mm(pb_[64:96, 0:n], wv[:, k, 416:448], uT[:, k, t0:t0 + n], start=(k == 0), stop=(k == 7))
            rope_apply(kro, pa, pb_, t0, n)
        for h in range(4):
            M = 65 if h % 2 == 0 else 128
            vo = 0 if h % 2 == 0 else 64
            P.memset('dve', vm[:], 0.0)
            oc = 64 if h % 2 == 0 else 0
            P.memset('dve', vm[:, :, oc:oc + 1], 1.0)
            for (t0, n) in TBS:
                pa = PS_T.get()
                pb_ = PS_T.get()
                for k in range(2):
                    P.mm(pa[0:96, 0:n], wuq[:, k, 128 * h:128 * h + 96], cqn[:, k, t0:t0 + n], start=(k == 0), stop=(k == 1))
                for k in range(2):
                    P.mm(pb_[64:96, 0:n], wuq[:, k, 128 * h + 96:128 * h + 128], cqn[:, k, t0:t0 + n], start=(k == 0), stop=(k == 1))
                P.copy('act', qm[0:64, t0:t0 + n], pa[0:64, 0:n])
                rope_apply(qm, pa, pb_, t0, n)
                pk = PS_T.get()
                P.mm(pk[0:64, 0:n], wukv[:, 128 * h:128 * h + 64], ckvn[:, t0:t0 + n])
                P.copy('act', km[0:64, t0:t0 + n], pk[0:64, 0:n])
                P.copy('dve', km[64:96, t0:t0 + n], kro[64:96, t0:t0 + n])
            for g0 in range(0, 18, 8):
                gn = min(8, 18 - g0)
                pvv = PS_T.get()
                for tc in range(g0, g0 + gn):
                    P.mm(pvv[:, 64 * (tc - g0):64 * (tc - g0) + 64], ckvn[:, 128 * tc:128 * tc + 128],
                         wukv[:, 128 * h + 64:128 * h + 128])
                P.copy(ev_eng(), vm[:, g0:g0 + gn, vo:vo + 64], pvv[:, 0:64 * gn].rearrange("p (a d) -> p a d", d=64))
            for (t0, n) in TBS:
                seq = list(range(18)) if t0 < SEQ else [16, 17]
                po = PS_A.get()
                def sc_(i):
                    ps = PS_T.get()
                    P.mm(ps[:, 0:n], km[0:96, 128 * i:128 * i + 128], qm[0:96, t0:t0 + n])
                    return ps
                pq = [sc_(seq[0]), sc_(seq[1])]
                for idx, i in enumerate(seq):
                    ps = pq.pop(0)
                    if idx + 2 < len(seq):
                        pq.append(sc_(seq[idx + 2]))
                    pt = nextpt()
                    P.act(pt[:, 0:n], ps[:, 0:n], AF.Exp, scale=MLA_SCALE)
                    P.mm(po[0:M, 0:n], vm[:, i, 0:M], pt[:, 0:n], start=(idx == 0), stop=(idx == len(seq) - 1))
                finalize_attn(po, h, yT[3], t0, n)


    u2tok = None
    if stage >= 5:
        AR.off = mark_y
        merged = AR.view([128, 8, T], BF16)
        wbr = AR.view([128, 2, 4, 128], BF16)
        sg = [AR.view([128, 512], F32) for _ in range(2)]
        macc = AR.view([128, 512], F32)
        mt = AR.view([128, 512], F32)
        for jd in range(8):
            wg_ = nextw()
            for nb in range(4):
                P.dma('pool', wg_[:, :, 128 * nb:128 * nb + 128],
                      w_in[:, C_GT + 1024 * nb + 128 * jd:C_GT + 1024 * nb + 128 * jd + 128].rearrange("(k p) n -> p k n", p=128))
                P.dma('pool', wbr[:, :, nb, :],
                      w_br[256 * nb:256 * nb + 256, 128 * jd:128 * jd + 128].rearrange("(k p) n -> p k n", p=128))
            for (t0, n) in TBS:
                for nb in range(4):
                    pg = PS_T.get()
                    for k in range(8):
                        P.mm(pg[:, 0:n], wg_[:, k, 128 * nb:128 * nb + 128], uT[:, k, t0:t0 + n], start=(k == 0), stop=(k == 7))
                    pp = PS_T.get()
                    for k in range(2):
                        P.mm(pp[:, 0:n], wbr[:, k, nb, :], yT[nb][:, k, t0:t0 + n], start=(k == 0), stop=(k == 1))
                    s_ = sg[nb % 2]
                    P.act(s_[:, 0:n], pg[:, 0:n], AF.Sigmoid)
                    if nb == 0:
                        P.tt('dve', macc[:, 0:n], s_[:, 0:n], pp[:, 0:n], ALU.mult)
                    elif nb < 3:
                        P.tt('dve', mt[:, 0:n], s_[:, 0:n], pp[:, 0:n], ALU.mult)
                        P.tt('dve', macc[:, 0:n], macc[:, 0:n], mt[:, 0:n], ALU.add)
                    else:
                        P.tt('dve', mt[:, 0:n], s_[:, 0:n], pp[:, 0:n], ALU.mult)
                        P.tt('dve', merged[:, jd, t0:t0 + n], macc[:, 0:n], mt[:, 0:n], ALU.add)
        AR.off = mark_u
        u2tok = AR.view([128, 18, D], BF16)
        assert AR.off <= mark_y
        AR.off = mark_y + 8 * T * 2 + 64
        wo0 = nextw()
        wo1 = nextw()
        P.dma('pool', wo0[:, :, 0:512], w_out[:, 0:512].rearrange("(k p) n -> p k n", p=128))
        P.dma('pool', wo1[:, :, 0:512], w_out[:, 512:1024].rearrange("(k p) n -> p k n", p=128))
        wrt = AR.view([128, 8, NE], F32)
        P.dma('sp', wrt[:], w_rt.rearrange("(k p) n -> p k n", p=128))
        u2b = AR.view([128, 8, 512], BF16)
        lgT = AR.view([NE, T], F32)
        mark_m = AR.off
        AR.off = 0
        hbk = [AR.view([128, 8, 512], F32)]
        zts = [AR.view([128, 8, 512], F32)]
        AR.off = mark_m
        zts.append(AR.view([128, 8, 512], F32))
        for bi, (t0, n) in enumerate(TBS):
            x = 0 if t0 < SEQ else 1
            hb_ = hbk[0]
            zt = zts[bi % 2]
            P.dma('sp', hb_[:, :, 0:n], hsrc[:, t0:t0 + n].rearrange("(k p) n -> p k n", p=128))
            for jo in range(8):
                po = PS_T.get()
                for k in range(8):
                    P.mm(po[:, 0:n], (wo0 if jo < 4 else wo1)[:, k, 128 * (jo % 4):128 * (jo % 4) + 128], merged[:, k, t0:t0 + n], start=(k == 0), stop=(k == 7))
                P.act(hb_[:, jo, 0:n], hb_[:, jo, 0:n], AF.Identity, scale=float(DN_ALPHA))
                P.stt('dve', zt[:, jo, 0:n], po[:, 0:n], modT[:, G1 + jo, x:x + 1], hb_[:, jo, 0:n], ALU.mult, ALU.add)

            def after_ln(j, ap, t0=t0, n=n, x=x):
                P.dma('sp', h1T[128 * j:128 * j + 128, t0:t0 + n], ap)
                P.act(zt[:, j, 0:n], ap, AF.Identity, bias=modT[:, SH2 + j, x:x + 1], scale=ops2[:, j, x:x + 1])
                P.copy('act', u2b[:, j, 0:n], zt[:, j, 0:n])
            ln_block(P, PS_A, zt, n, ln1s[:, 0, :], ln1s[:, 1, :], after_ln, ones_f, lntmp, ['dve', 'pool'])
            pl = PS_B.get()
            for k in range(8):
                P.mm(pl[0:NE, 0:n], wrt[:, k, :], zt[:, k, 0:n], start=(k == 0), stop=(k == 7))
            P.copy('dve', lgT[:, t0:t0 + n], pl[0:NE, 0:n])
            for q in range(n // 128):
                tc = t0 // 128 + q
                for k in range(8):
                    P.tr(pstb[:, 128 * k:128 * k + 128], u2b[:, k, 128 * q:128 * q + 128], idb[:])
                P.copy('act', u2tok[:, tc, :], pstb[:, :])

    if stage >= 6:
        AR.off = mark_y
        mask = AR.view([128, 18, NE], F32)
        maskb = AR.view([128, 18, NE], BF16)
        cs = AR.view([128, 18, NE], F32)
        csm = AR.view([128, 18, NE], F32)
        affT = AR.view([NE, T], F32)
        maskT = AR.view([NE, T], F32)
        maskTb = AR.view([NE, T], BF16)
        gmT = AR.view([NE, T], F32)
        m8 = AR.view([NE, 8], F32)
        m8c = AR.view([NE, 8], F32)
        rsm = AR.view([NE, 512], F32)
        assert AR.off <= mark_m - NE * 0 - T * 4
        P.act(affT[:], lgT[:], AF.Exp)
        for (t0, n) in TBS:
            ps = PS_T.get()
            P.mm(ps[0:NE, 0:n], ones_f[0:NE, 0:NE], affT[:, t0:t0 + n])
            P.add('dve', lambda e, ps=ps, n=n: e.reciprocal(rsm[:, 0:n], ps[0:NE, 0:n]), [ps[0:NE, 0:n]], [rsm[:, 0:n]])
            P.tt('dve', affT[:, t0:t0 + n], affT[:, t0:t0 + n], rsm[:, 0:n], ALU.mult)
        if sub >= 1:
            mark_r = AR.off
            AR.off = 0
            wk = AR.view([NE, SEQ], F32)
            wkc = AR.view([NE, LC], F32)
            csT = AR.view([NE, T], F32)
            P.copy('dve', wk[:], affT[:, 0:SEQ])
            P.copy('dve', wkc[:], affT[:, SEQ:T])
            for r in range(CAP // 8):
                P.add('dve', lambda e: e.max(m8[:], wk[:]), [wk[:]], [m8[:]])
                if r < CAP // 8 - 1:
                    P.add('dve', lambda e: e.match_replace(wk[:], m8[:], wk[:], -1.0), [wk[:], m8[:]], [wk[:]])
            for r in range(CAPC // 8):
                P.add('dve', lambda e: e.max(m8c[:], wkc[:]), [wkc[:]], [m8c[:]])
                if r < CAPC // 8 - 1:
                    P.add('dve', lambda e: e.match_replace(wkc[:], m8c[:], wkc[:], -1.0), [wkc[:], m8c[:]], [wkc[:]])
        if sub >= 2:
            P.ts('dve', maskT[:, 0:SEQ], affT[:, 0:SEQ], m8[:, 7:8], None, ALU.is_ge)
            P.ts('dve', maskT[:, SEQ:T], affT[:, SEQ:T], m8c[:, 7:8], None, ALU.is_ge)
            P.tt('dve', gmT[:], maskT[:], affT[:], ALU.mult)
            P.dma('sp', gmT_o[:, :], gmT[:])
        if sub >= 3:
            import os
            S3 = float(os.environ.get('SUB3', '9'))
            if S3 >= 0.1:
                P.copy('dve', maskTb[:], maskT[:])
            pmk = PS_T.get()
            if S3 >= 0.2:
                for tc in range(18):
                    P.mm(pmk[:, NE * tc:NE * tc + NE], maskTb[:, 128 * tc:128 * tc + 128], idb[0:NE, 0:NE])
            if S3 >= 0.3:
                P.copy('dve', mask[:], pmk[:, 0:18 * NE].rearrange("p (a b) -> p a b", b=NE))
            if S3 >= 0.4:
                P.copy('dve', maskb[:], mask[:])
            for tc in range(18 if S3 >= 2 else 0):
                base = 0 if tc < 16 else 16
                pc_ = PS_B.get()
                for c2 in range(base, tc):
                    P.mm(pc_[:, 0:NE], ones_b[:], maskb[:, c2, :], start=(c2 == base), stop=False)
                P.mm(pc_[:, 0:NE], ust[:], maskb[:, tc, :], start=(tc == base), stop=True)
                P.ts('dve', cs[:, tc, :], pc_[:, 0:NE], 0.0 if tc < 16 else float(CAP), None, ALU.add)
                if S3 < 3:
                    continue
                pt_ = PS_T.get()
                for c2 in range(base, tc):
                    P.mm(pt_[0:NE, 0:128], maskb[:, c2, :], ones_b[:], start=(c2 == base), stop=False)
                P.mm(pt_[0:NE, 0:128], maskb[:, tc, :], ust[:], start=(tc == base), stop=True)
                P.ts('dve', csT[:, 128 * tc:128 * tc + 128], pt_[0:NE, 0:128], 0.0 if tc < 16 else float(CAP), None, ALU.add)
            if S3 >= 3:
                P.dma('sp', csT_o[:, :], csT[:])
        if sub >= 4:
            P.tt('dve', csm[:], cs[:], mask[:], ALU.mult)
            P.tt('dve', csm[:], csm[:], mask[:], ALU.add)
            P.ts('dve', csm[:], csm[:], -1.0, None, ALU.add)
            AR.off = 0
            selL = [AR.view([128, 16, CAP], BF16) for _ in range(2)]
            selC = [AR.view([128, 2, CAPC], BF16) for _ in range(2)]
            xs = [AR.view([128, 8, NSLOT], BF16) for _ in range(2)]
            assert AR.off <= mark_u
            for e_ in range(NE):
                sl, sc_, x_ = selL[e_ % 2], selC[e_ % 2], xs[e_ % 2]
                P.tt('dve', sl[:], iof[:, 0:CAP].unsqueeze(1).to_broadcast([128, 16, CAP]),
                     csm[:, 0:16, e_:e_ + 1].to_broadcast([128, 16, CAP]), ALU.is_equal)
                P.tt('dve', sc_[:], iof[:, CAP:NSLOT].unsqueeze(1).to_broadcast([128, 2, CAPC]),
                     csm[:, 16:18, e_:e_ + 1].to_broadcast([128, 2, CAPC]), ALU.is_equal)
                for j in range(8):
                    px = PS_T.get()
                    for tc in range(16):
                        P.mm(px[:, 0:CAP], u2tok[:, tc, 128 * j:128 * j + 128], sl[:, tc, :], start=(tc == 0), stop=(tc == 15))
                    for tc in range(16, 18):
                        P.mm(px[:, CAP:NSLOT], u2tok[:, tc, 128 * j:128 * j + 128], sc_[:, tc - 16, :], start=(tc == 16), stop=(tc == 17))
                    P.copy(ev_eng(), x_[:, j, :], px[:, 0:NSLOT])
                P.dma('sp', xeT[e_].rearrange("(j p) s -> p j s", p=128), x_[:])
    def dump_fm(src, nchunk, row0=0):
        for c in range(nchunk):
            tmpd = lntmp['sq'][c % 2]
            for (t0, n) in TBS:
                P.copy('dve', tmpd[:, 0:n], src[:, c, t0:t0 + n])
                P.dma('sp', dbg[row0 + 128 * c:row0 + 128 * c + 128, t0:t0 + n], tmpd[:, 0:n])
    if stage < 99:
        for bi in range(4):
            if stage >= [1, 2, 3, 4][bi]:
                dump_fm(yT[bi], 2, 256 * bi)
    P.finish(outs)
    P.emit()
    return nc, P


def build_F(nl=DEPTH):
    stage = 99
    sub = 99
    nc = bass.Bass("TRN2", target_bir_lowering=False)
    P = Prog(nc, ring_sizes={'sp': 16, 'pool': 12, 'act': 4})
    I = {}

    def inp(name, shape, dt=F32):
        I[name] = P.dram_in(name, shape, dt)
        return I[name]
    hT_in = inp("hT_in", [D, T])
    cs2 = inp("cs2", [128, 8, 2])
    wmod_all = inp("wmod", [DEPTH] + [D, 6 * D])
    bmod_all = inp("bmod", [DEPTH] + [128, 48])
    w_in_all = inp("w_in", [DEPTH] + [D, WIN_COLS])
    nab_all = inp("nab", [DEPTH] + [4, 128, NSTRIP])
    nam = inp("nam", [2, 128, NSTRIP])
    pool_w_all = inp("pool_w", [DEPTH] + [4, 64, 64])
    pvec_all = inp("pvec", [DEPTH] + [128, 8])
    w_uq_all = inp("w_uq", [DEPTH] + [256, 512])
    w_ukv_all = inp("w_ukv", [DEPTH] + [128, 512])
    w_br_all = inp("w_br", [DEPTH] + [D, D])
    w_out_all = inp("w_out", [DEPTH] + [D, D])
    ln1_all = inp("ln1", [DEPTH] + [128, 2, 8])
    w_rt_all = inp("w_rt", [DEPTH] + [D, NE])
    ropeC = inp("ropeC", [32, SEQ], BF16)
    ropeS = inp("ropeS", [32, SEQ], BF16)
    dftc = inp("dftc", [128, 256], BF16)
    ident_f = inp("ident_f", [128, 128])
    ident_b = inp("ident_b", [128, 128], BF16)
    ustri = inp("ustri", [128, 128], BF16)
    iota_f = inp("iota_f", [128, NSLOT])
    pidx = inp("pidx", [128, 3])
    lval = inp("lval", [128, 16])
    jrow = inp("jrow", [128, 256], I32)
    pooledge = inp("pooledge", [128, 64])
    poolinvw = inp("poolinvw", [128, 2])
    ln2_all = inp("ln2", [DEPTH, 128, 2, 8])
    eye16rep = inp("eye16rep", [NE, NE * 128], BF16)
    wg_all = inp("wg", [DEPTH, NE, D, FF])
    wu_all = inp("wu", [DEPTH, NE, D, FF])
    wd_all = inp("wd", [DEPTH, NE, FF, D])
    out_d = P.dram_out("out", [D, SEQ], F32)
    h1T = P.dram_tmp("h1s", [D, T], F32)
    hs = P.dram_tmp("hs", [D, T], F32)
    ye = P.dram_tmp("ye_scr", [NE, NSLOT, D], BF16)
    csT_o = P.dram_tmp("csT_scr", [NE, T], F32)
    gmT_o = P.dram_tmp("gmT_scr", [NE, T], F32)
    mod_scr = P.dram_tmp("mod_scr", [DEPTH, 128, 48, 2], F32)
    tab_scr = P.dram_tmp("tab_scr", [8, 2, 128, 16 * 256], BF16)
    dbg = None
    outs = [out_d]
    S = {}

    def sb(name, shape, dt=F32):
        S[name] = P.sbuf(name, shape, dt)
        return S[name]
    ones_f = sb("ones_f", [128, 128])
    ones_b = sb("ones_b", [128, 128], BF16)
    idf = sb("idf", [128, 128])
    idb = sb("idb", [128, 128], BF16)
    ust = sb("ust", [128, 128], BF16)
    iof = sb("iof", [128, NSLOT])
    pix = sb("pix", [128, 3])
    modT = sb("modT", [128, 48, 2])
    ops1 = sb("ops1", [128, 8, 2])
    ops2 = sb("ops2", [128, 8, 2])
    pv = sb("pv", [128, 8])
    ln1s = sb("ln1s", [128, 2, 8])
    scs = sb("scs", [128, 8, 2])
    bm = sb("bm", [128, 48])
    lntmp = dict(sq=[sb("lnsq0", [128, 512]), sb("lnsq1", [128, 512])], mean=sb("lnmean", [128, 512]),
                 rstd=sb("lnrstd", [128, 512]), nmr=sb("lnnmr", [128, 512]),
                 zb=[sb("lnzb0", [128, 512], BF16), sb("lnzb1", [128, 512], BF16)],
                 sqb=[sb("lnsqb0", [128, 512], BF16), sb("lnsqb1", [128, 512], BF16)], ones_b=ones_b)
    wring = [sb("wr%d" % i, [128, 8, 512], BF16) for i in range(3)]
    wri = [0]
    PT = [sb("pt%d" % i, [128, 512], BF16) for i in range(4)]
    pti = [0]
    rden = sb("rden", [128, 512])
    rbc = sb("rbc", [128, 512])
    AR = Arena(P, "arena", 150 * 1024)
    PS_T = PsumRing(P, 4, "pst")
    PS_A = PsumRing(P, 2, "psa")
    PS_B = PsumRing(P, 1, "psbx")
    pstb = P.psum("pstb", [128, 1024], BF16)

    def nextw():
        w = wring[wri[0] % 3]
        wri[0] += 1
        return w

    def nextpt():
        t = PT[pti[0] % 4]
        pti[0] += 1
        return t
    alt = [0]

    def ev_eng():
        alt[0] += 1
        return 'dve' if alt[0] % 2 else 'act'

    def loadw(src, lo, n, kch=8):
        w = nextw()
        P.dma('pool', w[:, 0:kch, 0:n], src[:, lo:lo + n].rearrange("(k p) n -> p k n", p=128))
        return w

    P.memset('dve', ones_f[:], 1.0)
    P.memset('dve', ones_b[:], 1.0)
    P.dma('sp', idf[:], ident_f[:, :])
    P.dma('sp', idb[:], ident_b[:, :])
    P.dma('sp', ust[:], ustri[:, :])
    P.dma('sp', iof[:], iota_f[:, :])
    P.dma('sp', pix[:], pidx[:, :])
    P.dma('sp', scs[:], cs2[:, :, :])
    for l in range(nl):
        prologue = l > 0
        wmod, bmod, w_in, nab, pool_w, pvec = wmod_all[l], bmod_all[l], w_in_all[l], nab_all[l], pool_w_all[l], pvec_all[l]
        w_uq, w_ukv, w_br, w_out, ln1, w_rt = w_uq_all[l], w_ukv_all[l], w_br_all[l], w_out_all[l], ln1_all[l], w_rt_all[l]
        AR.off = 0
        P.dma('sp', pv[:], pvec[:, :])
        P.dma('sp', ln1s[:], ln1[:, :, :])
        P.dma('sp', bm[:], bmod[:, :])
        P.dma('sp', scs[:], cs2[:, :, :])
        P.act(scs[:], scs[:], AF.Silu)
        mark0 = AR.off
        wm = [AR.view([128, 8, 1024], F32) for _ in range(2)]
        modrow = AR.view([2, 6 * D], F32)
        pm = PS_B.get()
        for blk in range(6):
            w = wm[blk % 2]
            P.dma('sp', w[:], wmod[:, 1024 * blk:1024 * blk + 1024].rearrange("(k p) n -> p k n", p=128))
            for hb2 in range(2):
                prow = PS_T.get()
                for k in range(8):
                    P.mm(prow[0:2, 0:512], scs[:, k, :], w[:, k, 512 * hb2:512 * hb2 + 512], start=(k == 0), stop=(k == 7))
                P.copy('dve', modrow[:, 1024 * blk + 512 * hb2:1024 * blk + 512 * hb2 + 512], prow[0:2, 0:512])
        for j in range(48):
            P.mm(pm[:, 2 * j:2 * j + 2], modrow[:, 128 * j:128 * j + 128], idf[0:2, 0:2])
        for x in range(2):
            P.tt('dve', modT[:, :, x], pm[:, 0:96].rearrange("p (j x) -> p j x", x=2)[:, :, x], bm[:, :], ALU.add)
        P.ts('dve', ops1[:], modT[:, 8:16, :], 1.0, None, ALU.add)
        P.ts('dve', ops2[:], modT[:, 32:40, :], 1.0, None, ALU.add)
        P.dma('sp', mod_scr[l], modT[:])
        AR.off = mark0
        SH1, G1, SH2, G2 = 0, 16, 24, 40

        uT = AR.view([128, 8, T], BF16)
        mark_u = AR.off
        if prologue:
            hsrc = hs
            emit_prologue(P, dict(AR=AR, PS_T=PS_T, PS_A=PS_A, PS_B=PS_B, ones_f=ones_f, lntmp=lntmp, pix=pix, hT_in=h1T, ye=ye,
                                  csT_p=csT_o, gmT_p=gmT_o, modp=mod_scr[l - 1], ln2p=ln2_all[l - 1], eye16rep=eye16rep, hdst=hs, uT=uT, ops1=ops1, modT=modT))
        else:
            hsrc = hT_in
            hb = [AR.view([128, 8, 512], F32) for _ in range(2)]
            for bi, (t0, n) in enumerate(TBS):
                x = 0 if t0 < SEQ else 1
                b = hb[bi % 2]
                P.dma('sp', b[:, :, 0:n], hT_in[:, t0:t0 + n].rearrange("(k p) n -> p k n", p=128))
                for j in range(8):
                    P.ts('dve', uT[:, j, t0:t0 + n], b[:, j, 0:n], ops1[:, j, x:x + 1],
                         modT[:, SH1 + j, x:x + 1], ALU.mult, ALU.add)
        AR.off = mark_u
        yT = [AR.view([128, 2, T], BF16) for _ in range(4)]
        mark_y = AR.off

        def proj_fm(wv, m_lo, M, consume, po=0):
            for (t0, n) in TBS:
                ps = PS_T.get()
                for k in range(8):
                    P.mm(ps[po:po + M, 0:n], wv[:, k, m_lo:m_lo + M], uT[:, k, t0:t0 + n], start=(k == 0), stop=(k == 7))
                consume(ps, t0, n)

        def finalize_attn(po, h, dst, t0, n):
            c = h // 2
            if h % 2 == 0:
                dp, lo, hi, op_ = 64, 0, 64, 64
            else:
                dp, lo, hi, op_ = 0, 64, 128, 0
            P.add('dve', lambda e: e.reciprocal(rden[dp:dp + 1, 0:n], po[dp:dp + 1, 0:n]), [po[dp:dp + 1, 0:n]], [rden[dp:dp + 1, 0:n]])
            pbc = PS_B.get()
            P.mm(pbc[lo:hi, 0:n], ones_f[dp:dp + 1, 0:64], rden[dp:dp + 1, 0:n])
            P.copy('act', rbc[lo:hi, 0:n], pbc[lo:hi, 0:n])
            P.tt('dve', dst[lo:hi, c, t0:t0 + n], po[lo:hi, 0:n], rbc[lo:hi, 0:n], ALU.mult)

        def init_vpad(v):
            P.memset('dve', v, 0.0)

        if stage >= 1:
            AR.off = mark_y
            qaT = AR.view([128, 2, T], BF16)
            kaT = AR.view([128, 2, T], BF16)
            va2 = AR.view([128, 18, 4, 128], BF16)
            wall = AR.view([128, NSTRIP], BF16)
            wint = AR.view([128, NSTRIP], BF16)
            nbf = AR.view([128, NSTRIP], F32)
            nmk = AR.view([128, 2, NSTRIP], F32)
            P.dma('sp', nmk[:], nam.rearrange("a p n -> p a n"))
            P.memset('dve', va2[:], 0.0)
            for h in range(4):
                cc_ = 64 if h % 2 == 0 else 0
                P.memset('dve', va2[:, :, h, cc_:cc_ + 1], 1.0)
            wv = loadw(w_in, C_QA, 512)
            for ci, dst in ((0, qaT), (256, kaT)):
                for c in range(2):
                    proj_fm(wv, ci + 128 * c, 128,
                            lambda ps, t0, n, dst=dst, c=c: P.copy(ev_eng(), dst[:, c, t0:t0 + n], ps[:, 0:n]))
            wv = loadw(w_in, C_VA, 256)
            for tc in range(18):
                ps = PS_T.get()
                for k in range(8):
                    P.mm(ps[:, 0:256], uT[:, k, 128 * tc:128 * tc + 128], wv[:, k, 0:256], start=(k == 0), stop=(k == 7))
                for hp in range(2):
                    src = ps[:, 0:256].rearrange("p (h d) -> p h d", h=4)
                    if hp == 0:
                        P.copy(ev_eng(), va2[:, tc, 0::2, 0:64], src[:, 0::2, :])
                    else:
                        P.copy(ev_eng(), va2[:, tc, 1::2, 64:128], src[:, 1::2, :])
            for h in range(4):
                c, pb = h // 2, 64 * (h % 2)
                M = 65 if h % 2 == 0 else 128
                P.dma('sp', nbf[:], nab[h, :, :])
                for which, dstw in ((0, wall), (1, wint)):
                    P.tt('dve', dstw[:], nbf[:], nmk[:, which, :], ALU.add)
                    P.act(dstw[:], dstw[:], AF.Exp)
                for qb in range(4):
                    lo_i = [0, 2, 6, 10][qb]
                    hi_i = [5, 9, 13, 15][qb]
                    seq = list(range(lo_i, hi_i + 1)) + [16, 17]
                    po = PS_A.get()
                    def sc_(i):
                        ps = PS_T.get()
                        P.mm(ps[:, 0:512], kaT[pb:pb + 64, c, 128 * i:128 * i + 128], qaT[pb:pb + 64, c, 512 * qb:512 * qb + 512])
                        return ps
                    pq = [sc_(seq[0]), sc_(seq[1])]
                    for idx, i in enumerate(seq):
                        ps = pq.pop(0)
                        if idx + 2 < len(seq):
                            pq.append(sc_(seq[idx + 2]))
                        pt = nextpt()
                        P.act(pt[:], ps[:, 0:512], AF.Exp, scale=NA_SCALE)
                        if i < 16:
                            s0 = (10 - (2 * i - 8 * qb)) * 64
                            me = 'dve'
                            if qb == 0:
                                wa = wall if i <= 3 else wint
                                P.tt(me, pt[:, 0:256], pt[:, 0:256], wa[:, s0:s0 + 256], ALU.mult)
                                P.tt(me, pt[:, 256:512], pt[:, 256:512], wint[:, s0 + 256:s0 + 512], ALU.mult)
                            elif qb == 3:
                                wa = wall if i >= 12 else wint
                                P.tt(me, pt[:, 0:320], pt[:, 0:320], wint[:, s0:s0 + 320], ALU.mult)
                                P.tt(me, pt[:, 320:512], pt[:, 320:512], wa[:, s0 + 320:s0 + 512], ALU.mult)
                            else:
                                P.tt(me, pt[:], pt[:], wint[:, s0:s0 + 512], ALU.mult)
                        P.mm(po[0:M, 0:512], va2[:, i, h, 0:M], pt[:], start=(idx == 0), stop=(idx == len(seq) - 1))
                    finalize_attn(po, h, yT[0], 512 * qb, 512)
                po = PS_A.get()
                for idx, i in enumerate([16, 17]):
                    ps = PS_T.get()
                    P.mm(ps[:, 0:256], kaT[pb:pb + 64, c, 128 * i:128 * i + 128], qaT[pb:pb + 64, c, SEQ:T])
                    pt = nextpt()
                    P.act(pt[:, 0:256], ps[:, 0:256], AF.Exp, scale=NA_SCALE)
                    P.mm(po[0:M, 0:256], va2[:, i, h, 0:M], pt[:, 0:256], start=(idx == 0), stop=(idx == 1))
                finalize_attn(po, h, yT[0], SEQ, 256)


        if stage >= 2:
            AR.off = mark_y
            xp = AR.view([128, 2, XPW], F32)
            la = AR.view([128, XPW], F32)
            lb = AR.view([128, XPW], F32)
            ybf = AR.view([128, 2, T], BF16)
            pwbd = AR.view([128, 2, 128], BF16)
            pwf = AR.view([128, 2, 128], F32)
            pe = AR.view([128, 64], F32)
            piw = AR.view([128, 2], F32)
            etmp = AR.view([128, 8], F32)
            P.memset('dve', xp[:], 0.0)
            P.memset('dve', la[:], 0.0)
            P.memset('dve', lb[:], 0.0)
            P.memset('dve', pwf[:], 0.0)
            for g in range(4):
                o = 64 * (g % 2)
                P.dma('sp', pwf[o:o + 64, g // 2, o:o + 64], pool_w[g, :, :])
            P.copy('dve', pwbd[:], pwf[:])
            P.dma('sp', pe[:], pooledge[:, :])
            P.dma('sp', piw[:], poolinvw[:, :])
            wv = loadw(w_in, C_UP, 256)

            def up_consume(ps, t0, n, c):
                off = XP_L + t0 if t0 < SEQ else XP_C
                P.copy(ev_eng(), xp[:, c, off:off + n], ps[:, 0:n])
            for c in range(2):
                proj_fm(wv, 128 * c, 128, lambda ps, t0, n, c=c: up_consume(ps, t0, n, c))
            W = XPW
            for c in range(2):
                x = xp[:, c, :]
                P.tt('dve', la[:, 1:W], x[:, 1:W], x[:, 0:W - 1], ALU.add)
                P.tt('dve', lb[:, 1:W - 1], la[:, 2:W], la[:, 0:W - 2], ALU.add)
                if c == 1:
                    P.tt('dve', la[:, 2:W - 2], lb[:, 4:W], lb[:, 0:W - 4], ALU.add)
                    P.tt('dve', lb[:, 4:W - 4], la[:, 8:W], la[:, 0:W - 8], ALU.add)
                for (pl, src) in ((0, la), (64, lb)):
                    for (off, L, toff) in ((XP_L, SEQ, 0), (XP_C, LC, SEQ)):
                        P.stt('dve', ybf[pl:pl + 64, c, toff:toff + L], src[pl:pl + 64, off:off + L], piw[pl:pl + 64, c:c + 1],
                              x[pl:pl + 64, off:off + L], ALU.mult, ALU.subtract)
                    for reg, (off, toff) in enumerate(((XP_L, 0), (XP_L + SEQ - 8, SEQ - 8), (XP_C, SEQ), (XP_C + LC - 8, T - 8))):
                        ec = (c * 4 + reg) * 8
                        P.tt('dve', etmp[pl:pl + 64, :], src[pl:pl + 64, off:off + 8], pe[pl:pl + 64, ec:ec + 8], ALU.mult)
                        P.tt('dve', ybf[pl:pl + 64, c, toff:toff + 8], etmp[pl:pl + 64, :], x[pl:pl + 64, off:off + 8], ALU.subtract)
                for (t0, n) in TBS:
                    ps = PS_T.get()
                    P.mm(ps[:, 0:n], pwbd[:, c, :], ybf[:, c, t0:t0 + n])
                    P.ts('dve', yT[1][:, c, t0:t0 + n], ps[:, 0:n], pv[:, c:c + 1], None, ALU.mult)

        if stage >= 3:
            AR.off = mark_y
            ufT = AR.view([128, 2, T], BF16)
            AB = AR.view([128, 18, 2, 256], BF16)
            dfc = AR.view([128, 256], BF16)
            lv = AR.view([128, 16], F32)
            lofs = AR.view([128, 16, 8, 2], F32)
            jr = AR.view([128, 256], I32)
            tabCs = [AR.view([128, 16, 256], BF16) for _ in range(2)]
            tabSs = [AR.view([128, 16, 256], BF16) for _ in range(2)]
            tabC, tabS = tabCs[0], tabSs[0]
            ki = [AR.view([128, 256], I32) for _ in range(2)]
            P.dma('sp', dfc[:], dftc[:, :])
            P.dma('sp', lv[:], lval[:, :])
            P.dma('sp', jr[:], jrow[:, :])
            for jb in range(8):
                P.ts('dve', lofs[:, :, jb, 0], lv[:], 256.0 * jb, 512.0, ALU.mult, ALU.add)
                P.ts('dve', lofs[:, :, jb, 1], lv[:], 256.0 * jb, None, ALU.mult)
            wv = loadw(w_in, C_UF, 256)
            for c in range(2):
                proj_fm(wv, 128 * c, 128, lambda ps, t0, n, c=c: P.copy(ev_eng(), ufT[:, c, t0:t0 + n], ps[:, 0:n]))
            for tc in range(18):
                for c in range(2):
                    ps = PS_T.get()
                    P.mm(ps[:, 0:256], ufT[:, c, 128 * tc:128 * tc + 128], dfc[:, 0:256])
                    P.copy(ev_eng(), AB[:, tc, c, :], ps[:, 0:256])
            kc_ = [0]

            def gen_tab(dst, a, ofs_ap, ofs_imm, mask, scale):
                k = ki[kc_[0] % 2]
                kc_[0] += 1
                if ofs_ap is not None:
                    P.ts('dve', k[:], jr[:], lv[:, a:a + 1], ofs_ap, ALU.mult, ALU.add)
                else:
                    P.ts('dve', k[:], jr[:], lv[:, a:a + 1], float(ofs_imm), ALU.mult, ALU.add)
                P.ts('dve', k[:], k[:], mask, None, ALU.bitwise_and)
                P.act(dst, k[:], AF.Sin, bias=mpi[:, 0:1], scale=scale)
            mpi = AR.view([128, 1], F32)
            P.memset('dve', mpi[:], -float(np.pi))
            for jb in range(8):
                tabC, tabS = tabCs[jb % 2], tabSs[jb % 2]
                if l == 0:
                    for a in range(16):
                        gen_tab(tabC[:, a, :], a, lofs[:, a, jb, 0:1], None, 2047, 2.0 * np.pi / 2048.0)
                        gen_tab(tabS[:, a, :], a, lofs[:, a, jb, 1:2], None, 2047, 2.0 * np.pi / 2048.0)
                    P.dma('sp', tab_scr[jb, 0].rearrange("p (a b) -> p a b", b=256), tabC[:])
                    P.dma('sp', tab_scr[jb, 1].rearrange("p (a b) -> p a b", b=256), tabS[:])
                else:
                    P.dma('sp', tabC[:], tab_scr[jb, 0].rearrange("p (a b) -> p a b", b=256))
                    P.dma('sp', tabS[:], tab_scr[jb, 1].rearrange("p (a b) -> p a b", b=256))
                for c in range(2):
                    po = PS_A.get()
                    for a in range(16):
                        P.mm(po[:, 0:256], AB[:, a, c, 0:128], tabC[:, a, :], start=(a == 0), stop=False)
                        P.mm(po[:, 0:256], AB[:, a, c, 128:256], tabS[:, a, :], start=False, stop=(a == 15))
                    P.ts('dve', yT[2][:, c, 256 * jb:256 * jb + 256], po[:, 0:256], float((SEQ * 64.0) ** -0.5), None, ALU.mult)
            for a in range(2):
                gen_tab(tabC[:, a, :], a, None, 64, 255, 2.0 * np.pi / 256.0)
                gen_tab(tabS[:, a, :], a, None, 0, 255, 2.0 * np.pi / 256.0)
            for c in range(2):
                po = PS_A.get()
                for a in range(2):
                    P.mm(po[:, 0:256], AB[:, 16 + a, c, 0:128], tabC[:, a, :], start=(a == 0), stop=False)
                    P.mm(po[:, 0:256], AB[:, 16 + a, c, 128:256], tabS[:, a, :], start=False, stop=(a == 1))
                P.ts('dve', yT[2][:, c, SEQ:T], po[:, 0:256], float((LC * 64.0) ** -0.5), None, ALU.mult)


        if stage >= 4:
            AR.off = mark_y
            cqn = AR.view([128, 2, T], BF16)
            ckvn = AR.view([128, T], BF16)
            kro = AR.view([128, T], BF16)
            rC = AR.view([128, SEQ], BF16)
            rS = AR.view([128, SEQ], BF16)
            qm = AR.view([128, T], BF16)
            km = AR.view([128, T], BF16)
            vm = AR.view([128, 18, 128], BF16)
            wuq = AR.view([128, 2, 512], BF16)
            wukv = AR.view([128, 512], BF16)
            sqb = [AR.view([128, 512], BF16) for _ in range(2)]
            rst = AR.view([128, 512], F32)
            rt1 = AR.view([128, 512], F32)
            rt2 = AR.view([128, 512], F32)
            P.dma('sp', rC[64:96, :], ropeC[:, :])
            P.dma('sp', rS[64:96, :], ropeS[:, :])
            P.dma('pool', wuq[:], w_uq.rearrange("(k p) n -> p k n", p=128))
            P.dma('pool', wukv[:], w_ukv[:, :])
            wv = loadw(w_in, C_CQ, 448)

            def rope_apply(dst, psa, psb, t0, n):
                if t0 < SEQ:
                    P.tt('dve', rt1[64:96, 0:n], psa[64:96, 0:n], rC[64:96, t0:t0 + n], ALU.mult)
                    P.tt('dve', rt2[64:96, 0:n], psb[64:96, 0:n], rS[64:96, t0:t0 + n], ALU.mult)
                    P.tt('dve', dst[64:96, t0:t0 + n], rt1[64:96, 0:n], rt2[64:96, 0:n], ALU.add)
                else:
                    P.copy('act', dst[64:96, t0:t0 + n], psa[64:96, 0:n])
            for (t0, n) in TBS:
                pc = [PS_T.get(), PS_T.get()]
                pss = PS_B.get()
                for c in range(2):
                    for k in range(8):
                        P.mm(pc[c][:, 0:n], wv[:, k, 128 * c:128 * c + 128], uT[:, k, t0:t0 + n], start=(k == 0), stop=(k == 7))
                    P.act(sqb[c][:, 0:n], pc[c][:, 0:n], AF.Square)
                    P.mm(pss[:, 0:n], ones_b[:], sqb[c][:, 0:n], start=(c == 0), stop=(c == 1))
                P.ts('dve', rst[:, 0:n], pss[:, 0:n], 1.0 / 256.0, LN_EPS, ALU.mult, ALU.add)
                P.act(rst[:, 0:n], rst[:, 0:n], AF.Sqrt)
                P.add('dve', lambda e, n=n: e.reciprocal(rst[:, 0:n], rst[:, 0:n]), [rst[:, 0:n]], [rst[:, 0:n]])
                for c in range(2):
                    P.stt('dve', cqn[:, c, t0:t0 + n], pc[c][:, 0:n], pv[:, 2 + c:3 + c], rst[:, 0:n], ALU.mult, ALU.mult)
                pk = PS_T.get()
                pss = PS_B.get()
                for k in range(8):
                    P.mm(pk[:, 0:n], wv[:, k, 256:384], uT[:, k, t0:t0 + n], start=(k == 0), stop=(k == 7))
                P.act(sqb[0][:, 0:n], pk[:, 0:n], AF.Square)
                P.mm(pss[:, 0:n], ones_b[:], sqb[0][:, 0:n])
                P.ts('dve', rst[:, 0:n], pss[:, 0:n], 1.0 / 128.0, LN_EPS, ALU.mult, ALU.add)
                P.act(rst[:, 0:n], rst[:, 0:n], AF.Sqrt)
                P.add('dve', lambda e, n=n: e.reciprocal(rst[:, 0:n], rst[:, 0:n]), [rst[:, 0:n]], [rst[:, 0:n]])
                P.stt('dve', ckvn[:, t0:t0 + n], pk[:, 0:n], pv[:, 4:5], rst[:, 0:n], ALU.mult, ALU.mult)
                pa = PS_T.get()
                pb_ = PS_T.get()
                for k in range(8):
                    P.mm(pa[64:96, 0:n], wv[:, k, 384:416], uT[:, k, t0:t0 + n], start=(k == 0), stop=(k == 7))
                for k in range(8):
                    P.mm(pb_[64:96, 0:n], wv[:, k, 416:448], uT[:, k, t0:t0 + n], start=(k == 0), stop=(k == 7))
                rope_apply(kro, pa, pb_, t0, n)
            for h in range(4):
                M = 65 if h % 2 == 0 else 128
                vo = 0 if h % 2 == 0 else 64
                P.memset('dve', vm[:], 0.0)
                oc = 64 if h % 2 == 0 else 0
                P.memset('dve', vm[:, :, oc:oc + 1], 1.0)
                for (t0, n) in TBS:
                    pa = PS_T.get()
                    pb_ = PS_T.get()
                    for k in range(2):
                        P.mm(pa[0:96, 0:n], wuq[:, k, 128 * h:128 * h + 96], cqn[:, k, t0:t0 + n], start=(k == 0), stop=(k == 1))
                    for k in range(2):
                        P.mm(pb_[64:96, 0:n], wuq[:, k, 128 * h + 96:128 * h + 128], cqn[:, k, t0:t0 + n], start=(k == 0), stop=(k == 1))
                    P.copy('act', qm[0:64, t0:t0 + n], pa[0:64, 0:n])
                    rope_apply(qm, pa, pb_, t0, n)
                    pk = PS_T.get()
                    P.mm(pk[0:64, 0:n], wukv[:, 128 * h:128 * h + 64], ckvn[:, t0:t0 + n])
                    P.copy('act', km[0:64, t0:t0 + n], pk[0:64, 0:n])
                    P.copy('dve', km[64:96, t0:t0 + n], kro[64:96, t0:t0 + n])
                for g0 in range(0, 18, 8):
                    gn = min(8, 18 - g0)
                    pvv = PS_T.get()
                    for tc in range(g0, g0 + gn):
                        P.mm(pvv[:, 64 * (tc - g0):64 * (tc - g0) + 64], ckvn[:, 128 * tc:128 * tc + 128],
                             wukv[:, 128 * h + 64:128 * h + 128])
                    P.copy(ev_eng(), vm[:, g0:g0 + gn, vo:vo + 64], pvv[:, 0:64 * gn].rearrange("p (a d) -> p a d", d=64))
                for (t0, n) in TBS:
                    seq = list(range(18)) if t0 < SEQ else [16, 17]
                    po = PS_A.get()
                    def sc_(i):
                        ps = PS_T.get()
                        P.mm(ps[:, 0:n], km[0:96, 128 * i:128 * i + 128], qm[0:96, t0:t0 + n])
                        return ps
                    pq = [sc_(seq[0]), sc_(seq[1])]
                    for idx, i in enumerate(seq):
                        ps = pq.pop(0)
                        if idx + 2 < len(seq):
                            pq.append(sc_(seq[idx + 2]))
                        pt = nextpt()
                        P.act(pt[:, 0:n], ps[:, 0:n], AF.Exp, scale=MLA_SCALE)
                        P.mm(po[0:M, 0:n], vm[:, i, 0:M], pt[:, 0:n], start=(idx == 0), stop=(idx == len(seq) - 1))
                    finalize_attn(po, h, yT[3], t0, n)


        u2tok = None
        if stage >= 5:
            AR.off = mark_y
            merged = AR.view([128, 8, T], BF16)
            wbr = AR.view([128, 2, 4, 128], BF16)
            sg = [AR.view([128, 512], F32) for _ in range(2)]
            macc = AR.view([128, 512], F32)
            mt = AR.view([128, 512], F32)
            for jd in range(8):
                wg_ = nextw()
                for nb in range(4):
                    P.dma('pool', wg_[:, :, 128 * nb:128 * nb + 128],
                          w_in[:, C_GT + 1024 * nb + 128 * jd:C_GT + 1024 * nb + 128 * jd + 128].rearrange("(k p) n -> p k n", p=128))
                    P.dma('pool', wbr[:, :, nb, :],
                          w_br[256 * nb:256 * nb + 256, 128 * jd:128 * jd + 128].rearrange("(k p) n -> p k n", p=128))
                for (t0, n) in TBS:
                    for nb in range(4):
                        pg = PS_T.get()
                        for k in range(8):
                            P.mm(pg[:, 0:n], wg_[:, k, 128 * nb:128 * nb + 128], uT[:, k, t0:t0 + n], start=(k == 0), stop=(k == 7))
                        pp = PS_T.get()
                        for k in range(2):
                            P.mm(pp[:, 0:n], wbr[:, k, nb, :], yT[nb][:, k, t0:t0 + n], start=(k == 0), stop=(k == 1))
                        s_ = sg[nb % 2]
                        P.act(s_[:, 0:n], pg[:, 0:n], AF.Sigmoid)
                        if nb == 0:
                            P.tt('dve', macc[:, 0:n], s_[:, 0:n], pp[:, 0:n], ALU.mult)
                        elif nb < 3:
                            P.tt('dve', mt[:, 0:n], s_[:, 0:n], pp[:, 0:n], ALU.mult)
                            P.tt('dve', macc[:, 0:n], macc[:, 0:n], mt[:, 0:n], ALU.add)
                        else:
                            P.tt('dve', mt[:, 0:n], s_[:, 0:n], pp[:, 0:n], ALU.mult)
                            P.tt('dve', merged[:, jd, t0:t0 + n], macc[:, 0:n], mt[:, 0:n], ALU.add)
            AR.off = mark_u
            u2tok = AR.view([128, 18, D], BF16)
            assert AR.off <= mark_y
            AR.off = mark_y + 8 * T * 2 + 64
            wo0 = nextw()
            wo1 = nextw()
            P.dma('pool', wo0[:, :, 0:512], w_out[:, 0:512].rearrange("(k p) n -> p k n", p=128))
            P.dma('pool', wo1[:, :, 0:512], w_out[:, 512:1024].rearrange("(k p) n -> p k n", p=128))
            wrt = AR.view([128, 8, NE], F32)
            P.dma('sp', wrt[:], w_rt.rearrange("(k p) n -> p k n", p=128))
            u2b = AR.view([128, 8, 512], BF16)
            lgT = AR.view([NE, T], F32)
            mark_m = AR.off
            AR.off = 0
            hbk = [AR.view([128, 8, 512], F32)]
            zts = [AR.view([128, 8, 512], F32)]
            AR.off = mark_m
            zts.append(AR.view([128, 8, 512], F32))
            for bi, (t0, n) in enumerate(TBS):
                x = 0 if t0 < SEQ else 1
                hb_ = hbk[0]
                zt = zts[bi % 2]
                P.dma('sp', hb_[:, :, 0:n], hsrc[:, t0:t0 + n].rearrange("(k p) n -> p k n", p=128))
                for jo in range(8):
                    po = PS_T.get()
                    for k in range(8):
                        P.mm(po[:, 0:n], (wo0 if jo < 4 else wo1)[:, k, 128 * (jo % 4):128 * (jo % 4) + 128], merged[:, k, t0:t0 + n], start=(k == 0), stop=(k == 7))
                    P.act(hb_[:, jo, 0:n], hb_[:, jo, 0:n], AF.Identity, scale=float(DN_ALPHA))
                    P.stt('dve', zt[:, jo, 0:n], po[:, 0:n], modT[:, G1 + jo, x:x + 1], hb_[:, jo, 0:n], ALU.mult, ALU.add)

                def after_ln(j, ap, t0=t0, n=n, x=x):
                    P.dma('sp', h1T[128 * j:128 * j + 128, t0:t0 + n], ap)
                    P.act(zt[:, j, 0:n], ap, AF.Identity, bias=modT[:, SH2 + j, x:x + 1], scale=ops2[:, j, x:x + 1])
                    P.copy('act', u2b[:, j, 0:n], zt[:, j, 0:n])
                ln_block(P, PS_A, zt, n, ln1s[:, 0, :], ln1s[:, 1, :], after_ln, ones_f, lntmp, ['dve', 'pool'])
                pl = PS_B.get()
                for k in range(8):
                    P.mm(pl[0:NE, 0:n], wrt[:, k, :], zt[:, k, 0:n], start=(k == 0), stop=(k == 7))
                P.copy('dve', lgT[:, t0:t0 + n], pl[0:NE, 0:n])
                for q in range(n // 128):
                    tc = t0 // 128 + q
                    for k in range(8):
                        P.tr(pstb[:, 128 * k:128 * k + 128], u2b[:, k, 128 * q:128 * q + 128], idb[:])
                    P.copy('act', u2tok[:, tc, :], pstb[:, :])

        if stage >= 6:
            AR.off = mark_y
            mask = AR.view([128, 18, NE], F32)
            maskb = AR.view([128, 18, NE], BF16)
            cs = AR.view([128, 18, NE], F32)
            csm = AR.view([128, 18, NE], F32)
            ffn_mark = AR.off
            affT = AR.view([NE, T], F32)
            maskT = AR.view([NE, T], F32)
            maskTb = AR.view([NE, T], BF16)
            gmT = AR.view([NE, T], F32)
            m8 = AR.view([NE, 8], F32)
            m8c = AR.view([NE, 8], F32)
            rsm = AR.view([NE, 512], F32)
            assert AR.off <= mark_m - NE * 0 - T * 4
            P.act(affT[:], lgT[:], AF.Exp)
            for (t0, n) in TBS:
                ps = PS_T.get()
                P.mm(ps[0:NE, 0:n], ones_f[0:NE, 0:NE], affT[:, t0:t0 + n])
                P.add('dve', lambda e, ps=ps, n=n: e.reciprocal(rsm[:, 0:n], ps[0:NE, 0:n]), [ps[0:NE, 0:n]], [rsm[:, 0:n]])
                P.tt('dve', affT[:, t0:t0 + n], affT[:, t0:t0 + n], rsm[:, 0:n], ALU.mult)
            if sub >= 1:
                mark_r = AR.off
                AR.off = 0
                wk = AR.view([NE, SEQ], F32)
                wkc = AR.view([NE, LC], F32)
                csT = AR.view([NE, T], F32)
                P.copy('dve', wk[:], affT[:, 0:SEQ])
                P.copy('dve', wkc[:], affT[:, SEQ:T])
                for r in range(CAP // 8):
                    P.add('dve', lambda e: e.max(m8[:], wk[:]), [wk[:]], [m8[:]])
                    if r < CAP // 8 - 1:
                        P.add('dve', lambda e: e.match_replace(wk[:], m8[:], wk[:], -1.0), [wk[:], m8[:]], [wk[:]])
                for r in range(CAPC // 8):
                    P.add('dve', lambda e: e.max(m8c[:], wkc[:]), [wkc[:]], [m8c[:]])
                    if r < CAPC // 8 - 1:
                        P.add('dve', lambda e: e.match_replace(wkc[:], m8c[:], wkc[:], -1.0), [wkc[:], m8c[:]], [wkc[:]])
            if sub >= 2:
                P.ts('dve', maskT[:, 0:SEQ], affT[:, 0:SEQ], m8[:, 7:8], None, ALU.is_ge)
                P.ts('dve', maskT[:, SEQ:T], affT[:, SEQ:T], m8c[:, 7:8], None, ALU.is_ge)
                P.tt('dve', gmT[:], maskT[:], affT[:], ALU.mult)
                P.dma('sp', gmT_o[:, :], gmT[:])
            if sub >= 3:
                import os
                S3 = float(os.environ.get('SUB3', '9'))
                if S3 >= 0.1:
                    P.copy('dve', maskTb[:], maskT[:])
                pmk = PS_T.get()
                if S3 >= 0.2:
                    for tc in range(18):
                        P.mm(pmk[:, NE * tc:NE * tc + NE], maskTb[:, 128 * tc:128 * tc + 128], idb[0:NE, 0:NE])
                if S3 >= 0.3:
                    P.copy('dve', mask[:], pmk[:, 0:18 * NE].rearrange("p (a b) -> p a b", b=NE))
                if S3 >= 0.4:
                    P.copy('dve', maskb[:], mask[:])
                for tc in range(18 if S3 >= 2 else 0):
                    base = 0 if tc < 16 else 16
                    pc_ = PS_B.get()
                    for c2 in range(base, tc):
                        P.mm(pc_[:, 0:NE], ones_b[:], maskb[:, c2, :], start=(c2 == base), stop=False)
                    P.mm(pc_[:, 0:NE], ust[:], maskb[:, tc, :], start=(tc == base), stop=True)
                    P.ts('dve', cs[:, tc, :], pc_[:, 0:NE], 0.0 if tc < 16 else float(CAP), None, ALU.add)
                    if S3 < 3:
                        continue
                    pt_ = PS_T.get()
                    for c2 in range(base, tc):
                        P.mm(pt_[0:NE, 0:128], maskb[:, c2, :], ones_b[:], start=(c2 == base), stop=False)
                    P.mm(pt_[0:NE, 0:128], maskb[:, tc, :], ust[:], start=(tc == base), stop=True)
                    P.ts('dve', csT[:, 128 * tc:128 * tc + 128], pt_[0:NE, 0:128], 0.0 if tc < 16 else float(CAP), None, ALU.add)
                if S3 >= 3:
                    P.dma('sp', csT_o[:, :], csT[:])
            if sub >= 4:
                AR.off = ffn_mark
                wdb = AR.view([128, 16, D], BF16)
                actT = AR.view([128, 16, NSLOT], BF16)
                silb = [AR.view([128, NSLOT], F32) for _ in range(2)]
                yob = [AR.view([128, D], BF16) for _ in range(2)]
                wub = [wring[2], AR.view([128, 8, 512], BF16)]
                wgb = [wring[0], wring[1]]
                P.tt('dve', csm[:], cs[:], mask[:], ALU.mult)
                P.tt('dve', csm[:], csm[:], mask[:], ALU.add)
                P.ts('dve', csm[:], csm[:], -1.0, None, ALU.add)
                AR.off = 0
                selL = [AR.view([128, 16, CAP], BF16) for _ in range(2)]
                selC = [AR.view([128, 2, CAPC], BF16) for _ in range(2)]
                xs = [AR.view([128, 8, NSLOT], BF16) for _ in range(2)]
                assert AR.off <= mark_u
                for e_ in range(NE):
                    sl, sc_, x_ = selL[e_ % 2], selC[e_ % 2], xs[e_ % 2]
                    P.tt('dve', sl[:], iof[:, 0:CAP].unsqueeze(1).to_broadcast([128, 16, CAP]),
                         csm[:, 0:16, e_:e_ + 1].to_broadcast([128, 16, CAP]), ALU.is_equal)
                    P.tt('dve', sc_[:], iof[:, CAP:NSLOT].unsqueeze(1).to_broadcast([128, 2, CAPC]),
                         csm[:, 16:18, e_:e_ + 1].to_broadcast([128, 2, CAPC]), ALU.is_equal)
                    for j in range(8):
                        px = PS_T.get()
                        for tc in range(16):
                            P.mm(px[:, 0:CAP], u2tok[:, tc, 128 * j:128 * j + 128], sl[:, tc, :], start=(tc == 0), stop=(tc == 15))
                        for tc in range(16, 18):
                            P.mm(px[:, CAP:NSLOT], u2tok[:, tc, 128 * j:128 * j + 128], sc_[:, tc - 16, :], start=(tc == 16), stop=(tc == 17))
                        P.copy(ev_eng(), x_[:, j, :], px[:, 0:NSLOT])
                    for pc in range(4):
                        wgp = wring[(2 * pc) % 3] if False else wgb[pc % 2]
                        wup = wub[pc % 2]
                        P.dma('pool', wgp[:], wg_all[l, e_, :, 512 * pc:512 * pc + 512].rearrange("(k p) n -> p k n", p=128))
                        P.dma('pool', wup[:], wu_all[l, e_, :, 512 * pc:512 * pc + 512].rearrange("(k p) n -> p k n", p=128))
                        for f in range(4):
                            pa = PS_T.get()
                            pu = PS_T.get()
                            for k in range(8):
                                P.mm(pa[:, 0:NSLOT], wgp[:, k, 128 * f:128 * f + 128], x_[:, k, :], start=(k == 0), stop=(k == 7))
                            for k in range(8):
                                P.mm(pu[:, 0:NSLOT], wup[:, k, 128 * f:128 * f + 128], x_[:, k, :], start=(k == 0), stop=(k == 7))
                            s_ = silb[f % 2]
                            P.act(s_[:], pa[:, 0:NSLOT], AF.Silu)
                            P.tt('dve', actT[:, 4 * pc + f, :], s_[:], pu[:, 0:NSLOT], ALU.mult)
                    for pc in range(4):
                        P.dma('pool', wdb[:, 4 * pc:4 * pc + 4, :], wd_all[l, e_, 512 * pc:512 * pc + 512, :].rearrange("(k p) n -> p k n", p=128))
                    for qi, (s0, sn) in enumerate(((0, 128), (128, 128), (256, 32))):
                        y_ = yob[qi % 2]
                        for hf in range(2):
                            po = PS_A.get()
                            for f in range(16):
                                P.mm(po[0:sn, 0:512], actT[:, f, s0:s0 + sn], wdb[:, f, 512 * hf:512 * hf + 512], start=(f == 0), stop=(f == 15))
                            P.copy('act' if hf == 0 else 'dve', y_[0:sn, 512 * hf:512 * hf + 512], po[0:sn, 0:512])
                        P.dma('sp', ye[e_, s0:s0 + sn, :], y_[0:sn, :])
    AR.off = 0
    emit_prologue(P, dict(AR=AR, PS_T=PS_T, PS_A=PS_A, PS_B=PS_B, ones_f=ones_f, lntmp=lntmp, pix=pix, hT_in=h1T, ye=ye,
                          csT_p=csT_o, gmT_p=gmT_o, modp=mod_scr[nl - 1], ln2p=ln2_all[nl - 1], eye16rep=eye16rep, hdst=out_d), final=True)
    P.finish(outs)
    P.emit()
    return nc, P


def host_inputs_F(inp, b, consts):
    perm_a = np.concatenate([np.arange(0, 32, 2), np.arange(1, 32, 2)])
    perm_b = np.concatenate([np.arange(1, 32, 2), np.arange(0, 32, 2)])
    m = {}
    w_in = inp['w_in']
    kr = w_in[:, :, 1664:1696]
    m['w_in'] = np.ascontiguousarray(np.concatenate([w_in[:, :, :1664], kr[:, :, perm_a], kr[:, :, perm_b], w_in[:, :, 1696:]], axis=2))
    m['cs2'] = np.ascontiguousarray(np.stack([vec_pj(inp['c'][b]), vec_pj(inp['c_ctx'])], axis=-1))
    m['wmod'] = inp['w_mod']
    m['bmod'] = np.ascontiguousarray(np.stack([vec_pj(inp['b_mod'][l]) for l in range(DEPTH)]))
    m['nab'] = np.ascontiguousarray(inp['na_rpb'][:, :, consts['_na_ri'], consts['_na_ci']])
    m['pool_w'] = inp['pool_w']
    pvv = np.zeros((DEPTH, 128, 8), np.float32)
    for l in range(DEPTH):
        pvv[l, :, 0:2] = vec_pj(inp['pool_scale'][l]); pvv[l, :, 2:4] = vec_pj(inp['mla_q_norm'][l]); pvv[l, :, 4:5] = vec_pj(inp['mla_kv_norm'][l])
    m['pvec'] = pvv
    wuq = inp['mla_w_uq'].reshape(DEPTH, 256, 4, 96)
    m['w_uq'] = np.ascontiguousarray(np.concatenate([wuq[..., :64], wuq[..., 64:][..., perm_a], wuq[..., 64:][..., perm_b]], axis=-1).reshape(DEPTH, 256, 512))
    m['w_ukv'] = inp['mla_w_ukv']
    m['w_br'] = np.ascontiguousarray(inp['w_branch'].reshape(DEPTH, 1024, 1024))
    m['w_out'] = inp['w_out']
    m['ln1'] = np.ascontiguousarray(np.stack([np.stack([vec_pj(inp['ln1_g'][l]), vec_pj(inp['ln1_b'][l])], axis=1) for l in range(DEPTH)]))
    m['ln2'] = np.ascontiguousarray(np.stack([np.stack([vec_pj(inp['ln2_g'][l]), vec_pj(inp['ln2_b'][l])], axis=1) for l in range(DEPTH)]))
    m['w_rt'] = inp['w_router']
    m['wg'] = inp['w_gate']; m['wu'] = inp['w_up']; m['wd'] = inp['w_down']
    m['eye16rep'] = np.kron(np.eye(NE), np.ones((1, 128))).astype(ml_dtypes.bfloat16)
    for k in ('nam', 'ropeC', 'ropeS', 'dftc', 'ident_f', 'ident_b', 'ustri', 'iota_f', 'pidx', 'lval', 'jrow', 'pooledge', 'poolinvw'):
        m[k] = consts[k]
    m['hT_in'] = np.ascontiguousarray(np.concatenate([inp['x'][b].T, inp['ctx'][b].T], axis=1))
    return m


_PROGS = {}


def kernel(**inp):
    inp = {k: np.asarray(v) for k, v in inp.items()}
    consts = host_constants()
    NCORE = 8
    if 'F' not in _PROGS:
        _PROGS['F'] = build_F(DEPTH)[0]
    maps = []
    shared = None
    for b in range(NCORE):
        if shared is None:
            shared = host_inputs_F(inp, b, consts)
            m = shared
        else:
            m = dict(shared)
            m['cs2'] = np.ascontiguousarray(np.stack([vec_pj(inp['c'][b]), vec_pj(inp['c_ctx'])], axis=-1))
            m['hT_in'] = np.ascontiguousarray(np.concatenate([inp['x'][b].T, inp['ctx'][b].T], axis=1))
        maps.append(m)
    res = run_bass_kernel_spmd(_PROGS['F'], maps, core_ids=list(range(NCORE))).results
    out = np.stack([np.ascontiguousarray(res[b]['out'].T) for b in range(NCORE)], axis=0)
    return out.astype(np.float32)
```

```python
import contextlib
import numpy as np
import concourse.bass as bass
import concourse.mybir as mybir
from concourse.bass_utils import run_bass_kernel_spmd

F32 = mybir.dt.float32
F32R = mybir.dt.float32r
BF16 = mybir.dt.bfloat16
I32 = mybir.dt.int32
U32 = mybir.dt.uint32
AF = mybir.ActivationFunctionType
ALU = mybir.AluOpType
AX = mybir.AxisListType

_DSZ = {F32: 4, F32R: 4, BF16: 2, I32: 4, U32: 4, mybir.dt.float16: 2,
        mybir.dt.uint16: 2, mybir.dt.int16: 2, mybir.dt.uint8: 1, mybir.dt.int8: 1}


def _region(ap):
    esz = _DSZ[ap.dtype]
    steps = ap.ap
    off = int(ap.offset)
    sp = str(ap.space)
    if 'SB' in sp or 'PSUM' in sp:
        pstep = steps[0][0]
        if pstep == 0:
            pstep = 1 << 40
        plo = off // pstep
        phi = plo + steps[0][1]
        flo = off % pstep
        ext = 0
        for st, cnt in steps[1:]:
            ext += abs(st) * (cnt - 1)
        return (ap.name, plo, phi, flo * esz, (flo + ext + 1) * esz)
    ext = 0
    for st, cnt in steps:
        ext += abs(st) * (cnt - 1)
    return (ap.name, 0, 1, off * esz, (off + ext + 1) * esz)


def _ovl(a, b):
    return a[1] < b[2] and b[1] < a[2] and a[3] < b[4] and b[3] < a[4]


def _cov(a, b):
    return a[1] <= b[1] and a[2] >= b[2] and a[3] <= b[3] and a[4] >= b[4]


class Prog:
    CENG = ('pe', 'act', 'dve', 'pool')
    ENGS = ('pe', 'act', 'dve', 'pool', 'sp')

    def __init__(self, nc, ring_sizes=None):
        self.nc = nc
        self.es = contextlib.ExitStack()
        self.ops = {e: [] for e in self.ENGS}
        self.writers = {}
        self.readers = {}
        self.known = {e: {} for e in self.ENGS}
        self.ring = ring_sizes or {'sp': 24, 'pool': 12, 'act': 8}
        self.ndma = {k: 0 for k in self.ring}
        self.untracked = set()
        self.nops = 0
        self.order = []
        self.use_block = True

    def sbuf(self, name, shape, dtype):
        return self.es.enter_context(self.nc.sbuf_tensor(name, list(shape), dtype))

    def psum(self, name, shape, dtype):
        return self.es.enter_context(self.nc.psum_tensor(name, list(shape), dtype))

    def dram_in(self, name, shape, dtype):
        t = self.nc.dram_tensor(name, list(shape), dtype, kind="ExternalInput")
        self.untracked.add(name)
        return t.ap()

    def dram_out(self, name, shape, dtype):
        return self.nc.dram_tensor(name, list(shape), dtype, kind="ExternalOutput").ap()

    def dram_tmp(self, name, shape, dtype):
        return self.nc.dram_tensor(name, list(shape), dtype, kind="Internal").ap()

    def add(self, eng, fn, reads, writes, dma=False):
        self.nops += 1
        deps = {}

        def need(ev):
            k, v = ev
            if deps.get(k, -1) < v:
                deps[k] = v

        rr = [_region(a) for a in reads if a is not None and a.name not in self.untracked]
        ww = [_region(a) for a in writes]
        for r in rr:
            for (w, ev) in self.writers.get(r[0], ()):
                if _ovl(r, w):
                    need(ev)
        for r in ww:
            for (w, ev) in self.writers.get(r[0], ()):
                if _ovl(r, w):
                    need(ev)
            for (w, ev) in self.readers.get(r[0], ()):
                if _ovl(r, w):
                    need(ev)
        idx = len(self.ops[eng])
        if dma:
            ns = self.ring[eng]
            k = self.ndma[eng]
            self.ndma[eng] += 1
            slot = k % ns
            val = 16 * (k // ns + 1)
            if val > 16:
                need((('d', eng, slot), val - 16))
            event = (('d', eng, slot), val)
        else:
            event = (('c', eng), idx)
        waits = []
        kn = self.known[eng]
        for k, v in deps.items():
            if k == ('c', 'pe') and eng == 'pe':
                continue
            if kn.get(k, -1) >= v:
                continue
            kn[k] = v
            waits.append((k, v))
            if k[0] == 'c':
                dop = self.ops[k[1]][v]
                dop['marked'] = True
                for k2, v2 in dop['kn'].items():
                    if kn.get(k2, -1) < v2:
                        kn[k2] = v2
        op = dict(fn=fn, waits=waits, event=event, marked=False, dma=dma, eng=eng,
                  kn={k: v for k, v in kn.items() if k[0] == 'c'})
        self.ops[eng].append(op)
        self.order.append(op)
        for r in ww:
            lst = self.writers.setdefault(r[0], [])
            lst[:] = [(w, ev) for (w, ev) in lst if not _cov(r, w)]
            lst.append((r, event))
            rl = self.readers.get(r[0])
            if rl:
                rl[:] = [(w, ev) for (w, ev) in rl if not _cov(r, w)]
        for r in rr:
            lst = self.readers.setdefault(r[0], [])
            lst[:] = [(w, ev) for (w, ev) in lst if not (ev[0] == event[0] and _cov(r, w))]
            lst.append((r, event))
        return op

    def mm(self, out, lhsT, rhs, start=True, stop=True, **kw):
        self.add('pe', lambda e: e.matmul(out, lhsT, rhs, start=start, stop=stop, **kw),
                 [lhsT, rhs], [out])

    def tr(self, out, in_, ident):
        self.add('pe', lambda e: e.transpose(out, in_, ident), [in_, ident], [out])

    def act(self, out, in_, func, bias=None, scale=None, accum_out=None, eng='act'):
        kw = {}
        rd = [in_]
        wr = [out]
        if bias is not None:
            kw['bias'] = bias
            if not isinstance(bias, (int, float)):
                rd.append(bias)
        if scale is not None:
            kw['scale'] = scale
            if not isinstance(scale, (int, float)):
                rd.append(scale)
        if accum_out is not None:
            kw['accum_out'] = accum_out
            wr.append(accum_out)
        self.add('act', lambda e: e.activation(out, in_, func, **kw), rd, wr)

    def tt(self, eng, out, in0, in1, op):
        self.add(eng, lambda e: e.tensor_tensor(out, in0, in1, op), [in0, in1], [out])

    def ts(self, eng, out, in0, s1, s2, op0, op1=None, accum_out=None):
        rd = [in0]
        if not isinstance(s1, (int, float)):
            rd.append(s1)
        if s2 is not None and not isinstance(s2, (int, float)):
            rd.append(s2)
        wr = [out]
        kw = {}
        if op1 is not None:
            kw['op1'] = op1
        if accum_out is not None:
            kw['accum_out'] = accum_out
            wr.append(accum_out)
        self.add(eng, lambda e: e.tensor_scalar(out, in0, s1, s2, op0, **kw), rd, wr)

    def stt(self, eng, out, in0, scalar, in1, op0, op1):
        rd = [in0, in1]
        if not isinstance(scalar, (int, float)):
            rd.append(scalar)
        self.add(eng, lambda e: e.scalar_tensor_tensor(out, in0, scalar, in1, op0, op1), rd, [out])

    def copy(self, eng, out, in_):
        if eng == 'act':
            self.add('act', lambda e: e.activation(out, in_, AF.Identity), [in_], [out])
        else:
            self.add(eng, lambda e: e.tensor_copy(out, in_), [in_], [out])

    def memset(self, eng, out, val):
        self.add(eng, lambda e: e.memset(out, val), [], [out])

    def reduce(self, eng, out, in_, op, axis=AX.X):
        self.add(eng, lambda e: e.tensor_reduce(out, in_, axis, op), [in_], [out])

    def dma(self, eng, out, in_, **kw):
        self.add(eng, lambda e: e.dma_start(out, in_, **kw), [in_], [out], dma=True)

    def finish(self, out_aps):
        self.add('sp', lambda e: e.nop(), list(out_aps), [])

    def emit(self):
        nc = self.nc
        for e in self.CENG:
            c = 0
            for op in self.ops[e]:
                if op['marked']:
                    c += 1
                op['count'] = c
        csem = {e: self.es.enter_context(nc.semaphore("s_" + e)) for e in self.CENG}
        rsem = {r: [self.es.enter_context(nc.semaphore("d_%s%d" % (r, i))) for i in range(n)]
                for r, n in self.ring.items() if self.ndma[r] > 0}
        ops = self.ops
        stats = {e: [len(ops[e]), sum(len(o['waits']) for o in ops[e])] for e in self.ENGS}
        self.stats = stats

        def run(ename, eng):
            for op in ops[ename]:
                for (k, v) in op['waits']:
                    if k[0] == 'c':
                        eng.wait_ge(csem[k[1]], ops[k[1]][v]['count'])
                    else:
                        eng.wait_ge(rsem[k[1]][k[2]], v)
                inst = op['fn'](eng)
                if op['dma']:
                    k = op['event'][0]
                    inst.then_inc(rsem[k[1]][k[2]], 16)
                elif op['marked']:
                    inst.then_inc(csem[ename], 1)

        if not self.use_block:
            engs = {'pe': nc.tensor, 'act': nc.scalar, 'dve': nc.vector, 'pool': nc.gpsimd, 'sp': nc.sync}
            for op in self.order:
                ename = op['eng']
                eng = engs[ename]
                for (k, v) in op['waits']:
                    if k[0] == 'c':
                        eng.wait_ge(csem[k[1]], ops[k[1]][v]['count'])
                    else:
                        eng.wait_ge(rsem[k[1]][k[2]], v)
                inst = op['fn'](eng)
                if op['dma']:
                    k = op['event'][0]
                    inst.then_inc(rsem[k[1]][k[2]], 16)
                elif op['marked']:
                    inst.then_inc(csem[ename], 1)
            self.es.close()
            return nc
        with nc.Block() as block:
            @block.tensor
            def _(eng):
                run('pe', eng)

            @block.scalar
            def _(eng):
                run('act', eng)

            @block.vector
            def _(eng):
                run('dve', eng)

            @block.gpsimd
            def _(eng):
                run('pool', eng)

            @block.sync
            def _(eng):
                run('sp', eng)
        self.es.close()
        return nc

import ml_dtypes

D = 1024
SEQ = 2048
LC = 256
T = SEQ + LC
DEPTH = 4
NE = 16
CAP = 256
CAPC = 32
NSLOT = CAP + CAPC
FF = 2048
DN_ALPHA = (2 * DEPTH) ** 0.25
LN_EPS = 1e-5
NA_SCALE = 64 ** -0.5
MLA_SCALE = 96 ** -0.5
NEGM = -30000.0
TBS = [(0, 512), (512, 512), (1024, 512), (1536, 512), (2048, 256)]
C_QA, C_KA, C_VA, C_UP, C_UF, C_CQ, C_CKV, C_KRA, C_KRB, C_GT = 0, 256, 512, 768, 1024, 1280, 1536, 1664, 1696, 1728
WIN_COLS = 1728 + 4096
NSTRIP = 22 * 64
XPW = 8 + SEQ + 16 + LC + 8
XP_L = 8
XP_C = 8 + SEQ + 16


def host_constants():
    c = {}
    p = np.arange(128)
    a = p // 64
    kc = p % 64
    dd = np.arange(22)
    d = 10 - dd
    qc = np.arange(64)
    dr = a[:, None] + d[None, :]
    c0 = np.clip(qc - 8, 0, 48)
    colin = (kc[:, None] >= c0[None, :]) & (kc[:, None] < c0[None, :] + 16)
    mall = np.where((np.abs(dr) <= 7)[:, :, None] & colin[:, None, :], 0.0, NEGM)
    mint = np.where(((dr >= -4) & (dr <= 3))[:, :, None] & colin[:, None, :], 0.0, NEGM)
    c['nam'] = np.stack([mall.reshape(128, NSTRIP), mint.reshape(128, NSTRIP)]).astype(np.float32)
    ri = np.clip(dr + 7, 0, 14)
    ci = np.clip(kc[:, None] - qc[None, :], -15, 15) + 15
    c['_na_ri'] = np.broadcast_to(ri[:, :, None], (128, 22, 64)).reshape(128, NSTRIP)
    c['_na_ci'] = np.broadcast_to(ci[:, None, :], (128, 22, 64)).reshape(128, NSTRIP)
    n_freq = 8
    inv = (10000.0 ** (-np.arange(n_freq, dtype=np.float32) / n_freq)).astype(np.float32)
    t = np.arange(SEQ)
    row = (t // 64).astype(np.float32)
    col = (t % 64).astype(np.float32)
    ang = np.concatenate([row[:, None] * inv, col[:, None] * inv], axis=-1)
    cs, sn = np.cos(ang).T, np.sin(ang).T
    c['ropeC'] = np.concatenate([cs, cs], 0).astype(ml_dtypes.bfloat16)
    c['ropeS'] = np.concatenate([-sn, sn], 0).astype(ml_dtypes.bfloat16)
    k = np.arange(64)
    th = 2 * np.pi * np.outer(k, k) / 64.0
    cc, sc = np.cos(th), np.sin(th)
    bd = np.zeros((128, 256), np.float32)
    for g in range(2):
        bd[64 * g:64 * g + 64, 64 * g:64 * g + 64] = -cc
        bd[64 * g:64 * g + 64, 128 + 64 * g:128 + 64 * g + 64] = sc
    c['dftc'] = bd.astype(ml_dtypes.bfloat16)
    c['ident_f'] = np.eye(128, dtype=np.float32)
    c['ident_b'] = np.eye(128).astype(ml_dtypes.bfloat16)
    c['ustri'] = np.triu(np.ones((128, 128)), 1).astype(ml_dtypes.bfloat16)
    c['iota_f'] = np.broadcast_to(np.arange(NSLOT, dtype=np.float32)[None, :], (128, NSLOT)).copy()
    c['pidx'] = np.stack([np.arange(128), np.arange(128) + 128, np.arange(128) + 256], 1).astype(np.float32)
    c['lval'] = (128 * np.arange(16)[None, :] + np.arange(128)[:, None]).astype(np.float32)
    c['jrow'] = np.broadcast_to(np.arange(256, dtype=np.int32)[None, :], (128, 256)).copy()
    half = np.array([1, 2, 4, 8])
    pe = np.zeros((128, 2, 4, 8), np.float32)
    for ch in range(2):
        for pp in range(128):
            g = 2 * ch + pp // 64
            h = half[g]
            for reg, (L, left) in enumerate([(SEQ, True), (SEQ, False), (LC, True), (LC, False)]):
                for j in range(8):
                    tt = j if left else L - 8 + j
                    cnt = min(tt + h, L) - max(tt - h, 0)
                    pe[pp, ch, reg, j] = 1.0 / cnt
    c['pooledge'] = pe.reshape(128, 64)
    pw = np.zeros((128, 2), np.float32)
    for ch in range(2):
        for pp in range(128):
            pw[pp, ch] = 1.0 / (2 * half[2 * ch + pp // 64])
    c['poolinvw'] = pw
    return c


def vec_pj(v):
    return np.ascontiguousarray(v.reshape(-1, 128).T)


class Arena:
    def __init__(self, P, name, nbytes):
        self.t = P.sbuf(name, [128, nbytes // 4], F32)
        self.nbytes = nbytes
        self.off = 0

    def reset(self):
        self.off = 0

    def view(self, shape, dtype):
        esz = _DSZ[dtype]
        n = 1
        for s in shape[1:]:
            n *= s
        nb = (n * esz + 31) // 32 * 32
        assert self.off + nb <= self.nbytes, (self.off, nb, self.nbytes)
        v = self.t[0:shape[0], self.off // 4:(self.off + nb) // 4]
        self.off += nb
        if dtype != F32:
            v = v.bitcast(dtype)
        v = v[:, 0:n]
        if len(shape) == 3:
            v = v.rearrange("p (a b) -> p a b", a=shape[1])
        elif len(shape) == 4:
            v = v.rearrange("p (a b c) -> p a b c", a=shape[1], b=shape[2])
        return v


class PsumRing:
    def __init__(self, P, n=8, name="psb"):
        self.banks = [P.psum("%s%d" % (name, i), [128, 512], F32) for i in range(n)]
        self.i = 0

    def get(self):
        b = self.banks[self.i % len(self.banks)]
        self.i += 1
        return b


def ln_block(P, PS, zt, n, gcol, bcol, out_cb, ones_f, tmp, eng_alt):
    ps_s = PS.get()
    ps_q = PS.get()
    ones_b = tmp['ones_b']
    for j in range(8):
        zb = tmp['zb'][j % 2]
        P.copy('act' if j % 2 else 'dve', zb[:, 0:n], zt[:, j, 0:n])
        P.mm(ps_s[:, 0:n], ones_b[:], zb[:, 0:n], start=(j == 0), stop=(j == 7))
    for j in range(8):
        sq = tmp['sqb'][j % 2]
        P.act(sq[:, 0:n], zt[:, j, 0:n], AF.Square)
        P.mm(ps_q[:, 0:n], ones_b[:], sq[:, 0:n], start=(j == 0), stop=(j == 7))
    mean, rstd, nmr = tmp['mean'], tmp['rstd'], tmp['nmr']
    P.ts('dve', mean[:, 0:n], ps_s[:, 0:n], 1.0 / D, None, ALU.mult)
    P.tt('dve', nmr[:, 0:n], mean[:, 0:n], mean[:, 0:n], ALU.mult)
    P.stt('dve', rstd[:, 0:n], ps_q[:, 0:n], 1.0 / D, nmr[:, 0:n], ALU.mult, ALU.subtract)
    P.ts('dve', rstd[:, 0:n], rstd[:, 0:n], LN_EPS, None, ALU.add)
    P.act(rstd[:, 0:n], rstd[:, 0:n], AF.Sqrt)
    P.add('dve', lambda e: e.reciprocal(rstd[:, 0:n], rstd[:, 0:n]), [rstd[:, 0:n]], [rstd[:, 0:n]])
    P.stt('dve', nmr[:, 0:n], mean[:, 0:n], -1.0, rstd[:, 0:n], ALU.mult, ALU.mult)
    for j in range(8):
        P.tt('dve', zt[:, j, 0:n], zt[:, j, 0:n], rstd[:, 0:n], ALU.mult)
        P.tt('dve', zt[:, j, 0:n], zt[:, j, 0:n], nmr[:, 0:n], ALU.add)
        P.act(zt[:, j, 0:n], zt[:, j, 0:n], AF.Identity, bias=bcol[:, j:j + 1], scale=gcol[:, j:j + 1])
        out_cb(j, zt[:, j, 0:n])


def emit_prologue(P, c, final=False):
    AR, PS_T, PS_A, PS_B = c['AR'], c['PS_T'], c['PS_A'], c['PS_B']
    ones_f, lntmp, pix = c['ones_f'], c['lntmp'], c['pix']
    hT_in, ye, csT_p, gmT_p, modp_d, ln2p_d, e16_d = c['hT_in'], c['ye'], c['csT_p'], c['gmT_p'], c['modp'], c['ln2p'], c['eye16rep']
    mark = AR.off
    csb = AR.view([NE, T], BF16)
    gmb = AR.view([NE, T], BF16)
    csf = AR.view([NE, T], F32)
    e16 = AR.view([NE, NE * 128], BF16)
    modp = AR.view([128, 48, 2], F32)
    ln2s = AR.view([128, 2, 8], F32)
    GE = 4
    yeg = AR.view([128, GE, 3, D], BF16)
    stg = AR.view([128, GE, 2, 512], BF16)
    eqt = [AR.view([128, 512], BF16) for _ in range(2)]
    hb = AR.view([128, 8, 512], F32)
    zt = AR.view([128, 8, 512], F32)
    P.dma('sp', csf[:], csT_p[:, :])
    P.ts('dve', csf[:, SEQ:T], csf[:, SEQ:T], -float(CAP), None, ALU.add)
    P.copy('dve', csb[:], csf[:])
    P.dma('sp', csf[:], gmT_p[:, :])
    P.copy('dve', gmb[:], csf[:])
    P.dma('sp', e16[:], e16_d[:, :])
    P.dma('sp', modp[:], modp_d[:, :, :])
    P.dma('sp', ln2s[:], ln2p_d[:, :, :])
    G2 = 40
    for (t0, n) in TBS:
        x = 0 if t0 < SEQ else 1
        lat = t0 < SEQ
        if final and not lat:
            continue
        P.dma('sp', hb[:, :, 0:n], hT_in[:, t0:t0 + n].rearrange("(k p) n -> p k n", p=128))
        for g in range(NE // GE):
            for q in range(2):
                P.dma('sp' if q == 0 else 'act', yeg[:, :, q, :], ye[GE * g:GE * g + GE, 128 * q:128 * q + 128, :].rearrange("e p d -> p e d"))
            P.dma('sp', yeg[0:CAPC, :, 2, :], ye[GE * g:GE * g + GE, CAP:NSLOT, :].rearrange("e p d -> p e d"))
            for el in range(GE):
                e_ = GE * g + el
                pcs = PS_A.get()
                pgm = PS_A.get()
                P.mm(pcs[:, 0:n], e16[:, 128 * e_:128 * e_ + 128], csb[:, t0:t0 + n])
                P.mm(pgm[:, 0:n], e16[:, 128 * e_:128 * e_ + 128], gmb[:, t0:t0 + n])
                if lat:
                    for q in range(2):
                        et = eqt[q]
                        P.ts('dve', et[:, 0:n], pcs[:, 0:n], pix[:, q:q + 1], None, ALU.is_equal)
                        P.tt('dve', stg[:, el, q, 0:n], et[:, 0:n], pgm[:, 0:n], ALU.mult)
                else:
                    et = eqt[0]
                    P.ts('dve', et[0:CAPC, 0:n], pcs[0:CAPC, 0:n], pix[0:CAPC, 0:1], None, ALU.is_equal)
                    P.tt('dve', stg[0:CAPC, el, 0, 0:n], et[0:CAPC, 0:n], pgm[0:CAPC, 0:n], ALU.mult)
            for jo in range(8):
                po = PS_T.get()
                if lat:
                    for el in range(GE):
                        for q in range(2):
                            P.mm(po[:, 0:n], yeg[:, el, q, 128 * jo:128 * jo + 128], stg[:, el, q, 0:n],
                                 start=(el == 0 and q == 0), stop=(el == GE - 1 and q == 1))
                else:
                    for el in range(GE):
                        P.mm(po[:, 0:n], yeg[0:CAPC, el, 2, 128 * jo:128 * jo + 128], stg[0:CAPC, el, 0, 0:n],
                             start=(el == 0), stop=(el == GE - 1))
                if g == 0:
                    P.act(hb[:, jo, 0:n], hb[:, jo, 0:n], AF.Identity, scale=float(DN_ALPHA))
                    P.stt('dve', zt[:, jo, 0:n], po[:, 0:n], modp[:, G2 + jo, x:x + 1], hb[:, jo, 0:n], ALU.mult, ALU.add)
                else:
                    P.stt('dve', zt[:, jo, 0:n], po[:, 0:n], modp[:, G2 + jo, x:x + 1], zt[:, jo, 0:n], ALU.mult, ALU.add)

        def after_ln(j, ap, t0=t0, n=n, x=x):
            if final:
                P.dma('sp', c['hdst'][128 * j:128 * j + 128, t0:t0 + n], ap)
            else:
                P.dma('sp', c['hdst'][128 * j:128 * j + 128, t0:t0 + n], ap)
                P.act(c['uT'][:, j, t0:t0 + n], ap, AF.Identity, bias=c['modT'][:, j, x:x + 1], scale=c['ops1'][:, j, x:x + 1])
        ln_block(P, PS_A, zt, n, ln2s[:, 0, :], ln2s[:, 1, :], after_ln, ones_f, lntmp, ['dve', 'pool'])
    AR.off = mark


def build_D():
    nc = bass.Bass("TRN2", target_bir_lowering=False)
    P = Prog(nc, ring_sizes={'sp': 16, 'pool': 12, 'act': 4})
    c = {}
    c['hT_in'] = P.dram_in("hT_in", [D, T], F32)
    c['ye'] = P.dram_in("ye", [NE, NSLOT, D], BF16)
    c['csT_p'] = P.dram_in("csT_p", [NE, T], F32)
    c['gmT_p'] = P.dram_in("gmT_p", [NE, T], F32)
    c['modp'] = P.dram_in("modp", [128, 48, 2], F32)
    c['ln2p'] = P.dram_in("ln2p", [128, 2, 8], F32)
    c['eye16rep'] = P.dram_in("eye16rep", [NE, NE * 128], BF16)
    pidx = P.dram_in("pidx", [128, 3], F32)
    out = P.dram_out("out", [D, SEQ], F32)
    c['hdst'] = out
    c['ones_f'] = P.sbuf("ones_f", [128, 128], F32)
    c['pix'] = P.sbuf("pix", [128, 3], F32)
    c['lntmp'] = dict(sq=[P.sbuf("lnsq0", [128, 512], F32), P.sbuf("lnsq1", [128, 512], F32)], mean=P.sbuf("lnmean", [128, 512], F32),
                      rstd=P.sbuf("lnrstd", [128, 512], F32), nmr=P.sbuf("lnnmr", [128, 512], F32),
                      zb=[P.sbuf("lnzb0", [128, 512], BF16), P.sbuf("lnzb1", [128, 512], BF16)],
                      sqb=[P.sbuf("lnsqb0", [128, 512], BF16), P.sbuf("lnsqb1", [128, 512], BF16)])
    c['lntmp']['ones_b'] = P.sbuf("ones_b", [128, 128], BF16)
    P.memset('dve', c['lntmp']['ones_b'][:], 1.0)
    c['AR'] = Arena(P, "arena", 150 * 1024)
    c['PS_T'] = PsumRing(P, 4, "pst")
    c['PS_A'] = PsumRing(P, 2, "psa")
    c['PS_B'] = PsumRing(P, 1, "psbx")
    P.memset('dve', c['ones_f'][:], 1.0)
    P.dma('sp', c['pix'][:], pidx[:, :])
    emit_prologue(P, c, final=True)
    P.finish([out])
    P.emit()
    return nc, P


def build_B():
    TBB = 8 * NSLOT
    nc = bass.Bass("TRN2", target_bir_lowering=False)
    P = Prog(nc, ring_sizes={'sp': 16, 'pool': 12, 'act': 4})
    xe_in = P.dram_in("xe", [2, D, TBB], BF16)
    wg = P.dram_in("wg", [2, D, FF], F32)
    wu = P.dram_in("wu", [2, D, FF], F32)
    wd = P.dram_in("wd", [2, FF, D], F32)
    ye = P.dram_out("ye", [2, TBB, D], BF16)
    xes = P.sbuf("xes", [128, 8, TBB], BF16)
    wgs = P.sbuf("wgs", [128, 8, FF], BF16)
    wus = P.sbuf("wus", [128, 8, FF], BF16)
    wds = P.sbuf("wds", [128, 16, D], BF16)
    actT = P.sbuf("actT", [128, 16, 512], BF16)
    sil = [P.sbuf("sil%d" % i, [128, 512], F32) for i in range(2)]
    yo = [P.sbuf("yo%d" % i, [128, D], BF16) for i in range(2)]
    PS_T = PsumRing(P, 4, "pst")
    PS_A = PsumRing(P, 2, "psa")
    blocks = [(0, 512), (512, 512), (1024, 512), (1536, 512), (2048, 256)]
    yi = 0
    for e_ in range(2):
        P.dma('sp', xes[:], xe_in[e_].rearrange("(k p) n -> p k n", p=128))
        for pc in range(4):
            P.dma('pool', wgs[:, :, 512 * pc:512 * pc + 512], wg[e_, :, 512 * pc:512 * pc + 512].rearrange("(k p) n -> p k n", p=128))
            P.dma('pool', wus[:, :, 512 * pc:512 * pc + 512], wu[e_, :, 512 * pc:512 * pc + 512].rearrange("(k p) n -> p k n", p=128))
        for pc in range(4):
            P.dma('pool', wds[:, 4 * pc:4 * pc + 4, :], wd[e_, 512 * pc:512 * pc + 512, :].rearrange("(k p) n -> p k n", p=128))
        for (t0, n) in blocks:
            for f in range(16):
                pa = PS_T.get()
                pu = PS_T.get()
                for k in range(8):
                    P.mm(pa[:, 0:n], wgs[:, k, 128 * f:128 * f + 128], xes[:, k, t0:t0 + n], start=(k == 0), stop=(k == 7))
                for k in range(8):
                    P.mm(pu[:, 0:n], wus[:, k, 128 * f:128 * f + 128], xes[:, k, t0:t0 + n], start=(k == 0), stop=(k == 7))
                s_ = sil[f % 2]
                P.act(s_[:, 0:n], pa[:, 0:n], AF.Silu)
                P.tt('dve', actT[:, f, 0:n], s_[:, 0:n], pu[:, 0:n], ALU.mult)
            for m in range(n // 128):
                y_ = yo[yi % 2]
                yi += 1
                for hf in range(2):
                    po = PS_A.get()
                    for f in range(16):
                        P.mm(po[:, 0:512], actT[:, f, 128 * m:128 * m + 128], wds[:, f, 512 * hf:512 * hf + 512], start=(f == 0), stop=(f == 15))
                    P.copy('act' if hf == 0 else 'dve', y_[:, 512 * hf:512 * hf + 512], po[:, 0:512])
                P.dma('sp', ye[e_, t0 + 128 * m:t0 + 128 * m + 128, :], y_[:])
    P.finish([ye])
    P.emit()
    return nc, P


def build_A(prologue, stage=99, sub=99):
    nc = bass.Bass("TRN2", target_bir_lowering=False)
    P = Prog(nc, ring_sizes={'sp': 16, 'pool': 12, 'act': 4})
    I = {}

    def inp(name, shape, dt=F32):
        I[name] = P.dram_in(name, shape, dt)
        return I[name]
    hT_in = inp("hT_in", [D, T])
    cs2 = inp("cs2", [128, 8, 2])
    wmod = inp("wmod", [D, 6 * D])
    bmod = inp("bmod", [128, 48])
    w_in = inp("w_in", [D, WIN_COLS])
    nab = inp("nab", [4, 128, NSTRIP])
    nam = inp("nam", [2, 128, NSTRIP])
    pool_w = inp("pool_w", [4, 64, 64])
    pvec = inp("pvec", [128, 8])
    w_uq = inp("w_uq", [256, 512])
    w_ukv = inp("w_ukv", [128, 512])
    w_br = inp("w_br", [D, D])
    w_out = inp("w_out", [D, D])
    ln1 = inp("ln1", [128, 2, 8])
    w_rt = inp("w_rt", [D, NE])
    ropeC = inp("ropeC", [32, SEQ], BF16)
    ropeS = inp("ropeS", [32, SEQ], BF16)
    dftc = inp("dftc", [128, 256], BF16)
    ident_f = inp("ident_f", [128, 128])
    ident_b = inp("ident_b", [128, 128], BF16)
    ustri = inp("ustri", [128, 128], BF16)
    iota_f = inp("iota_f", [128, NSLOT])
    pidx = inp("pidx", [128, 3])
    lval = inp("lval", [128, 16])
    jrow = inp("jrow", [128, 256], I32)
    pooledge = inp("pooledge", [128, 64])
    poolinvw = inp("poolinvw", [128, 2])
    if prologue:
        ye = inp("ye", [NE, NSLOT, D], BF16)
        csT_p = inp("csT_p", [NE, T])
        gmT_p = inp("gmT_p", [NE, T])
        modp = inp("modp", [128, 48, 2])
        ln2p = inp("ln2p", [128, 2, 8])
        eye16rep = inp("eye16rep", [NE, NE * 128], BF16)
    h1T = P.dram_out("h1T", [D, T], F32)
    xeT = P.dram_out("xeT", [NE, D, NSLOT], BF16)
    csT_o = P.dram_out("csT_out", [NE, T], F32)
    gmT_o = P.dram_out("gmT_out", [NE, T], F32)
    modT_o = P.dram_out("modT_out", [128, 48, 2], F32)
    dbg = P.dram_out("dbg", [D, T], F32) if stage < 99 else None
    hs = P.dram_tmp("hs", [D, T], F32)
    outs = [h1T, xeT, csT_o, gmT_o, modT_o] + ([dbg] if dbg is not None else [])

    S = {}

    def sb(name, shape, dt=F32):
        S[name] = P.sbuf(name, shape, dt)
        return S[name]
    ones_f = sb("ones_f", [128, 128])
    ones_b = sb("ones_b", [128, 128], BF16)
    idf = sb("idf", [128, 128])
    idb = sb("idb", [128, 128], BF16)
    ust = sb("ust", [128, 128], BF16)
    iof = sb("iof", [128, NSLOT])
    pix = sb("pix", [128, 3])
    modT = sb("modT", [128, 48, 2])
    ops1 = sb("ops1", [128, 8, 2])
    ops2 = sb("ops2", [128, 8, 2])
    pv = sb("pv", [128, 8])
    ln1s = sb("ln1s", [128, 2, 8])
    scs = sb("scs", [128, 8, 2])
    bm = sb("bm", [128, 48])
    lntmp = dict(sq=[sb("lnsq0", [128, 512]), sb("lnsq1", [128, 512])], mean=sb("lnmean", [128, 512]),
                 rstd=sb("lnrstd", [128, 512]), nmr=sb("lnnmr", [128, 512]),
                 zb=[sb("lnzb0", [128, 512], BF16), sb("lnzb1", [128, 512], BF16)],
                 sqb=[sb("lnsqb0", [128, 512], BF16), sb("lnsqb1", [128, 512], BF16)], ones_b=ones_b)
    wring = [sb("wr%d" % i, [128, 8, 512], BF16) for i in range(3)]
    wri = [0]
    PT = [sb("pt%d" % i, [128, 512], BF16) for i in range(4)]
    pti = [0]
    rden = sb("rden", [128, 512])
    rbc = sb("rbc", [128, 512])
    AR = Arena(P, "arena", 150 * 1024)
    PS_T = PsumRing(P, 4, "pst")
    PS_A = PsumRing(P, 2, "psa")
    PS_B = PsumRing(P, 1, "psbx")
    pstb = P.psum("pstb", [128, 1024], BF16)

    def nextw():
        w = wring[wri[0] % 3]
        wri[0] += 1
        return w

    def nextpt():
        t = PT[pti[0] % 4]
        pti[0] += 1
        return t
    alt = [0]

    def ev_eng():
        alt[0] += 1
        return 'dve' if alt[0] % 2 else 'act'

    def loadw(src, lo, n, kch=8):
        w = nextw()
        P.dma('pool', w[:, 0:kch, 0:n], src[:, lo:lo + n].rearrange("(k p) n -> p k n", p=128))
        return w

    P.memset('dve', ones_f[:], 1.0)
    P.memset('dve', ones_b[:], 1.0)
    P.dma('sp', idf[:], ident_f[:, :])
    P.dma('sp', idb[:], ident_b[:, :])
    P.dma('sp', ust[:], ustri[:, :])
    P.dma('sp', iof[:], iota_f[:, :])
    P.dma('sp', pix[:], pidx[:, :])
    P.dma('sp', pv[:], pvec[:, :])
    P.dma('sp', ln1s[:], ln1[:, :, :])
    P.dma('sp', scs[:], cs2[:, :, :])
    P.dma('sp', bm[:], bmod[:, :])
    P.act(scs[:], scs[:], AF.Silu)
    mark0 = AR.off
    wm = [AR.view([128, 8, 1024], F32) for _ in range(2)]
    modrow = AR.view([2, 6 * D], F32)
    pm = PS_B.get()
    for blk in range(6):
        w = wm[blk % 2]
        P.dma('sp', w[:], wmod[:, 1024 * blk:1024 * blk + 1024].rearrange("(k p) n -> p k n", p=128))
        for hb2 in range(2):
            prow = PS_T.get()
            for k in range(8):
                P.mm(prow[0:2, 0:512], scs[:, k, :], w[:, k, 512 * hb2:512 * hb2 + 512], start=(k == 0), stop=(k == 7))
            P.copy('dve', modrow[:, 1024 * blk + 512 * hb2:1024 * blk + 512 * hb2 + 512], prow[0:2, 0:512])
    for j in range(48):
        P.mm(pm[:, 2 * j:2 * j + 2], modrow[:, 128 * j:128 * j + 128], idf[0:2, 0:2])
    for x in range(2):
        P.tt('dve', modT[:, :, x], pm[:, 0:96].rearrange("p (j x) -> p j x", x=2)[:, :, x], bm[:, :], ALU.add)
    P.ts('dve', ops1[:], modT[:, 8:16, :], 1.0, None, ALU.add)
    P.ts('dve', ops2[:], modT[:, 32:40, :], 1.0, None, ALU.add)
    P.dma('sp', modT_o[:, :, :], modT[:])
    AR.off = mark0
    SH1, G1, SH2, G2 = 0, 16, 24, 40

    uT = AR.view([128, 8, T], BF16)
    mark_u = AR.off
    if prologue:
        hsrc = hs
        emit_prologue(P, dict(AR=AR, PS_T=PS_T, PS_A=PS_A, PS_B=PS_B, ones_f=ones_f, lntmp=lntmp, pix=pix, hT_in=hT_in, ye=ye,
                              csT_p=csT_p, gmT_p=gmT_p, modp=modp, ln2p=ln2p, eye16rep=eye16rep, hdst=hs, uT=uT, ops1=ops1, modT=modT))
    else:
        hsrc = hT_in
        hb = [AR.view([128, 8, 512], F32) for _ in range(2)]
        for bi, (t0, n) in enumerate(TBS):
            x = 0 if t0 < SEQ else 1
            b = hb[bi % 2]
            P.dma('sp', b[:, :, 0:n], hT_in[:, t0:t0 + n].rearrange("(k p) n -> p k n", p=128))
            for j in range(8):
                P.ts('dve', uT[:, j, t0:t0 + n], b[:, j, 0:n], ops1[:, j, x:x + 1],
                     modT[:, SH1 + j, x:x + 1], ALU.mult, ALU.add)
    AR.off = mark_u
    yT = [AR.view([128, 2, T], BF16) for _ in range(4)]
    mark_y = AR.off

    def proj_fm(wv, m_lo, M, consume, po=0):
        for (t0, n) in TBS:
            ps = PS_T.get()
            for k in range(8):
                P.mm(ps[po:po + M, 0:n], wv[:, k, m_lo:m_lo + M], uT[:, k, t0:t0 + n], start=(k == 0), stop=(k == 7))
            consume(ps, t0, n)

    def finalize_attn(po, h, dst, t0, n):
        c = h // 2
        if h % 2 == 0:
            dp, lo, hi, op_ = 64, 0, 64, 64
        else:
            dp, lo, hi, op_ = 0, 64, 128, 0
        P.add('dve', lambda e: e.reciprocal(rden[dp:dp + 1, 0:n], po[dp:dp + 1, 0:n]), [po[dp:dp + 1, 0:n]], [rden[dp:dp + 1, 0:n]])
        pbc = PS_B.get()
        P.mm(pbc[lo:hi, 0:n], ones_f[dp:dp + 1, 0:64], rden[dp:dp + 1, 0:n])
        P.copy('act', rbc[lo:hi, 0:n], pbc[lo:hi, 0:n])
        P.tt('dve', dst[lo:hi, c, t0:t0 + n], po[lo:hi, 0:n], rbc[lo:hi, 0:n], ALU.mult)

    def init_vpad(v):
        P.memset('pool', v, 0.0)

    if stage >= 1:
        AR.off = mark_y
        qaT = AR.view([128, 2, T], BF16)
        kaT = AR.view([128, 2, T], BF16)
        va2 = AR.view([128, 18, 4, 128], BF16)
        wall = AR.view([128, NSTRIP], BF16)
        wint = AR.view([128, NSTRIP], BF16)
        nbf = AR.view([128, NSTRIP], F32)
        nmk = AR.view([128, 2, NSTRIP], F32)
        P.dma('sp', nmk[:], nam.rearrange("a p n -> p a n"))
        P.memset('pool', va2[:], 0.0)
        for h in range(4):
            cc_ = 64 if h % 2 == 0 else 0
            P.memset('pool', va2[:, :, h, cc_:cc_ + 1], 1.0)
        wv = loadw(w_in, C_QA, 512)
        for ci, dst in ((0, qaT), (256, kaT)):
            for c in range(2):
                proj_fm(wv, ci + 128 * c, 128,
                        lambda ps, t0, n, dst=dst, c=c: P.copy(ev_eng(), dst[:, c, t0:t0 + n], ps[:, 0:n]))
        wv = loadw(w_in, C_VA, 256)
        for tc in range(18):
            ps = PS_T.get()
            for k in range(8):
                P.mm(ps[:, 0:256], uT[:, k, 128 * tc:128 * tc + 128], wv[:, k, 0:256], start=(k == 0), stop=(k == 7))
            for hp in range(2):
                src = ps[:, 0:256].rearrange("p (h d) -> p h d", h=4)
                if hp == 0:
                    P.copy(ev_eng(), va2[:, tc, 0::2, 0:64], src[:, 0::2, :])
                else:
                    P.copy(ev_eng(), va2[:, tc, 1::2, 64:128], src[:, 1::2, :])
        for h in range(4):
            c, pb = h // 2, 64 * (h % 2)
            M = 65 if h % 2 == 0 else 128
            P.dma('sp', nbf[:], nab[h, :, :])
            for which, dstw in ((0, wall), (1, wint)):
                P.tt('pool', dstw[:], nbf[:], nmk[:, which, :], ALU.add)
                P.act(dstw[:], dstw[:], AF.Exp)
            for qb in range(4):
                lo_i = [0, 2, 6, 10][qb]
                hi_i = [5, 9, 13, 15][qb]
                seq = list(range(lo_i, hi_i + 1)) + [16, 17]
                po = PS_A.get()
                def sc_(i):
                    ps = PS_T.get()
                    P.mm(ps[:, 0:512], kaT[pb:pb + 64, c, 128 * i:128 * i + 128], qaT[pb:pb + 64, c, 512 * qb:512 * qb + 512])
                    return ps
                pq = [sc_(seq[0]), sc_(seq[1])]
                for idx, i in enumerate(seq):
                    ps = pq.pop(0)
                    if idx + 2 < len(seq):
                        pq.append(sc_(seq[idx + 2]))
                    pt = nextpt()
                    P.act(pt[:], ps[:, 0:512], AF.Exp, scale=NA_SCALE)
                    if i < 16:
                        s0 = (10 - (2 * i - 8 * qb)) * 64
                        me = 'dve'
                        if qb == 0:
                            wa = wall if i <= 3 else wint
                            P.tt(me, pt[:, 0:256], pt[:, 0:256], wa[:, s0:s0 + 256], ALU.mult)
                            P.tt(me, pt[:, 256:512], pt[:, 256:512], wint[:, s0 + 256:s0 + 512], ALU.mult)
                        elif qb == 3:
                            wa = wall if i >= 12 else wint
                            P.tt(me, pt[:, 0:320], pt[:, 0:320], wint[:, s0:s0 + 320], ALU.mult)
                            P.tt(me, pt[:, 320:512], pt[:, 320:512], wa[:, s0 + 320:s0 + 512], ALU.mult)
                        else:
                            P.tt(me, pt[:], pt[:], wint[:, s0:s0 + 512], ALU.mult)
                    P.mm(po[0:M, 0:512], va2[:, i, h, 0:M], pt[:], start=(idx == 0), stop=(idx == len(seq) - 1))
                finalize_attn(po, h, yT[0], 512 * qb, 512)
            po = PS_A.get()
            for idx, i in enumerate([16, 17]):
                ps = PS_T.get()
                P.mm(ps[:, 0:256], kaT[pb:pb + 64, c, 128 * i:128 * i + 128], qaT[pb:pb + 64, c, SEQ:T])
                pt = nextpt()
                P.act(pt[:, 0:256], ps[:, 0:256], AF.Exp, scale=NA_SCALE)
                P.mm(po[0:M, 0:256], va2[:, i, h, 0:M], pt[:, 0:256], start=(idx == 0), stop=(idx == 1))
            finalize_attn(po, h, yT[0], SEQ, 256)


    if stage >= 2:
        AR.off = mark_y
        xp = AR.view([128, 2, XPW], F32)
        la = AR.view([128, XPW], F32)
        lb = AR.view([128, XPW], F32)
        ybf = AR.view([128, 2, T], BF16)
        pwbd = AR.view([128, 2, 128], BF16)
        pwf = AR.view([128, 2, 128], F32)
        pe = AR.view([128, 64], F32)
        piw = AR.view([128, 2], F32)
        etmp = AR.view([128, 8], F32)
        P.memset('pool', xp[:], 0.0)
        P.memset('pool', la[:], 0.0)
        P.memset('pool', lb[:], 0.0)
        P.memset('pool', pwf[:], 0.0)
        for g in range(4):
            o = 64 * (g % 2)
            P.dma('sp', pwf[o:o + 64, g // 2, o:o + 64], pool_w[g, :, :])
        P.copy('dve', pwbd[:], pwf[:])
        P.dma('sp', pe[:], pooledge[:, :])
        P.dma('sp', piw[:], poolinvw[:, :])
        wv = loadw(w_in, C_UP, 256)

        def up_consume(ps, t0, n, c):
            off = XP_L + t0 if t0 < SEQ else XP_C
            P.copy(ev_eng(), xp[:, c, off:off + n], ps[:, 0:n])
        for c in range(2):
            proj_fm(wv, 128 * c, 128, lambda ps, t0, n, c=c: up_consume(ps, t0, n, c))
        W = XPW
        for c in range(2):
            x = xp[:, c, :]
            P.tt('dve', la[:, 1:W], x[:, 1:W], x[:, 0:W - 1], ALU.add)
            P.tt('dve', lb[:, 1:W - 1], la[:, 2:W], la[:, 0:W - 2], ALU.add)
            if c == 1:
                P.tt('dve', la[:, 2:W - 2], lb[:, 4:W], lb[:, 0:W - 4], ALU.add)
                P.tt('dve', lb[:, 4:W - 4], la[:, 8:W], la[:, 0:W - 8], ALU.add)
            for (pl, src) in ((0, la), (64, lb)):
                for (off, L, toff) in ((XP_L, SEQ, 0), (XP_C, LC, SEQ)):
                    P.stt('dve', ybf[pl:pl + 64, c, toff:toff + L], src[pl:pl + 64, off:off + L], piw[pl:pl + 64, c:c + 1],
                          x[pl:pl + 64, off:off + L], ALU.mult, ALU.subtract)
                for reg, (off, toff) in enumerate(((XP_L, 0), (XP_L + SEQ - 8, SEQ - 8), (XP_C, SEQ), (XP_C + LC - 8, T - 8))):
                    ec = (c * 4 + reg) * 8
                    P.tt('dve', etmp[pl:pl + 64, :], src[pl:pl + 64, off:off + 8], pe[pl:pl + 64, ec:ec + 8], ALU.mult)
                    P.tt('dve', ybf[pl:pl + 64, c, toff:toff + 8], etmp[pl:pl + 64, :], x[pl:pl + 64, off:off + 8], ALU.subtract)
            for (t0, n) in TBS:
                ps = PS_T.get()
                P.mm(ps[:, 0:n], pwbd[:, c, :], ybf[:, c, t0:t0 + n])
                P.ts('dve', yT[1][:, c, t0:t0 + n], ps[:, 0:n], pv[:, c:c + 1], None, ALU.mult)

    if stage >= 3:
        AR.off = mark_y
        ufT = AR.view([128, 2, T], BF16)
        AB = AR.view([128, 18, 2, 256], BF16)
        dfc = AR.view([128, 256], BF16)
        lv = AR.view([128, 16], F32)
        lofs = AR.view([128, 16, 8, 2], F32)
        jr = AR.view([128, 256], I32)
        tabC = AR.view([128, 16, 256], BF16)
        tabS = AR.view([128, 16, 256], BF16)
        ki = [AR.view([128, 256], I32) for _ in range(2)]
        P.dma('sp', dfc[:], dftc[:, :])
        P.dma('sp', lv[:], lval[:, :])
        P.dma('sp', jr[:], jrow[:, :])
        for jb in range(8):
            P.ts('dve', lofs[:, :, jb, 0], lv[:], 256.0 * jb, 512.0, ALU.mult, ALU.add)
            P.ts('dve', lofs[:, :, jb, 1], lv[:], 256.0 * jb, None, ALU.mult)
        wv = loadw(w_in, C_UF, 256)
        for c in range(2):
            proj_fm(wv, 128 * c, 128, lambda ps, t0, n, c=c: P.copy(ev_eng(), ufT[:, c, t0:t0 + n], ps[:, 0:n]))
        for tc in range(18):
            for c in range(2):
                ps = PS_T.get()
                P.mm(ps[:, 0:256], ufT[:, c, 128 * tc:128 * tc + 128], dfc[:, 0:256])
                P.copy(ev_eng(), AB[:, tc, c, :], ps[:, 0:256])
        kc_ = [0]

        def gen_tab(dst, a, ofs_ap, ofs_imm, mask, scale):
            k = ki[kc_[0] % 2]
            kc_[0] += 1
            if ofs_ap is not None:
                P.ts('dve', k[:], jr[:], lv[:, a:a + 1], ofs_ap, ALU.mult, ALU.add)
            else:
                P.ts('dve', k[:], jr[:], lv[:, a:a + 1], float(ofs_imm), ALU.mult, ALU.add)
            P.ts('dve', k[:], k[:], mask, None, ALU.bitwise_and)
            P.act(dst, k[:], AF.Sin, bias=mpi[:, 0:1], scale=scale)
        mpi = AR.view([128, 1], F32)
        P.memset('dve', mpi[:], -float(np.pi))
        for jb in range(8):
            for a in range(16):
                gen_tab(tabC[:, a, :], a, lofs[:, a, jb, 0:1], None, 2047, 2.0 * np.pi / 2048.0)
                gen_tab(tabS[:, a, :], a, lofs[:, a, jb, 1:2], None, 2047, 2.0 * np.pi / 2048.0)
            for c in range(2):
                po = PS_A.get()
                for a in range(16):
                    P.mm(po[:, 0:256], AB[:, a, c, 0:128], tabC[:, a, :], start=(a == 0), stop=False)
                    P.mm(po[:, 0:256], AB[:, a, c, 128:256], tabS[:, a, :], start=False, stop=(a == 15))
                P.ts('dve', yT[2][:, c, 256 * jb:256 * jb + 256], po[:, 0:256], float((SEQ * 64.0) ** -0.5), None, ALU.mult)
        for a in range(2):
            gen_tab(tabC[:, a, :], a, None, 64, 255, 2.0 * np.pi / 256.0)
            gen_tab(tabS[:, a, :], a, None, 0, 255, 2.0 * np.pi / 256.0)
        for c in range(2):
            po = PS_A.get()
            for a in range(2):
                P.mm(po[:, 0:256], AB[:, 16 + a, c, 0:128], tabC[:, a, :], start=(a == 0), stop=False)
                P.mm(po[:, 0:256], AB[:, 16 + a, c, 128:256], tabS[:, a, :], start=False, stop=(a == 1))
            P.ts('dve', yT[2][:, c, SEQ:T], po[:, 0:256], float((LC * 64.0) ** -0.5), None, ALU.mult)


    if stage >= 4:
        AR.off = mark_y
        cqn = AR.view([128, 2, T], BF16)
        ckvn = AR.view([128, T], BF16)
        kro = AR.view([128, T], BF16)
        rC = AR.view([128, SEQ], BF16)
        rS = AR.view([128, SEQ], BF16)
        qm = AR.view([128, T], BF16)
        km = AR.view([128, T], BF16)
        vm = AR.view([128, 18, 128], BF16)
        wuq = AR.view([128, 2, 512], BF16)
        wukv = AR.view([128, 512], BF16)
        sqb = [AR.view([128, 512], BF16) for _ in range(2)]
        rst = AR.view([128, 512], F32)
        rt1 = AR.view([128, 512], F32)
        rt2 = AR.view([128, 512], F32)
        P.dma('sp', rC[64:96, :], ropeC[:, :])
        P.dma('sp', rS[64:96, :], ropeS[:, :])
        P.dma('pool', wuq[:], w_uq.rearrange("(k p) n -> p k n", p=128))
        P.dma('pool', wukv[:], w_ukv[:, :])
        wv = loadw(w_in, C_CQ, 448)

        def rope_apply(dst, psa, psb, t0, n):
            if t0 < SEQ:
                P.tt('dve', rt1[64:96, 0:n], psa[64:96, 0:n], rC[64:96, t0:t0 + n], ALU.mult)
                P.tt('dve', rt2[64:96, 0:n], psb[64:96, 0:n], rS[64:96, t0:t0 + n], ALU.mult)
                P.tt('dve', dst[64:96, t0:t0 + n], rt1[64:96, 0:n], rt2[64:96, 0:n], ALU.add)
            else:
                P.copy('act', dst[64:96, t0:t0 + n], psa[64:96, 0:n])
        for (t0, n) in TBS:
            pc = [PS_T.get(), PS_T.get()]
            pss = PS_B.get()
            for c in range(2):
                for k in range(8):
                    P.mm(pc[c][:, 0:n], wv[:, k, 128 * c:128 * c + 128], uT[:, k, t0:t0 + n], start=(k == 0), stop=(k == 7))
                P.act(sqb[c][:, 0:n], pc[c][:, 0:n], AF.Square)
                P.mm(pss[:, 0:n], ones_b[:], sqb[c][:, 0:n], start=(c == 0), stop=(c == 1))
            P.ts('dve', rst[:, 0:n], pss[:, 0:n], 1.0 / 256.0, LN_EPS, ALU.mult, ALU.add)
            P.act(rst[:, 0:n], rst[:, 0:n], AF.Sqrt)
            P.add('dve', lambda e, n=n: e.reciprocal(rst[:, 0:n], rst[:, 0:n]), [rst[:, 0:n]], [rst[:, 0:n]])
            for c in range(2):
                P.stt('dve', cqn[:, c, t0:t0 + n], pc[c][:, 0:n], pv[:, 2 + c:3 + c], rst[:, 0:n], ALU.mult, ALU.mult)
            pk = PS_T.get()
            pss = PS_B.get()
            for k in range(8):
                P.mm(pk[:, 0:n], wv[:, k, 256:384], uT[:, k, t0:t0 + n], start=(k == 0), stop=(k == 7))
            P.act(sqb[0][:, 0:n], pk[:, 0:n], AF.Square)
            P.mm(pss[:, 0:n], ones_b[:], sqb[0][:, 0:n])
            P.ts('dve', rst[:, 0:n], pss[:, 0:n], 1.0 / 128.0, LN_EPS, ALU.mult, ALU.add)
            P.act(rst[:, 0:n], rst[:, 0:n], AF.Sqrt)
            P.add('dve', lambda e, n=n: e.reciprocal(rst[:, 0:n], rst[:, 0:n]), [rst[:, 0:n]], [rst[:, 0:n]])
            P.stt('dve', ckvn[:, t0:t0 + n], pk[:, 0:n], pv[:, 4:5], rst[:, 0:n], ALU.mult, ALU.mult)
            pa = PS_T.get()
            pb_ = PS_T.get()
            for k in range(8):
                P.mm(pa[64:96, 0:n], wv[:, k, 384:416], uT[:, k, t0:t0 + n], start=(k == 0), stop=(k == 7))
            for k in range(8):
                P.mm(pb_[64:96, 0:n], wv[:, k, 416:448], uT[:, k, t0:t0 + n], start=(k == 0), stop=(k == 7))
            rope_apply(kro, pa, pb_, t0, n)
        for h in range(4):
            M = 65 if h % 2 == 0 else 128
            vo = 0 if h % 2 == 0 else 64
            P.memset('pool', vm[:], 0.0)
            oc = 64 if h % 2 == 0 else 0
            P.memset('pool', vm[:, :, oc:oc + 1], 1.0)
            for (t0, n) in TBS:
                pa = PS_T.get()
                pb_ = PS_T.get()
                for k in range(2):
                    P.mm(pa[0:96, 0:n], wuq[:, k, 128 * h:128 * h + 96], cqn[:, k, t0:t0 + n], start=(k == 0), stop=(k == 1))
                for k in range(2):
                    P.mm(pb_[64:96, 0:n], wuq[:, k, 128 * h + 96:128 * h + 128], cqn[:, k, t0:t0 + n], start=(k == 0), stop=(k == 1))
                P.copy('act', qm[0:64, t0:t0 + n], pa[0:64, 0:n])
                rope_apply(qm, pa, pb_, t0, n)
                pk = PS_T.get()
                P.mm(pk[0:64, 0:n], wukv[:, 128 * h:128 * h + 64], ckvn[:, t0:t0 + n])
                P.copy('act', km[0:64, t0:t0 + n], pk[0:64, 0:n])
                P.copy('pool', km[64:96, t0:t0 + n], kro[64:96, t0:t0 + n])
            for g0 in range(0, 18, 8):
                gn = min(8, 18 - g0)
                pvv = PS_T.get()
                for tc in range(g0, g0 + gn):
                    P.mm(pvv[:, 64 * (tc - g0):64 * (tc - g0) + 64], ckvn[:, 128 * tc:128 * tc + 128],
                         wukv[:, 128 * h + 64:128 * h + 128])
                P.copy(ev_eng(), vm[:, g0:g0 + gn, vo:vo + 64], pvv[:, 0:64 * gn].rearrange("p (a d) -> p a d", d=64))
            for (t0, n) in TBS:
                seq = list(range(18)) if t0 < SEQ else [16, 17]
                po = PS_A.get()
                def sc_(i):
                    ps = PS_T.get()
                    P.mm(ps[:, 0:n], km[0:96, 128 * i:128 * i + 128], qm[0:96, t0:t0 + n])
                    return ps
                pq = [sc_(seq[0]), sc_(seq[1])]
                for idx, i in enumerate(seq):
                    ps = pq.pop(0)
                    if idx + 2 < len(seq):
                        pq.append(sc_(seq[idx + 2]))
                    pt = nextpt()
                    P.act(pt[:, 0:n], ps[:, 0:n], AF.Exp, scale=MLA_SCALE)
                    P.mm(po[0:M, 0:n], vm[:, i, 0:M], pt[:, 0:n], start=(idx == 0), stop=(idx == len(seq) - 1))
                finalize_attn(po, h, yT[3], t0, n)


    u2tok = None
    if stage >= 5:
        AR.off = mark_y
        merged = AR.view([128, 8, T], BF16)
        wbr = AR.view([128, 2, 4, 128], BF16)
        sg = [AR.view([128, 512], F32) for _ in range(2)]
        macc = AR.view([128, 512], F32)
        mt = AR.view([128, 512], F32)
        for jd in range(8):
            wg_ = nextw()
            for nb in range(4):
                P.dma('pool', wg_[:, :, 128 * nb:128 * nb + 128],
                      w_in[:, C_GT + 1024 * nb + 128 * jd:C_GT + 1024 * nb + 128 * jd + 128].rearrange("(k p) n -> p k n", p=128))
                P.dma('pool', wbr[:, :, nb, :],
                      w_br[256 * nb:256 * nb + 256, 128 * jd:128 * jd + 128].rearrange("(k p) n -> p k n", p=128))
            for (t0, n) in TBS:
                for nb in range(4):
                    pg = PS_T.get()
                    for k in range(8):
                        P.mm(pg[:, 0:n], wg_[:, k, 128 * nb:128 * nb + 128], uT[:, k, t0:t0 + n], start=(k == 0), stop=(k == 7))
                    pp = PS_T.get()
                    for k in range(2):
                        P.mm(pp[:, 0:n], wbr[:, k, nb, :], yT[nb][:, k, t0:t0 + n], start=(k == 0), stop=(k == 1))
                    s_ = sg[nb % 2]
                    P.act(s_[:, 0:n], pg[:, 0:n], AF.Sigmoid)
                    if nb == 0:
                        P.tt('dve', macc[:, 0:n], s_[:, 0:n], pp[:, 0:n], ALU.mult)
                    elif nb < 3:
                        P.tt('dve', mt[:, 0:n], s_[:, 0:n], pp[:, 0:n], ALU.mult)
                        P.tt('dve', macc[:, 0:n], macc[:, 0:n], mt[:, 0:n], ALU.add)
                    else:
                        P.tt('dve', mt[:, 0:n], s_[:, 0:n], pp[:, 0:n], ALU.mult)
                        P.tt('dve', merged[:, jd, t0:t0 + n], macc[:, 0:n], mt[:, 0:n], ALU.add)
        AR.off = mark_u
        u2tok = AR.view([128, 18, D], BF16)
        assert AR.off <= mark_y
        AR.off = mark_y + 8 * T * 2 + 64
        wo0 = nextw()
        wo1 = nextw()
        P.dma('pool', wo0[:, :, 0:512], w_out[:, 0:512].rearrange("(k p) n -> p k n", p=128))
        P.dma('pool', wo1[:, :, 0:512], w_out[:, 512:1024].rearrange("(k p) n -> p k n", p=128))
        wrt = AR.view([128, 8, NE], F32)
        P.dma('sp', wrt[:], w_rt.rearrange("(k p) n -> p k n", p=128))
        u2b = AR.view([128, 8, 512], BF16)
        lgT = AR.view([NE, T], F32)
        mark_m = AR.off
        AR.off = 0
        hbk = [AR.view([128, 8, 512], F32)]
        zts = [AR.view([128, 8, 512], F32)]
        AR.off = mark_m
        zts.append(AR.view([128, 8, 512], F32))
        for bi, (t0, n) in enumerate(TBS):
            x = 0 if t0 < SEQ else 1
            hb_ = hbk[0]
            zt = zts[bi % 2]
            P.dma('sp', hb_[:, :, 0:n], hsrc[:, t0:t0 + n].rearrange("(k p) n -> p k n", p=128))
            for jo in range(8):
                po = PS_T.get()
                for k in range(8):
                    P.mm(po[:, 0:n], (wo0 if jo < 4 else wo1)[:, k, 128 * (jo % 4):128 * (jo % 4) + 128], merged[:, k, t0:t0 + n], start=(k == 0), stop=(k == 7))
                P.act(hb_[:, jo, 0:n], hb_[:, jo, 0:n], AF.Identity, scale=float(DN_ALPHA))
                P.stt('dve', zt[:, jo, 0:n], po[:, 0:n], modT[:, G1 + jo, x:x + 1], hb_[:, jo, 0:n], ALU.mult, ALU.add)

            def after_ln(j, ap, t0=t0, n=n, x=x):
                P.dma('sp', h1T[128 * j:128 * j + 128, t0:t0 + n], ap)
                P.act(zt[:, j, 0:n], ap, AF.Identity, bias=modT[:, SH2 + j, x:x + 1], scale=ops2[:, j, x:x + 1])
                P.copy('act', u2b[:, j, 0:n], zt[:, j, 0:n])
            ln_block(P, PS_A, zt, n, ln1s[:, 0, :], ln1s[:, 1, :], after_ln, ones_f, lntmp, ['dve', 'pool'])
            pl = PS_B.get()
            for k in range(8):
                P.mm(pl[0:NE, 0:n], wrt[:, k, :], zt[:, k, 0:n], start=(k == 0), stop=(k == 7))
            P.copy('dve', lgT[:, t0:t0 + n], pl[0:NE, 0:n])
            for q in range(n // 128):
                tc = t0 // 128 + q
                for k in range(8):
                    P.tr(pstb[:, 128 * k:128 * k + 128], u2b[:, k, 128 * q:128 * q + 128], idb[:])
                P.copy('act', u2tok[:, tc, :], pstb[:, :])

    if stage >= 6:
        AR.off = mark_y
        mask = AR.view([128, 18, NE], F32)
        maskb = AR.view([128, 18, NE], BF16)
        cs = AR.view([128, 18, NE], F32)
        csm = AR.view([128, 18, NE], F32)
        affT = AR.view([NE, T], F32)
        maskT = AR.view([NE, T], F32)
        maskTb = AR.view([NE, T], BF16)
        gmT = AR.view([NE, T], F32)
        m8 = AR.view([NE, 8], F32)
        m8c = AR.view([NE, 8], F32)
        rsm = AR.view([NE, 512], F32)
        assert AR.off <= mark_m - NE * 0 - T * 4
        P.act(affT[:], lgT[:], AF.Exp)
        for (t0, n) in TBS:
            ps = PS_T.get()
            P.mm(ps[0:NE, 0:n], ones_f[0:NE, 0:NE], affT[:, t0:t0 + n])
            P.add('dve', lambda e, ps=ps, n=n: e.reciprocal(rsm[:, 0:n], ps[0:NE, 0:n]), [ps[0:NE, 0:n]], [rsm[:, 0:n]])
            P.tt('dve', affT[:, t0:t0 + n], affT[:, t0:t0 + n], rsm[:, 0:n], ALU.mult)
        if sub >= 1:
            mark_r = AR.off
            AR.off = 0
            wk = AR.view([NE, SEQ], F32)
            wkc = AR.view([NE, LC], F32)
            csT = AR.view([NE, T], F32)
            P.copy('dve', wk[:], affT[:, 0:SEQ])
            P.copy('dve', wkc[:], affT[:, SEQ:T])
            for r in range(CAP // 8):
                P.add('dve', lambda e: e.max(m8[:], wk[:]), [wk[:]], [m8[:]])
                if r < CAP // 8 - 1:
                    P.add('dve', lambda e: e.match_replace(wk[:], m8[:], wk[:], -1.0), [wk[:], m8[:]], [wk[:]])
            for r in range(CAPC // 8):
                P.add('dve', lambda e: e.max(m8c[:], wkc[:]), [wkc[:]], [m8c[:]])
                if r < CAPC // 8 - 1:
                    P.add('dve', lambda e: e.match_replace(wkc[:], m8c[:], wkc[:], -1.0), [wkc[:], m8c[:]], [wkc[:]])
        if sub >= 2:
            P.ts('dve', maskT[:, 0:SEQ], affT[:, 0:SEQ], m8[:, 7:8], None, ALU.is_ge)
            P.ts('dve', maskT[:, SEQ:T], affT[:, SEQ:T], m8c[:, 7:8], None, ALU.is_ge)
            P.tt('dve', gmT[:], maskT[:], affT[:], ALU.mult)
            P.dma('sp', gmT_o[:, :], gmT[:])
        if sub >= 3:
            import os
            S3 = float(os.environ.get('SUB3', '9'))
            if S3 >= 0.1:
                P.copy('dve', maskTb[:], maskT[:])
            pmk = PS_T.get()
            if S3 >= 0.2:
                for tc in range(18):
                    P.mm(pmk[:, NE * tc:NE * tc + NE], maskTb[:, 128 * tc:128 * tc + 128], idb[0:NE, 0:NE])
            if S3 >= 0.3:
                P.copy('dve', mask[:], pmk[:, 0:18 * NE].rearrange("p (a b) -> p a b", b=NE))
            if S3 >= 0.4:
                P.copy('pool', maskb[:], mask[:])
            for tc in range(18 if S3 >= 2 else 0):
                base = 0 if tc < 16 else 16
                pc_ = PS_B.get()
                for c2 in range(base, tc):
                    P.mm(pc_[:, 0:NE], ones_b[:], maskb[:, c2, :], start=(c2 == base), stop=False)
                P.mm(pc_[:, 0:NE], ust[:], maskb[:, tc, :], start=(tc == base), stop=True)
                P.ts('dve', cs[:, tc, :], pc_[:, 0:NE], 0.0 if tc < 16 else float(CAP), None, ALU.add)
                if S3 < 3:
                    continue
                pt_ = PS_T.get()
                for c2 in range(base, tc):
                    P.mm(pt_[0:NE, 0:128], maskb[:, c2, :], ones_b[:], start=(c2 == base), stop=False)
                P.mm(pt_[0:NE, 0:128], maskb[:, tc, :], ust[:], start=(tc == base), stop=True)
                P.ts('dve', csT[:, 128 * tc:128 * tc + 128], pt_[0:NE, 0:128], 0.0 if tc < 16 else float(CAP), None, ALU.add)
            if S3 >= 3:
                P.dma('sp', csT_o[:, :], csT[:])
        if sub >= 4:
            P.tt('dve', csm[:], cs[:], mask[:], ALU.mult)
            P.tt('dve', csm[:], csm[:], mask[:], ALU.add)
            P.ts('dve', csm[:], csm[:], -1.0, None, ALU.add)
            AR.off = 0
            selL = [AR.view([128, 16, CAP], BF16) for _ in range(2)]
            selC = [AR.view([128, 2, CAPC], BF16) for _ in range(2)]
            xs = [AR.view([128, 8, NSLOT], BF16) for _ in range(2)]
            assert AR.off <= mark_u
            for e_ in range(NE):
                sl, sc_, x_ = selL[e_ % 2], selC[e_ % 2], xs[e_ % 2]
                P.tt('dve', sl[:], iof[:, 0:CAP].unsqueeze(1).to_broadcast([128, 16, CAP]),
                     csm[:, 0:16, e_:e_ + 1].to_broadcast([128, 16, CAP]), ALU.is_equal)
                P.tt('dve', sc_[:], iof[:, CAP:NSLOT].unsqueeze(1).to_broadcast([128, 2, CAPC]),
                     csm[:, 16:18, e_:e_ + 1].to_broadcast([128, 2, CAPC]), ALU.is_equal)
                for j in range(8):
                    px = PS_T.get()
                    for tc in range(16):
                        P.mm(px[:, 0:CAP], u2tok[:, tc, 128 * j:128 * j + 128], sl[:, tc, :], start=(tc == 0), stop=(tc == 15))
                    for tc in range(16, 18):
                        P.mm(px[:, CAP:NSLOT], u2tok[:, tc, 128 * j:128 * j + 128], sc_[:, tc - 16, :], start=(tc == 16), stop=(tc == 17))
                    P.copy(ev_eng(), x_[:, j, :], px[:, 0:NSLOT])
                P.dma('sp', xeT[e_].rearrange("(j p) s -> p j s", p=128), x_[:])
    def dump_fm(src, nchunk, row0=0):
        for c in range(nchunk):
            tmpd = lntmp['sq'][c % 2]
            for (t0, n) in TBS:
                P.copy('dve', tmpd[:, 0:n], src[:, c, t0:t0 + n])
                P.dma('sp', dbg[row0 + 128 * c:row0 + 128 * c + 128, t0:t0 + n], tmpd[:, 0:n])
    if stage < 99:
        for bi in range(4):
            if stage >= [1, 2, 3, 4][bi]:
                dump_fm(yT[bi], 2, 256 * bi)
    P.finish(outs)
    P.emit()
    return nc, P


def build_F(nl=DEPTH):
    stage = 99
    sub = 99
    nc = bass.Bass("TRN2", target_bir_lowering=False)
    P = Prog(nc, ring_sizes={'sp': 24, 'pool': 24, 'act': 4})
    I = {}

    def inp(name, shape, dt=F32):
        I[name] = P.dram_in(name, shape, dt)
        return I[name]
    hT_in = inp("hT_in", [D, T])
    cs2 = inp("cs2", [128, 8, 2])
    wmod_all = inp("wmod", [DEPTH] + [D, 6 * D])
    bmod_all = inp("bmod", [DEPTH] + [128, 48])
    w_in_all = inp("w_in", [DEPTH] + [D, WIN_COLS])
    nab_all = inp("nab", [DEPTH] + [4, 128, NSTRIP])
    nam = inp("nam", [2, 128, NSTRIP])
    pool_w_all = inp("pool_w", [DEPTH] + [4, 64, 64])
    pvec_all = inp("pvec", [DEPTH] + [128, 8])
    w_uq_all = inp("w_uq", [DEPTH] + [256, 512])
    w_ukv_all = inp("w_ukv", [DEPTH] + [128, 512])
    w_br_all = inp("w_br", [DEPTH] + [D, D])
    w_out_all = inp("w_out", [DEPTH] + [D, D])
    ln1_all = inp("ln1", [DEPTH] + [128, 2, 8])
    w_rt_all = inp("w_rt", [DEPTH] + [D, NE])
    ropeC = inp("ropeC", [32, SEQ], BF16)
    ropeS = inp("ropeS", [32, SEQ], BF16)
    dftc = inp("dftc", [128, 256], BF16)
    ident_f = inp("ident_f", [128, 128])
    ident_b = inp("ident_b", [128, 128], BF16)
    ustri = inp("ustri", [128, 128], BF16)
    iota_f = inp("iota_f", [128, NSLOT])
    pidx = inp("pidx", [128, 3])
    lval = inp("lval", [128, 16])
    jrow = inp("jrow", [128, 256], I32)
    pooledge = inp("pooledge", [128, 64])
    poolinvw = inp("poolinvw", [128, 2])
    ln2_all = inp("ln2", [DEPTH, 128, 2, 8])
    eye16rep = inp("eye16rep", [NE, NE * 128], BF16)
    wg_all = inp("wg", [DEPTH, NE, D, FF])
    wu_all = inp("wu", [DEPTH, NE, D, FF])
    wd_all = inp("wd", [DEPTH, NE, FF, D])
    out_d = P.dram_out("out", [D, SEQ], F32)
    h1T = P.dram_tmp("h1s", [D, T], F32)
    hs = P.dram_tmp("hs", [D, T], F32)
    ye = P.dram_tmp("ye_scr", [NE, NSLOT, D], BF16)
    csT_o = P.dram_tmp("csT_scr", [NE, T], F32)
    gmT_o = P.dram_tmp("gmT_scr", [NE, T], F32)
    mod_scr = P.dram_tmp("mod_scr", [DEPTH, 128, 48, 2], F32)
    tab_scr = P.dram_tmp("tab_scr", [8, 2, 128, 16 * 256], BF16)
    dbg = None
    outs = [out_d]
    S = {}

    def sb(name, shape, dt=F32):
        S[name] = P.sbuf(name, shape, dt)
        return S[name]
    ones_f = sb("ones_f", [128, 128])
    ones_b = sb("ones_b", [128, 128], BF16)
    idf = sb("idf", [128, 128])
    idb = sb("idb", [128, 128], BF16)
    ust = sb("ust", [128, 128], BF16)
    iof = sb("iof", [128, NSLOT])
    pix = sb("pix", [128, 3])
    modT = sb("modT", [128, 48, 2])
    ops1 = sb("ops1", [128, 8, 2])
    ops2 = sb("ops2", [128, 8, 2])
    pv = sb("pv", [128, 8])
    ln1s = sb("ln1s", [128, 2, 8])
    scs = sb("scs", [128, 8, 2])
    bm = sb("bm", [128, 48])
    lntmp = dict(sq=[sb("lnsq0", [128, 512]), sb("lnsq1", [128, 512])], mean=sb("lnmean", [128, 512]),
                 rstd=sb("lnrstd", [128, 512]), nmr=sb("lnnmr", [128, 512]),
                 zb=[sb("lnzb0", [128, 512], BF16), sb("lnzb1", [128, 512], BF16)],
                 sqb=[sb("lnsqb0", [128, 512], BF16), sb("lnsqb1", [128, 512], BF16)], ones_b=ones_b)
    wring = [sb("wr%d" % i, [128, 8, 512], BF16) for i in range(3)]
    wri = [0]
    PT = [sb("pt%d" % i, [128, 512], BF16) for i in range(4)]
    pti = [0]
    rden = sb("rden", [128, 512])
    rbc = sb("rbc", [128, 512])
    AR = Arena(P, "arena", 150 * 1024)
    PS_T = PsumRing(P, 4, "pst")
    PS_A = PsumRing(P, 2, "psa")
    PS_B = PsumRing(P, 1, "psbx")
    pstb = P.psum("pstb", [128, 1024], BF16)

    def nextw():
        w = wring[wri[0] % 3]
        wri[0] += 1
        return w

    def nextpt():
        t = PT[pti[0] % 4]
        pti[0] += 1
        return t
    alt = [0]

    def ev_eng():
        alt[0] += 1
        return 'dve' if alt[0] % 2 else 'act'

    def loadw(src, lo, n, kch=8):
        w = nextw()
        P.dma('pool', w[:, 0:kch, 0:n], src[:, lo:lo + n].rearrange("(k p) n -> p k n", p=128))
        return w

    P.memset('dve', ones_f[:], 1.0)
    P.memset('dve', ones_b[:], 1.0)
    P.dma('sp', idf[:], ident_f[:, :])
    P.dma('sp', idb[:], ident_b[:, :])
    P.dma('sp', ust[:], ustri[:, :])
    P.dma('sp', iof[:], iota_f[:, :])
    P.dma('sp', pix[:], pidx[:, :])
    P.dma('sp', scs[:], cs2[:, :, :])
    for l in range(nl):
        prologue = l > 0
        wmod, bmod, w_in, nab, pool_w, pvec = wmod_all[l], bmod_all[l], w_in_all[l], nab_all[l], pool_w_all[l], pvec_all[l]
        w_uq, w_ukv, w_br, w_out, ln1, w_rt = w_uq_all[l], w_ukv_all[l], w_br_all[l], w_out_all[l], ln1_all[l], w_rt_all[l]
        AR.off = 0
        P.dma('sp', pv[:], pvec[:, :])
        P.dma('sp', ln1s[:], ln1[:, :, :])
        P.dma('sp', bm[:], bmod[:, :])
        P.dma('sp', scs[:], cs2[:, :, :])
        P.act(scs[:], scs[:], AF.Silu)
        mark0 = AR.off
        wm = [AR.view([128, 8, 1024], F32) for _ in range(2)]
        modrow = AR.view([2, 6 * D], F32)
        pm = PS_B.get()
        for blk in range(6):
            w = wm[blk % 2]
            P.dma('sp', w[:], wmod[:, 1024 * blk:1024 * blk + 1024].rearrange("(k p) n -> p k n", p=128))
            for hb2 in range(2):
                prow = PS_T.get()
                for k in range(8):
                    P.mm(prow[0:2, 0:512], scs[:, k, :], w[:, k, 512 * hb2:512 * hb2 + 512], start=(k == 0), stop=(k == 7))
                P.copy('dve', modrow[:, 1024 * blk + 512 * hb2:1024 * blk + 512 * hb2 + 512], prow[0:2, 0:512])
        for j in range(48):
            P.mm(pm[:, 2 * j:2 * j + 2], modrow[:, 128 * j:128 * j + 128], idf[0:2, 0:2])
        for x in range(2):
            P.tt('dve', modT[:, :, x], pm[:, 0:96].rearrange("p (j x) -> p j x", x=2)[:, :, x], bm[:, :], ALU.add)
        P.ts('dve', ops1[:], modT[:, 8:16, :], 1.0, None, ALU.add)
        P.ts('dve', ops2[:], modT[:, 32:40, :], 1.0, None, ALU.add)
        P.dma('sp', mod_scr[l], modT[:])
        AR.off = mark0
        SH1, G1, SH2, G2 = 0, 16, 24, 40

        uT = AR.view([128, 8, T], BF16)
        mark_u = AR.off
        if prologue:
            hsrc = hs
            emit_prologue(P, dict(AR=AR, PS_T=PS_T, PS_A=PS_A, PS_B=PS_B, ones_f=ones_f, lntmp=lntmp, pix=pix, hT_in=h1T, ye=ye,
                                  csT_p=csT_o, gmT_p=gmT_o, modp=mod_scr[l - 1], ln2p=ln2_all[l - 1], eye16rep=eye16rep, hdst=hs, uT=uT, ops1=ops1, modT=modT))
        else:
            hsrc = hT_in
            hb = [AR.view([128, 8, 512], F32) for _ in range(2)]
            for bi, (t0, n) in enumerate(TBS):
                x = 0 if t0 < SEQ else 1
                b = hb[bi % 2]
                P.dma('sp', b[:, :, 0:n], hT_in[:, t0:t0 + n].rearrange("(k p) n -> p k n", p=128))
                for j in range(8):
                    P.ts('dve', uT[:, j, t0:t0 + n], b[:, j, 0:n], ops1[:, j, x:x + 1],
                         modT[:, SH1 + j, x:x + 1], ALU.mult, ALU.add)
        AR.off = mark_u
        yT = [AR.view([128, 2, T], BF16) for _ in range(4)]
        mark_y = AR.off

        def proj_fm(wv, m_lo, M, consume, po=0):
            for (t0, n) in TBS:
                ps = PS_T.get()
                for k in range(8):
                    P.mm(ps[po:po + M, 0:n], wv[:, k, m_lo:m_lo + M], uT[:, k, t0:t0 + n], start=(k == 0), stop=(k == 7))
                consume(ps, t0, n)

        def finalize_attn(po, h, dst, t0, n):
            c = h // 2
            if h % 2 == 0:
                dp, lo, hi, op_ = 64, 0, 64, 64
            else:
                dp, lo, hi, op_ = 0, 64, 128, 0
            P.add('dve', lambda e: e.reciprocal(rden[dp:dp + 1, 0:n], po[dp:dp + 1, 0:n]), [po[dp:dp + 1, 0:n]], [rden[dp:dp + 1, 0:n]])
            pbc = PS_B.get()
            P.mm(pbc[lo:hi, 0:n], ones_f[dp:dp + 1, 0:64], rden[dp:dp + 1, 0:n])
            P.copy('act', rbc[lo:hi, 0:n], pbc[lo:hi, 0:n])
            P.tt('dve', dst[lo:hi, c, t0:t0 + n], po[lo:hi, 0:n], rbc[lo:hi, 0:n], ALU.mult)

        def init_vpad(v):
            P.memset('pool', v, 0.0)

        if stage >= 1:
            AR.off = mark_y
            qaT = AR.view([128, 2, T], BF16)
            kaT = AR.view([128, 2, T], BF16)
            va2 = AR.view([128, 18, 4, 128], BF16)
            wall = AR.view([128, NSTRIP], BF16)
            wint = AR.view([128, NSTRIP], BF16)
            nbf = AR.view([128, NSTRIP], F32)
            nmk = AR.view([128, 2, NSTRIP], F32)
            P.dma('sp', nmk[:], nam.rearrange("a p n -> p a n"))
            P.memset('pool', va2[:], 0.0)
            for h in range(4):
                cc_ = 64 if h % 2 == 0 else 0
                P.memset('pool', va2[:, :, h, cc_:cc_ + 1], 1.0)
            wv = loadw(w_in, C_QA, 512)
            for ci, dst in ((0, qaT), (256, kaT)):
                for c in range(2):
                    proj_fm(wv, ci + 128 * c, 128,
                            lambda ps, t0, n, dst=dst, c=c: P.copy(ev_eng(), dst[:, c, t0:t0 + n], ps[:, 0:n]))
            wv = loadw(w_in, C_VA, 256)
            for tc in range(18):
                ps = PS_T.get()
                for k in range(8):
                    P.mm(ps[:, 0:256], uT[:, k, 128 * tc:128 * tc + 128], wv[:, k, 0:256], start=(k == 0), stop=(k == 7))
                for hp in range(2):
                    src = ps[:, 0:256].rearrange("p (h d) -> p h d", h=4)
                    if hp == 0:
                        P.copy(ev_eng(), va2[:, tc, 0::2, 0:64], src[:, 0::2, :])
                    else:
                        P.copy(ev_eng(), va2[:, tc, 1::2, 64:128], src[:, 1::2, :])
            for h in range(4):
                c, pb = h // 2, 64 * (h % 2)
                M = 65 if h % 2 == 0 else 128
                P.dma('sp', nbf[:], nab[h, :, :])
                for which, dstw in ((0, wall), (1, wint)):
                    P.tt('pool', dstw[:], nbf[:], nmk[:, which, :], ALU.add)
                    P.act(dstw[:], dstw[:], AF.Exp)
                for qb in range(4):
                    lo_i = [0, 2, 6, 10][qb]
                    hi_i = [5, 9, 13, 15][qb]
                    seq = list(range(lo_i, hi_i + 1)) + [16, 17]
                    po = PS_A.get()
                    def sc_(i):
                        ps = PS_T.get()
                        P.mm(ps[:, 0:512], kaT[pb:pb + 64, c, 128 * i:128 * i + 128], qaT[pb:pb + 64, c, 512 * qb:512 * qb + 512])
                        return ps
                    pq = [sc_(seq[0]), sc_(seq[1])]
                    for idx, i in enumerate(seq):
                        ps = pq.pop(0)
                        if idx + 2 < len(seq):
                            pq.append(sc_(seq[idx + 2]))
                        pt = nextpt()
                        P.act(pt[:], ps[:, 0:512], AF.Exp, scale=NA_SCALE)
                        if i < 16:
                            s0 = (10 - (2 * i - 8 * qb)) * 64
                            me = 'dve'
                            if qb == 0:
                                wa = wall if i <= 3 else wint
                                P.tt(me, pt[:, 0:256], pt[:, 0:256], wa[:, s0:s0 + 256], ALU.mult)
                                P.tt(me, pt[:, 256:512], pt[:, 256:512], wint[:, s0 + 256:s0 + 512], ALU.mult)
                            elif qb == 3:
                                wa = wall if i >= 12 else wint
                                P.tt(me, pt[:, 0:320], pt[:, 0:320], wint[:, s0:s0 + 320], ALU.mult)
                                P.tt(me, pt[:, 320:512], pt[:, 320:512], wa[:, s0 + 320:s0 + 512], ALU.mult)
                            else:
                                P.tt(me, pt[:], pt[:], wint[:, s0:s0 + 512], ALU.mult)
                        P.mm(po[0:M, 0:512], va2[:, i, h, 0:M], pt[:], start=(idx == 0), stop=(idx == len(seq) - 1))
                    finalize_attn(po, h, yT[0], 512 * qb, 512)
                po = PS_A.get()
                for idx, i in enumerate([16, 17]):
                    ps = PS_T.get()
                    P.mm(ps[:, 0:256], kaT[pb:pb + 64, c, 128 * i:128 * i + 128], qaT[pb:pb + 64, c, SEQ:T])
                    pt = nextpt()
                    P.act(pt[:, 0:256], ps[:, 0:256], AF.Exp, scale=NA_SCALE)
                    P.mm(po[0:M, 0:256], va2[:, i, h, 0:M], pt[:, 0:256], start=(idx == 0), stop=(idx == 1))
                finalize_attn(po, h, yT[0], SEQ, 256)


        if stage >= 2:
            AR.off = mark_y
            xp = AR.view([128, 2, XPW], F32)
            la = AR.view([128, XPW], F32)
            lb = AR.view([128, XPW], F32)
            ybf = AR.view([128, 2, T], BF16)
            pwbd = AR.view([128, 2, 128], BF16)
            pwf = AR.view([128, 2, 128], F32)
            pe = AR.view([128, 64], F32)
            piw = AR.view([128, 2], F32)
            etmp = AR.view([128, 8], F32)
            P.memset('pool', xp[:], 0.0)
            P.memset('pool', la[:], 0.0)
            P.memset('pool', lb[:], 0.0)
            P.memset('pool', pwf[:], 0.0)
            for g in range(4):
                o = 64 * (g % 2)
                P.dma('sp', pwf[o:o + 64, g // 2, o:o + 64], pool_w[g, :, :])
            P.copy('dve', pwbd[:], pwf[:])
            P.dma('sp', pe[:], pooledge[:, :])
            P.dma('sp', piw[:], poolinvw[:, :])
            wv = loadw(w_in, C_UP, 256)

            def up_consume(ps, t0, n, c):
                off = XP_L + t0 if t0 < SEQ else XP_C
                P.copy(ev_eng(), xp[:, c, off:off + n], ps[:, 0:n])
            for c in range(2):
                proj_fm(wv, 128 * c, 128, lambda ps, t0, n, c=c: up_consume(ps, t0, n, c))
            W = XPW
            for c in range(2):
                x = xp[:, c, :]
                P.tt('dve', la[:, 1:W], x[:, 1:W], x[:, 0:W - 1], ALU.add)
                P.tt('dve', lb[:, 1:W - 1], la[:, 2:W], la[:, 0:W - 2], ALU.add)
                if c == 1:
                    P.tt('dve', la[:, 2:W - 2], lb[:, 4:W], lb[:, 0:W - 4], ALU.add)
                    P.tt('dve', lb[:, 4:W - 4], la[:, 8:W], la[:, 0:W - 8], ALU.add)
                for (pl, src) in ((0, la), (64, lb)):
                    for (off, L, toff) in ((XP_L, SEQ, 0), (XP_C, LC, SEQ)):
                        P.stt('dve', ybf[pl:pl + 64, c, toff:toff + L], src[pl:pl + 64, off:off + L], piw[pl:pl + 64, c:c + 1],
                              x[pl:pl + 64, off:off + L], ALU.mult, ALU.subtract)
                    for reg, (off, toff) in enumerate(((XP_L, 0), (XP_L + SEQ - 8, SEQ - 8), (XP_C, SEQ), (XP_C + LC - 8, T - 8))):
                        ec = (c * 4 + reg) * 8
                        P.tt('dve', etmp[pl:pl + 64, :], src[pl:pl + 64, off:off + 8], pe[pl:pl + 64, ec:ec + 8], ALU.mult)
                        P.tt('dve', ybf[pl:pl + 64, c, toff:toff + 8], etmp[pl:pl + 64, :], x[pl:pl + 64, off:off + 8], ALU.subtract)
                for (t0, n) in TBS:
                    ps = PS_T.get()
                    P.mm(ps[:, 0:n], pwbd[:, c, :], ybf[:, c, t0:t0 + n])
                    P.ts('dve', yT[1][:, c, t0:t0 + n], ps[:, 0:n], pv[:, c:c + 1], None, ALU.mult)

        if stage >= 3:
            AR.off = mark_y
            ufT = AR.view([128, 2, T], BF16)
            AB = AR.view([128, 18, 2, 256], BF16)
            dfc = AR.view([128, 256], BF16)
            lv = AR.view([128, 16], F32)
            lofs = AR.view([128, 16, 8, 2], F32)
            jr = AR.view([128, 256], I32)
            tabCs = [AR.view([128, 16, 256], BF16) for _ in range(2)]
            tabSs = [AR.view([128, 16, 256], BF16) for _ in range(2)]
            tabC, tabS = tabCs[0], tabSs[0]
            ki = [AR.view([128, 256], I32) for _ in range(2)]
            P.dma('sp', dfc[:], dftc[:, :])
            P.dma('sp', lv[:], lval[:, :])
            P.dma('sp', jr[:], jrow[:, :])
            for jb in range(8):
                P.ts('dve', lofs[:, :, jb, 0], lv[:], 256.0 * jb, 512.0, ALU.mult, ALU.add)
                P.ts('dve', lofs[:, :, jb, 1], lv[:], 256.0 * jb, None, ALU.mult)
            wv = loadw(w_in, C_UF, 256)
            for c in range(2):
                proj_fm(wv, 128 * c, 128, lambda ps, t0, n, c=c: P.copy(ev_eng(), ufT[:, c, t0:t0 + n], ps[:, 0:n]))
            for tc in range(18):
                for c in range(2):
                    ps = PS_T.get()
                    P.mm(ps[:, 0:256], ufT[:, c, 128 * tc:128 * tc + 128], dfc[:, 0:256])
                    P.copy(ev_eng(), AB[:, tc, c, :], ps[:, 0:256])
            kc_ = [0]

            def gen_tab(dst, a, ofs_ap, ofs_imm, mask, scale):
                k = ki[kc_[0] % 2]
                kc_[0] += 1
                if ofs_ap is not None:
                    P.ts('dve', k[:], jr[:], lv[:, a:a + 1], ofs_ap, ALU.mult, ALU.add)
                else:
                    P.ts('dve', k[:], jr[:], lv[:, a:a + 1], float(ofs_imm), ALU.mult, ALU.add)
                P.ts('dve', k[:], k[:], mask, None, ALU.bitwise_and)
                P.act(dst, k[:], AF.Sin, bias=mpi[:, 0:1], scale=scale)
            mpi = AR.view([128, 1], F32)
            P.memset('dve', mpi[:], -float(np.pi))
            for jb in range(8):
                tabC, tabS = tabCs[jb % 2], tabSs[jb % 2]
                if l == 0:
                    for a in range(16):
                        gen_tab(tabC[:, a, :], a, lofs[:, a, jb, 0:1], None, 2047, 2.0 * np.pi / 2048.0)
                        gen_tab(tabS[:, a, :], a, lofs[:, a, jb, 1:2], None, 2047, 2.0 * np.pi / 2048.0)
                    P.dma('sp', tab_scr[jb, 0].rearrange("p (a b) -> p a b", b=256), tabC[:])
                    P.dma('sp', tab_scr[jb, 1].rearrange("p (a b) -> p a b", b=256), tabS[:])
                else:
                    P.dma('sp', tabC[:], tab_scr[jb, 0].rearrange("p (a b) -> p a b", b=256))
                    P.dma('sp', tabS[:], tab_scr[jb, 1].rearrange("p (a b) -> p a b", b=256))
                for c in range(2):
                    po = PS_A.get()
                    for a in range(16):
                        P.mm(po[:, 0:256], AB[:, a, c, 0:128], tabC[:, a, :], start=(a == 0), stop=False)
                        P.mm(po[:, 0:256], AB[:, a, c, 128:256], tabS[:, a, :], start=False, stop=(a == 15))
                    P.ts('dve', yT[2][:, c, 256 * jb:256 * jb + 256], po[:, 0:256], float((SEQ * 64.0) ** -0.5), None, ALU.mult)
            for a in range(2):
                gen_tab(tabC[:, a, :], a, None, 64, 255, 2.0 * np.pi / 256.0)
                gen_tab(tabS[:, a, :], a, None, 0, 255, 2.0 * np.pi / 256.0)
            for c in range(2):
                po = PS_A.get()
                for a in range(2):
                    P.mm(po[:, 0:256], AB[:, 16 + a, c, 0:128], tabC[:, a, :], start=(a == 0), stop=False)
                    P.mm(po[:, 0:256], AB[:, 16 + a, c, 128:256], tabS[:, a, :], start=False, stop=(a == 1))
                P.ts('dve', yT[2][:, c, SEQ:T], po[:, 0:256], float((LC * 64.0) ** -0.5), None, ALU.mult)


        if stage >= 4:
            AR.off = mark_y
            cqn = AR.view([128, 2, T], BF16)
            ckvn = AR.view([128, T], BF16)
            kro = AR.view([128, T], BF16)
            rC = AR.view([128, SEQ], BF16)
            rS = AR.view([128, SEQ], BF16)
            qm = AR.view([128, T], BF16)
            km = AR.view([128, T], BF16)
            vm = AR.view([128, 18, 128], BF16)
            wuq = AR.view([128, 2, 512], BF16)
            wukv = AR.view([128, 512], BF16)
            sqb = [AR.view([128, 512], BF16) for _ in range(2)]
            rst = AR.view([128, 512], F32)
            rt1 = AR.view([128, 512], F32)
            rt2 = AR.view([128, 512], F32)
            P.dma('sp', rC[64:96, :], ropeC[:, :])
            P.dma('sp', rS[64:96, :], ropeS[:, :])
            P.dma('pool', wuq[:], w_uq.rearrange("(k p) n -> p k n", p=128))
            P.dma('pool', wukv[:], w_ukv[:, :])
            wv = loadw(w_in, C_CQ, 448)

            def rope_apply(dst, psa, psb, t0, n):
                if t0 < SEQ:
                    P.tt('dve', rt1[64:96, 0:n], psa[64:96, 0:n], rC[64:96, t0:t0 + n], ALU.mult)
                    P.tt('dve', rt2[64:96, 0:n], psb[64:96, 0:n], rS[64:96, t0:t0 + n], ALU.mult)
                    P.tt('dve', dst[64:96, t0:t0 + n], rt1[64:96, 0:n], rt2[64:96, 0:n], ALU.add)
                else:
                    P.copy('act', dst[64:96, t0:t0 + n], psa[64:96, 0:n])
            for (t0, n) in TBS:
                pc = [PS_T.get(), PS_T.get()]
                pss = PS_B.get()
                for c in range(2):
                    for k in range(8):
                        P.mm(pc[c][:, 0:n], wv[:, k, 128 * c:128 * c + 128], uT[:, k, t0:t0 + n], start=(k == 0), stop=(k == 7))
                    P.act(sqb[c][:, 0:n], pc[c][:, 0:n], AF.Square)
                    P.mm(pss[:, 0:n], ones_b[:], sqb[c][:, 0:n], start=(c == 0), stop=(c == 1))
                P.ts('dve', rst[:, 0:n], pss[:, 0:n], 1.0 / 256.0, LN_EPS, ALU.mult, ALU.add)
                P.act(rst[:, 0:n], rst[:, 0:n], AF.Sqrt)
                P.add('dve', lambda e, n=n: e.reciprocal(rst[:, 0:n], rst[:, 0:n]), [rst[:, 0:n]], [rst[:, 0:n]])
                for c in range(2):
                    P.stt('dve', cqn[:, c, t0:t0 + n], pc[c][:, 0:n], pv[:, 2 + c:3 + c], rst[:, 0:n], ALU.mult, ALU.mult)
                pk = PS_T.get()
                pss = PS_B.get()
                for k in range(8):
                    P.mm(pk[:, 0:n], wv[:, k, 256:384], uT[:, k, t0:t0 + n], start=(k == 0), stop=(k == 7))
                P.act(sqb[0][:, 0:n], pk[:, 0:n], AF.Square)
                P.mm(pss[:, 0:n], ones_b[:], sqb[0][:, 0:n])
                P.ts('dve', rst[:, 0:n], pss[:, 0:n], 1.0 / 128.0, LN_EPS, ALU.mult, ALU.add)
                P.act(rst[:, 0:n], rst[:, 0:n], AF.Sqrt)
                P.add('dve', lambda e, n=n: e.reciprocal(rst[:, 0:n], rst[:, 0:n]), [rst[:, 0:n]], [rst[:, 0:n]])
                P.stt('dve', ckvn[:, t0:t0 + n], pk[:, 0:n], pv[:, 4:5], rst[:, 0:n], ALU.mult, ALU.mult)
                pa = PS_T.get()
                pb_ = PS_T.get()
                for k in range(8):
                    P.mm(pa[64:96, 0:n], wv[:, k, 384:416], uT[:, k, t0:t0 + n], start=(k == 0), stop=(k == 7))
                for k in range(8):
                    P.mm(pb_[64:96, 0:n], wv[:, k, 416:448], uT[:, k, t0:t0 + n], start=(k == 0), stop=(k == 7))
                rope_apply(kro, pa, pb_, t0, n)
            for h in range(4):
                M = 65 if h % 2 == 0 else 128
                vo = 0 if h % 2 == 0 else 64
                P.memset('pool', vm[:], 0.0)
                oc = 64 if h % 2 == 0 else 0
                P.memset('pool', vm[:, :, oc:oc + 1], 1.0)
                for (t0, n) in TBS:
                    pa = PS_T.get()
                    pb_ = PS_T.get()
                    for k in range(2):
                        P.mm(pa[0:96, 0:n], wuq[:, k, 128 * h:128 * h + 96], cqn[:, k, t0:t0 + n], start=(k == 0), stop=(k == 1))
                    for k in range(2):
                        P.mm(pb_[64:96, 0:n], wuq[:, k, 128 * h + 96:128 * h + 128], cqn[:, k, t0:t0 + n], start=(k == 0), stop=(k == 1))
                    P.copy('act', qm[0:64, t0:t0 + n], pa[0:64, 0:n])
                    rope_apply(qm, pa, pb_, t0, n)
                    pk = PS_T.get()
                    P.mm(pk[0:64, 0:n], wukv[:, 128 * h:128 * h + 64], ckvn[:, t0:t0 + n])
                    P.copy('act', km[0:64, t0:t0 + n], pk[0:64, 0:n])
                    P.copy('pool', km[64:96, t0:t0 + n], kro[64:96, t0:t0 + n])
                for g0 in range(0, 18, 8):
                    gn = min(8, 18 - g0)
                    pvv = PS_T.get()
                    for tc in range(g0, g0 + gn):
                        P.mm(pvv[:, 64 * (tc - g0):64 * (tc - g0) + 64], ckvn[:, 128 * tc:128 * tc + 128],
                             wukv[:, 128 * h + 64:128 * h + 128])
                    P.copy(ev_eng(), vm[:, g0:g0 + gn, vo:vo + 64], pvv[:, 0:64 * gn].rearrange("p (a d) -> p a d", d=64))
                for (t0, n) in TBS:
                    seq = list(range(18)) if t0 < SEQ else [16, 17]
                    po = PS_A.get()
                    def sc_(i):
                        ps = PS_T.get()
                        P.mm(ps[:, 0:n], km[0:96, 128 * i:128 * i + 128], qm[0:96, t0:t0 + n])
                        return ps
                    pq = [sc_(seq[0]), sc_(seq[1])]
                    for idx, i in enumerate(seq):
                        ps = pq.pop(0)
                        if idx + 2 < len(seq):
                            pq.append(sc_(seq[idx + 2]))
                        pt = nextpt()
                        P.act(pt[:, 0:n], ps[:, 0:n], AF.Exp, scale=MLA_SCALE)
                        P.mm(po[0:M, 0:n], vm[:, i, 0:M], pt[:, 0:n], start=(idx == 0), stop=(idx == len(seq) - 1))
                    finalize_attn(po, h, yT[3], t0, n)


        u2tok = None
        if stage >= 5:
            AR.off = mark_y
            merged = AR.view([128, 8, T], BF16)
            wbr = AR.view([128, 2, 4, 128], BF16)
            sg = [AR.view([128, 512], F32) for _ in range(2)]
            macc = AR.view([128, 512], F32)
            mt = AR.view([128, 512], F32)
            for jd in range(8):
                wg_ = nextw()
                for nb in range(4):
                    P.dma('pool', wg_[:, :, 128 * nb:128 * nb + 128],
                          w_in[:, C_GT + 1024 * nb + 128 * jd:C_GT + 1024 * nb + 128 * jd + 128].rearrange("(k p) n -> p k n", p=128))
                    P.dma('pool', wbr[:, :, nb, :],
                          w_br[256 * nb:256 * nb + 256, 128 * jd:128 * jd + 128].rearrange("(k p) n -> p k n", p=128))
                for (t0, n) in TBS:
                    for nb in range(4):
                        pg = PS_T.get()
                        for k in range(8):
                            P.mm(pg[:, 0:n], wg_[:, k, 128 * nb:128 * nb + 128], uT[:, k, t0:t0 + n], start=(k == 0), stop=(k == 7))
                        pp = PS_T.get()
                        for k in range(2):
                            P.mm(pp[:, 0:n], wbr[:, k, nb, :], yT[nb][:, k, t0:t0 + n], start=(k == 0), stop=(k == 1))
                        s_ = sg[nb % 2]
                        P.act(s_[:, 0:n], pg[:, 0:n], AF.Sigmoid)
                        if nb == 0:
                            P.tt('dve', macc[:, 0:n], s_[:, 0:n], pp[:, 0:n], ALU.mult)
                        elif nb < 3:
                            P.tt('dve', mt[:, 0:n], s_[:, 0:n], pp[:, 0:n], ALU.mult)
                            P.tt('dve', macc[:, 0:n], macc[:, 0:n], mt[:, 0:n], ALU.add)
                        else:
                            P.tt('dve', mt[:, 0:n], s_[:, 0:n], pp[:, 0:n], ALU.mult)
                            P.tt('dve', merged[:, jd, t0:t0 + n], macc[:, 0:n], mt[:, 0:n], ALU.add)
            AR.off = mark_u
            u2tok = AR.view([128, 18, D], BF16)
            assert AR.off <= mark_y
            AR.off = mark_y + 8 * T * 2 + 64
            wo0 = nextw()
            wo1 = nextw()
            P.dma('pool', wo0[:, :, 0:512], w_out[:, 0:512].rearrange("(k p) n -> p k n", p=128))
            P.dma('pool', wo1[:, :, 0:512], w_out[:, 512:1024].rearrange("(k p) n -> p k n", p=128))
            wrt = AR.view([128, 8, NE], F32)
            P.dma('sp', wrt[:], w_rt.rearrange("(k p) n -> p k n", p=128))
            u2b = AR.view([128, 8, 512], BF16)
            lgT = AR.view([NE, T], F32)
            mark_m = AR.off
            AR.off = 0
            hbk = [AR.view([128, 8, 512], F32)]
            zts = [AR.view([128, 8, 512], F32)]
            AR.off = mark_m
            zts.append(AR.view([128, 8, 512], F32))
            for bi, (t0, n) in enumerate(TBS):
                x = 0 if t0 < SEQ else 1
                hb_ = hbk[0]
                zt = zts[bi % 2]
                P.dma('sp', hb_[:, :, 0:n], hsrc[:, t0:t0 + n].rearrange("(k p) n -> p k n", p=128))
                for jo in range(8):
                    po = PS_T.get()
                    for k in range(8):
                        P.mm(po[:, 0:n], (wo0 if jo < 4 else wo1)[:, k, 128 * (jo % 4):128 * (jo % 4) + 128], merged[:, k, t0:t0 + n], start=(k == 0), stop=(k == 7))
                    P.act(hb_[:, jo, 0:n], hb_[:, jo, 0:n], AF.Identity, scale=float(DN_ALPHA))
                    P.stt('dve', zt[:, jo, 0:n], po[:, 0:n], modT[:, G1 + jo, x:x + 1], hb_[:, jo, 0:n], ALU.mult, ALU.add)

                def after_ln(j, ap, t0=t0, n=n, x=x):
                    P.dma('sp', h1T[128 * j:128 * j + 128, t0:t0 + n], ap)
                    P.act(zt[:, j, 0:n], ap, AF.Identity, bias=modT[:, SH2 + j, x:x + 1], scale=ops2[:, j, x:x + 1])
                    P.copy('act', u2b[:, j, 0:n], zt[:, j, 0:n])
                ln_block(P, PS_A, zt, n, ln1s[:, 0, :], ln1s[:, 1, :], after_ln, ones_f, lntmp, ['dve', 'pool'])
                pl = PS_B.get()
                for k in range(8):
                    P.mm(pl[0:NE, 0:n], wrt[:, k, :], zt[:, k, 0:n], start=(k == 0), stop=(k == 7))
                P.copy('dve', lgT[:, t0:t0 + n], pl[0:NE, 0:n])
                for q in range(n // 128):
                    tc = t0 // 128 + q
                    for k in range(8):
                        P.tr(pstb[:, 128 * k:128 * k + 128], u2b[:, k, 128 * q:128 * q + 128], idb[:])
                    P.copy('act', u2tok[:, tc, :], pstb[:, :])

        if stage >= 6:
            AR.off = mark_y
            mask = AR.view([128, 18, NE], F32)
            maskb = AR.view([128, 18, NE], BF16)
            cs = AR.view([128, 18, NE], F32)
            csm = AR.view([128, 18, NE], F32)
            ffn_mark = AR.off
            affT = AR.view([NE, T], F32)
            maskT = AR.view([NE, T], F32)
            maskTb = AR.view([NE, T], BF16)
            gmT = AR.view([NE, T], F32)
            m8 = AR.view([NE, 8], F32)
            m8c = AR.view([NE, 8], F32)
            rsm = AR.view([NE, 512], F32)
            assert AR.off <= mark_m - NE * 0 - T * 4
            P.act(affT[:], lgT[:], AF.Exp)
            for (t0, n) in TBS:
                ps = PS_T.get()
                P.mm(ps[0:NE, 0:n], ones_f[0:NE, 0:NE], affT[:, t0:t0 + n])
                P.add('dve', lambda e, ps=ps, n=n: e.reciprocal(rsm[:, 0:n], ps[0:NE, 0:n]), [ps[0:NE, 0:n]], [rsm[:, 0:n]])
                P.tt('dve', affT[:, t0:t0 + n], affT[:, t0:t0 + n], rsm[:, 0:n], ALU.mult)
            if sub >= 1:
                mark_r = AR.off
                AR.off = 0
                wk = AR.view([NE, SEQ], F32)
                wkc = AR.view([NE, LC], F32)
                csT = AR.view([NE, T], F32)
                P.copy('dve', wk[:], affT[:, 0:SEQ])
                P.copy('dve', wkc[:], affT[:, SEQ:T])
                for r in range(CAP // 8):
                    P.add('dve', lambda e: e.max(m8[:], wk[:]), [wk[:]], [m8[:]])
                    if r < CAP // 8 - 1:
                        P.add('dve', lambda e: e.match_replace(wk[:], m8[:], wk[:], -1.0), [wk[:], m8[:]], [wk[:]])
                for r in range(CAPC // 8):
                    P.add('dve', lambda e: e.max(m8c[:], wkc[:]), [wkc[:]], [m8c[:]])
                    if r < CAPC // 8 - 1:
                        P.add('dve', lambda e: e.match_replace(wkc[:], m8c[:], wkc[:], -1.0), [wkc[:], m8c[:]], [wkc[:]])
            if sub >= 2:
                P.ts('dve', maskT[:, 0:SEQ], affT[:, 0:SEQ], m8[:, 7:8], None, ALU.is_ge)
                P.ts('dve', maskT[:, SEQ:T], affT[:, SEQ:T], m8c[:, 7:8], None, ALU.is_ge)
                P.tt('dve', gmT[:], maskT[:], affT[:], ALU.mult)
                P.dma('sp', gmT_o[:, :], gmT[:])
            if sub >= 3:
                import os
                S3 = float(os.environ.get('SUB3', '9'))
                if S3 >= 0.1:
                    P.copy('dve', maskTb[:], maskT[:])
                pmk = PS_T.get()
                if S3 >= 0.2:
                    for tc in range(18):
                        P.mm(pmk[:, NE * tc:NE * tc + NE], maskTb[:, 128 * tc:128 * tc + 128], idb[0:NE, 0:NE])
                if S3 >= 0.3:
                    P.copy('dve', mask[:], pmk[:, 0:18 * NE].rearrange("p (a b) -> p a b", b=NE))
                if S3 >= 0.4:
                    P.copy('pool', maskb[:], mask[:])
                for tc in range(18 if S3 >= 2 else 0):
                    base = 0 if tc < 16 else 16
                    pc_ = PS_B.get()
                    for c2 in range(base, tc):
                        P.mm(pc_[:, 0:NE], ones_b[:], maskb[:, c2, :], start=(c2 == base), stop=False)
                    P.mm(pc_[:, 0:NE], ust[:], maskb[:, tc, :], start=(tc == base), stop=True)
                    P.ts('dve', cs[:, tc, :], pc_[:, 0:NE], 0.0 if tc < 16 else float(CAP), None, ALU.add)
                    if S3 < 3:
                        continue
                    pt_ = PS_T.get()
                    for c2 in range(base, tc):
                        P.mm(pt_[0:NE, 0:128], maskb[:, c2, :], ones_b[:], start=(c2 == base), stop=False)
                    P.mm(pt_[0:NE, 0:128], maskb[:, tc, :], ust[:], start=(tc == base), stop=True)
                    P.ts('dve', csT[:, 128 * tc:128 * tc + 128], pt_[0:NE, 0:128], 0.0 if tc < 16 else float(CAP), None, ALU.add)
                if S3 >= 3:
                    P.dma('sp', csT_o[:, :], csT[:])
            if sub >= 4:
                AR.off = ffn_mark
                wdb = AR.view([128, 16, D], BF16)
                actT = AR.view([128, 16, NSLOT], BF16)
                silb = [AR.view([128, NSLOT], F32) for _ in range(2)]
                yob = [AR.view([128, D], BF16) for _ in range(2)]
                wub = [wring[2], AR.view([128, 8, 512], BF16)]
                wgb = [wring[0], wring[1]]
                P.tt('dve', csm[:], cs[:], mask[:], ALU.mult)
                P.tt('dve', csm[:], csm[:], mask[:], ALU.add)
                P.ts('dve', csm[:], csm[:], -1.0, None, ALU.add)
                AR.off = 0
                selL = [AR.view([128, 16, CAP], BF16) for _ in range(2)]
                selC = [AR.view([128, 2, CAPC], BF16) for _ in range(2)]
                xs = [AR.view([128, 8, NSLOT], BF16) for _ in range(2)]
                assert AR.off <= mark_u
                for e_ in range(NE):
                    sl, sc_, x_ = selL[e_ % 2], selC[e_ % 2], xs[e_ % 2]
                    P.tt('dve', sl[:], iof[:, 0:CAP].unsqueeze(1).to_broadcast([128, 16, CAP]),
                         csm[:, 0:16, e_:e_ + 1].to_broadcast([128, 16, CAP]), ALU.is_equal)
                    P.tt('dve', sc_[:], iof[:, CAP:NSLOT].unsqueeze(1).to_broadcast([128, 2, CAPC]),
                         csm[:, 16:18, e_:e_ + 1].to_broadcast([128, 2, CAPC]), ALU.is_equal)
                    for j in range(8):
                        px = PS_T.get()
                        for tc in range(16):
                            P.mm(px[:, 0:CAP], u2tok[:, tc, 128 * j:128 * j + 128], sl[:, tc, :], start=(tc == 0), stop=(tc == 15))
                        for tc in range(16, 18):
                            P.mm(px[:, CAP:NSLOT], u2tok[:, tc, 128 * j:128 * j + 128], sc_[:, tc - 16, :], start=(tc == 16), stop=(tc == 17))
                        P.copy(ev_eng(), x_[:, j, :], px[:, 0:NSLOT])
                    for pc in range(4):
                        wgp = wring[(2 * pc) % 3] if False else wgb[pc % 2]
                        wup = wub[pc % 2]
                        P.dma('pool', wgp[:], wg_all[l, e_, :, 512 * pc:512 * pc + 512].rearrange("(k p) n -> p k n", p=128))
                        P.dma('pool', wup[:], wu_all[l, e_, :, 512 * pc:512 * pc + 512].rearrange("(k p) n -> p k n", p=128))
                        for f in range(4):
                            pa = PS_T.get()
                            pu = PS_T.get()
                            for k in range(8):
                                P.mm(pa[:, 0:NSLOT], wgp[:, k, 128 * f:128 * f + 128], x_[:, k, :], start=(k == 0), stop=(k == 7))
                            for k in range(8):
                                P.mm(pu[:, 0:NSLOT], wup[:, k, 128 * f:128 * f + 128], x_[:, k, :], start=(k == 0), stop=(k == 7))
                            s_ = silb[f % 2]
                            P.act(s_[:], pa[:, 0:NSLOT], AF.Silu)
                            P.tt('dve', actT[:, 4 * pc + f, :], s_[:], pu[:, 0:NSLOT], ALU.mult)
                    for pc in range(4):
                        P.dma('pool', wdb[:, 4 * pc:4 * pc + 4, :], wd_all[l, e_, 512 * pc:512 * pc + 512, :].rearrange("(k p) n -> p k n", p=128))
                    for qi, (s0, sn) in enumerate(((0, 128), (128, 128), (256, 32))):
                        y_ = yob[qi % 2]
                        for hf in range(2):
                            po = PS_A.get()
                            for f in range(16):
                                P.mm(po[0:sn, 0:512], actT[:, f, s0:s0 + sn], wdb[:, f, 512 * hf:512 * hf + 512], start=(f == 0), stop=(f == 15))
                            P.copy('act' if hf == 0 else 'dve', y_[0:sn, 512 * hf:512 * hf + 512], po[0:sn, 0:512])
                        P.dma('sp', ye[e_, s0:s0 + sn, :], y_[0:sn, :])
    AR.off = 0
    emit_prologue(P, dict(AR=AR, PS_T=PS_T, PS_A=PS_A, PS_B=PS_B, ones_f=ones_f, lntmp=lntmp, pix=pix, hT_in=h1T, ye=ye,
                          csT_p=csT_o, gmT_p=gmT_o, modp=mod_scr[nl - 1], ln2p=ln2_all[nl - 1], eye16rep=eye16rep, hdst=out_d), final=True)
    P.finish(outs)
    P.emit()
    return nc, P


def host_inputs_F(inp, b, consts):
    perm_a = np.concatenate([np.arange(0, 32, 2), np.arange(1, 32, 2)])
    perm_b = np.concatenate([np.arange(1, 32, 2), np.arange(0, 32, 2)])
    m = {}
    w_in = inp['w_in']
    kr = w_in[:, :, 1664:1696]
    m['w_in'] = np.ascontiguousarray(np.concatenate([w_in[:, :, :1664], kr[:, :, perm_a], kr[:, :, perm_b], w_in[:, :, 1696:]], axis=2))
    m['cs2'] = np.ascontiguousarray(np.stack([vec_pj(inp['c'][b]), vec_pj(inp['c_ctx'])], axis=-1))
    m['wmod'] = inp['w_mod']
    m['bmod'] = np.ascontiguousarray(np.stack([vec_pj(inp['b_mod'][l]) for l in range(DEPTH)]))
    m['nab'] = np.ascontiguousarray(inp['na_rpb'][:, :, consts['_na_ri'], consts['_na_ci']])
    m['pool_w'] = inp['pool_w']
    pvv = np.zeros((DEPTH, 128, 8), np.float32)
    for l in range(DEPTH):
        pvv[l, :, 0:2] = vec_pj(inp['pool_scale'][l]); pvv[l, :, 2:4] = vec_pj(inp['mla_q_norm'][l]); pvv[l, :, 4:5] = vec_pj(inp['mla_kv_norm'][l])
    m['pvec'] = pvv
    wuq = inp['mla_w_uq'].reshape(DEPTH, 256, 4, 96)
    m['w_uq'] = np.ascontiguousarray(np.concatenate([wuq[..., :64], wuq[..., 64:][..., perm_a], wuq[..., 64:][..., perm_b]], axis=-1).reshape(DEPTH, 256, 512))
    m['w_ukv'] = inp['mla_w_ukv']
    m['w_br'] = np.ascontiguousarray(inp['w_branch'].reshape(DEPTH, 1024, 1024))
    m['w_out'] = inp['w_out']
    m['ln1'] = np.ascontiguousarray(np.stack([np.stack([vec_pj(inp['ln1_g'][l]), vec_pj(inp['ln1_b'][l])], axis=1) for l in range(DEPTH)]))
    m['ln2'] = np.ascontiguousarray(np.stack([np.stack([vec_pj(inp['ln2_g'][l]), vec_pj(inp['ln2_b'][l])], axis=1) for l in range(DEPTH)]))
    m['w_rt'] = inp['w_router']
    m['wg'] = inp['w_gate']; m['wu'] = inp['w_up']; m['wd'] = inp['w_down']
    m['eye16rep'] = np.kron(np.eye(NE), np.ones((1, 128))).astype(ml_dtypes.bfloat16)
    for k in ('nam', 'ropeC', 'ropeS', 'dftc', 'ident_f', 'ident_b', 'ustri', 'iota_f', 'pidx', 'lval', 'jrow', 'pooledge', 'poolinvw'):
        m[k] = consts[k]
    m['hT_in'] = np.ascontiguousarray(np.concatenate([inp['x'][b].T, inp['ctx'][b].T], axis=1))
    return m


_PROGS = {}


def kernel(**inp):
    inp = {k: np.asarray(v) for k, v in inp.items()}
    consts = host_constants()
    NCORE = 8
    if 'F' not in _PROGS:
        _PROGS['F'] = build_F(DEPTH)[0]
    maps = []
    shared = None
    for b in range(NCORE):
        if shared is None:
            shared = host_inputs_F(inp, b, consts)
            m = shared
        else:
            m = dict(shared)
            m['cs2'] = np.ascontiguousarray(np.stack([vec_pj(inp['c'][b]), vec_pj(inp['c_ctx'])], axis=-1))
            m['hT_in'] = np.ascontiguousarray(np.concatenate([inp['x'][b].T, inp['ctx'][b].T], axis=1))
        maps.append(m)
    res = run_bass_kernel_spmd(_PROGS['F'], maps, core_ids=list(range(NCORE))).results
    out = np.stack([np.ascontiguousarray(res[b]['out'].T) for b in range(NCORE)], axis=0)
    return out.astype(np.float32)
```
